# Optimizing a Trainium2 kernel written in Bass

```python
import math
import jax, jax.numpy as jnp
from jax import lax
import numpy as np

D_MODEL = 1024
BATCH = 8
SEQ = 2048
DEPTH = 2
DEC_BATCH = 128
DEC_SEQ = 4
PAST_LEN = 16384
PAGE_SIZE = 128

MIX_WIDTH = D_MODEL
RET_WIDTH = MIX_WIDTH // 2
CONV_WIDTH = MIX_WIDTH - RET_WIDTH
RET_HEADS = 4
RET_DK = RET_WIDTH // RET_HEADS
RET_DV = RET_WIDTH // RET_HEADS
RET_CHUNK = 128
ROPE_BASE = 10000.0
CONV_K = 31
IN_WIDTH = 4 * RET_WIDTH + 2 * CONV_WIDTH
PEER_HEADS = 8
N_KEYS = 128
N_EXPERTS = N_KEYS * N_KEYS
PEER_DQ = 256
PEER_TOPK = 16
PEER_BLOCK = 128
ALPHA = (2.0 * DEPTH) ** 0.25
BETA = (8.0 * DEPTH) ** -0.25
LN_EPS = 1e-5

kernel_name = "hymba_retnet_conformer_peer_step"


def layer_norm(x, g, b):
    xf = x.astype(jnp.float32)
    mu = jnp.mean(xf, axis=-1, keepdims=True)
    var = jnp.mean(jnp.square(xf - mu), axis=-1, keepdims=True)
    y = (xf - mu) * lax.rsqrt(var + LN_EPS)
    return (y * g.astype(jnp.float32) + b.astype(jnp.float32)).astype(x.dtype)


def head_norm(x):
    xf = x.astype(jnp.float32)
    mu = jnp.mean(xf, axis=-1, keepdims=True)
    var = jnp.mean(jnp.square(xf - mu), axis=-1, keepdims=True)
    return ((xf - mu) * lax.rsqrt(var + LN_EPS)).astype(x.dtype)


def rope_tables(pos, dtype):
    inv_freq = ROPE_BASE ** (-jnp.arange(0, RET_DK, 2, dtype=jnp.float32) / RET_DK)
    ang = pos.astype(jnp.float32)[:, None] * inv_freq[None, :]
    return jnp.cos(ang)[:, None, :].astype(dtype), jnp.sin(ang)[:, None, :].astype(dtype)


def apply_rope(t, cos, sin):
    t1, t2 = t[..., : RET_DK // 2], t[..., RET_DK // 2:]
    return jnp.concatenate([t1 * cos - t2 * sin, t1 * sin + t2 * cos], axis=-1)


def retention(q, k, v, r0, log_gamma):
    b, t, h, _ = q.shape
    c = min(t, RET_CHUNK)
    n = t // c
    dt = q.dtype
    qc = q.reshape(b, n, c, h, RET_DK)
    kc = k.reshape(b, n, c, h, RET_DK)
    vc = v.reshape(b, n, c, h, RET_DV)
    i = jnp.arange(c, dtype=jnp.float32)
    rel = i[:, None] - i[None, :]
    lg = log_gamma[:, None, None]
    dmask = jnp.where(rel >= 0, jnp.exp(jnp.maximum(rel, 0.0) * lg), 0.0).astype(dt)
    decay_in = jnp.exp((i[None, :] + 1.0) * log_gamma[:, None]).astype(dt)
    decay_out = jnp.exp((c - 1.0 - i[None, :]) * log_gamma[:, None]).astype(dt)
    chunk_decay = jnp.exp(c * log_gamma)[:, None, None].astype(dt)
    scores = jnp.einsum('bnihd,bnjhd->bnhij', qc, kc) * dmask
    o_intra = jnp.einsum('bnhij,bnjhe->bnihe', scores, vc)
    kv = jnp.einsum('bnjhd,bnjhe,hj->bnhde', kc, vc, decay_out)

    def step(r, kv_n):
        return chunk_decay * r + kv_n, r

    r_final, r_prev = lax.scan(step, r0, jnp.moveaxis(kv, 1, 0))
    r_prev = jnp.moveaxis(r_prev, 0, 1)
    o_cross = jnp.einsum('bnihd,bnhde,hi->bnihe', qc, r_prev, decay_in)
    return (o_intra + o_cross).reshape(b, t, h, RET_DV), r_final


def conformer_conv(ca, cg, buf, dw_kernel, dw_bias, ln_g, ln_b):
    u = ca * jax.nn.sigmoid(cg)
    xpad = jnp.concatenate([buf, u], axis=1)
    y = lax.conv_general_dilated(
        xpad, dw_kernel[:, None, :], window_strides=(1,), padding='VALID',
        dimension_numbers=('NWC', 'WIO', 'NWC'), feature_group_count=CONV_WIDTH) + dw_bias
    y = jax.nn.silu(layer_norm(y, ln_g, ln_b))
    return y, xpad[:, -(CONV_K - 1):]


def peer(x2d, w_query, sub_keys_1, sub_keys_2, peer_u, peer_v):
    t = x2d.shape[0]
    nb = -(-t // PEER_BLOCK)
    xp = jnp.pad(x2d, ((0, nb * PEER_BLOCK - t), (0, 0))).reshape(nb, PEER_BLOCK, D_MODEL)

    def block(xb):
        q = (xb @ w_query).reshape(PEER_BLOCK, PEER_HEADS, PEER_DQ)
        q1, q2 = q[..., : PEER_DQ // 2], q[..., PEER_DQ // 2:]
        s1 = jnp.einsum('phd,hkd->phk', q1, sub_keys_1)
        s2 = jnp.einsum('phd,hkd->phk', q2, sub_keys_2)
        v1, i1 = lax.top_k(s1, PEER_TOPK)
        v2, i2 = lax.top_k(s2, PEER_TOPK)
        cand = (v1[..., :, None] + v2[..., None, :]).reshape(PEER_BLOCK, PEER_HEADS, PEER_TOPK * PEER_TOPK)
        sv, ci = lax.top_k(cand, PEER_TOPK)
        e1 = jnp.take_along_axis(i1, ci // PEER_TOPK, axis=-1)
        e2 = jnp.take_along_axis(i2, ci % PEER_TOPK, axis=-1)
        idx = e1 * N_KEYS + e2
        gate = jax.nn.softmax(sv.astype(jnp.float32), axis=-1).astype(xb.dtype)
        u = jnp.take(peer_u, idx, axis=0)
        act = jax.nn.gelu(jnp.einsum('phkd,pd->phk', u, xb), approximate=False)
        vv = jnp.take(peer_v, idx, axis=0)
        return jnp.einsum('phk,phkd->pd', gate * act, vv)

    out = lax.map(block, xp)
    return out.reshape(nb * PEER_BLOCK, D_MODEL)[:t]


def decoder_layer(x, r0, buf, cos, sin, log_gamma, w_in, b_in, dw_kernel, dw_bias,
                  conv_ln_g, conv_ln_b, w_out, ln1_g, ln1_b, w_query, sub_keys_1,
                  sub_keys_2, peer_u, peer_v, ln2_g, ln2_b):
    b, t, _ = x.shape
    proj = jnp.einsum('btd,de->bte', x, w_in) + b_in
    q, k, v, g, ca, cg = jnp.split(proj, 6, axis=-1)
    q = apply_rope(q.reshape(b, t, RET_HEADS, RET_DK), cos, sin)
    k = apply_rope(k.reshape(b, t, RET_HEADS, RET_DK), cos, sin) * (RET_DK ** -0.5)
    v = v.reshape(b, t, RET_HEADS, RET_DV)
    o, r_new = retention(q, k, v, r0, log_gamma)
    ret_out = jax.nn.silu(g) * head_norm(o).reshape(b, t, RET_WIDTH)
    conv_out, buf_new = conformer_conv(ca, cg, buf, dw_kernel, dw_bias, conv_ln_g, conv_ln_b)
    mix = jnp.einsum('bte,ed->btd', jnp.concatenate([ret_out, conv_out], axis=-1), w_out)
    x = layer_norm(ALPHA * x + mix, ln1_g, ln1_b)
    ffn = peer(x.reshape(b * t, D_MODEL), w_query, sub_keys_1, sub_keys_2, peer_u, peer_v)
    x = layer_norm(ALPHA * x + ffn.reshape(b, t, D_MODEL), ln2_g, ln2_b)
    return x, r_new, buf_new


def setup_inputs(seed: int = 0) -> dict:
    key = jax.random.key(seed)
    ks = jax.random.split(key, 24)

    def nrm(k, shape, scale):
        return jax.random.normal(k, shape, jnp.float32) * scale

    s_in = D_MODEL ** -0.5
    w_in = jnp.concatenate([
        nrm(ks[0], (DEPTH, D_MODEL, RET_WIDTH), s_in),
        nrm(ks[1], (DEPTH, D_MODEL, RET_WIDTH), s_in),
        nrm(ks[2], (DEPTH, D_MODEL, RET_WIDTH), s_in * BETA),
        nrm(ks[3], (DEPTH, D_MODEL, RET_WIDTH), s_in),
        nrm(ks[4], (DEPTH, D_MODEL, CONV_WIDTH), s_in),
        nrm(ks[5], (DEPTH, D_MODEL, CONV_WIDTH), s_in),
    ], axis=-1)
    return {
        "x_prompt": nrm(ks[6], (BATCH, SEQ, D_MODEL), 1.0),
        "x_sample": nrm(ks[7], (DEC_BATCH, DEC_SEQ, D_MODEL), 1.0),
        "state_retention": nrm(ks[8], (DEPTH, DEC_BATCH, RET_HEADS, RET_DK, RET_DV), 0.3),
        "cache_conv": nrm(ks[9], (DEPTH, DEC_BATCH, CONV_K - 1, CONV_WIDTH), 0.5),
        "w_in": w_in,
        "b_in": nrm(ks[10], (DEPTH, IN_WIDTH), 0.02),
        "dw_kernel": nrm(ks[11], (DEPTH, CONV_K, CONV_WIDTH), CONV_K ** -0.5),
        "dw_bias": nrm(ks[12], (DEPTH, CONV_WIDTH), 0.02),
        "conv_ln_g": 1.0 + nrm(ks[13], (DEPTH, CONV_WIDTH), 0.02),
        "conv_ln_b": nrm(ks[14], (DEPTH, CONV_WIDTH), 0.02),
        "w_out": nrm(ks[15], (DEPTH, MIX_WIDTH, D_MODEL), MIX_WIDTH ** -0.5 * BETA),
        "ln1_g": 1.0 + nrm(ks[16], (DEPTH, D_MODEL), 0.02),
        "ln1_b": nrm(ks[17], (DEPTH, D_MODEL), 0.02),
        "w_query": nrm(ks[18], (DEPTH, D_MODEL, PEER_HEADS * PEER_DQ), s_in),
        "sub_keys_1": nrm(ks[19], (DEPTH, PEER_HEADS, N_KEYS, PEER_DQ // 2), (PEER_DQ // 2) ** -0.5),
        "sub_keys_2": nrm(ks[20], (DEPTH, PEER_HEADS, N_KEYS, PEER_DQ // 2), (PEER_DQ // 2) ** -0.5),
        "peer_u": nrm(ks[21], (DEPTH, N_EXPERTS, D_MODEL), s_in),
        "peer_v": nrm(ks[22], (DEPTH, N_EXPERTS, D_MODEL), BETA * PEER_HEADS ** -0.5),
        "ln2_g": 1.0 + nrm(ks[23], (DEPTH, D_MODEL), 0.02),
        "ln2_b": nrm(jax.random.fold_in(key, 99), (DEPTH, D_MODEL), 0.02),
    }


def reference(x_prompt, x_sample, state_retention, cache_conv, w_in, b_in, dw_kernel,
              dw_bias, conv_ln_g, conv_ln_b, w_out, ln1_g, ln1_b, w_query, sub_keys_1,
              sub_keys_2, peer_u, peer_v, ln2_g, ln2_b):
    dt = x_prompt.dtype
    log_gamma = jnp.log(1.0 - 2.0 ** (-5.0 - jnp.arange(RET_HEADS, dtype=jnp.float32)))
    cos_p, sin_p = rope_tables(jnp.arange(x_prompt.shape[1]), dt)
    cos_s, sin_s = rope_tables(PAST_LEN + jnp.arange(x_sample.shape[1]), dt)
    bp = x_prompt.shape[0]
    yp, ys = x_prompt, x_sample
    ret_p, conv_p, ret_s, conv_s = [], [], [], []
    for l in range(DEPTH):
        params = (w_in[l], b_in[l], dw_kernel[l], dw_bias[l], conv_ln_g[l], conv_ln_b[l],
                  w_out[l], ln1_g[l], ln1_b[l], w_query[l], sub_keys_1[l], sub_keys_2[l],
                  peer_u[l], peer_v[l], ln2_g[l], ln2_b[l])
        r0 = jnp.zeros((bp, RET_HEADS, RET_DK, RET_DV), dt)
        buf0 = jnp.zeros((bp, CONV_K - 1, CONV_WIDTH), dt)
        yp, rp, cp = decoder_layer(yp, r0, buf0, cos_p, sin_p, log_gamma, *params)
        ys, rs, cs = decoder_layer(ys, state_retention[l], cache_conv[l], cos_s, sin_s, log_gamma, *params)
        ret_p.append(rp)
        conv_p.append(cp)
        ret_s.append(rs)
        conv_s.append(cs)
    return (yp, ys, jnp.stack(ret_p), jnp.stack(conv_p), jnp.stack(ret_s), jnp.stack(conv_s))
```

```python
import math
from contextlib import ExitStack

import numpy as np
import concourse.bass as bass
import concourse.mybir as mybir
from concourse.bass_utils import run_bass_kernel_spmd
from concourse.alu_op_type import AluOpType as ALU

F32 = mybir.dt.float32
BF16 = mybir.dt.bfloat16
I32 = mybir.dt.int32
U32 = mybir.dt.uint32
AF = mybir.ActivationFunctionType
AX = mybir.AxisListType

N_CORES = 8
D = 1024
SEQ = 2048
DEPTH = 2
SB = 16
DEC_SEQ = 4
PAST_LEN = 16384
RW = 512
CW = 512
H = 4
DK = 128
CONV_K = 31
INW = 3072
PH = 8
NKEYS = 128
NEXP = NKEYS * NKEYS
TOPK = 16
ALPHA = (2.0 * DEPTH) ** 0.25
LN_EPS = 1e-5
NPT = SEQ // 128
STILE = NPT
NGROUPS = 2
NUG = 6
import os
DBG_BAR = os.environ.get("DBG_BAR", "")
SEM_CAP = 30000


class Sem:
    def __init__(self, tr):
        self.tr = tr
        self.h = tr.new_hw_sem()
        self.val = 0

    def next(self, inc):
        if self.val + inc > SEM_CAP:
            self.h = self.tr.new_hw_sem()
            self.val = 0
        self.val += inc
        return (self.h, self.val)


class Tracker:
    NAMES = ["pe", "act", "dve", "pool", "sp"]

    def __init__(self, nc, stack):
        self.nc = nc
        self.stack = stack
        self.nsem = 0
        self.esem = {e: Sem(self) for e in self.NAMES}
        self.known = {e: {} for e in self.NAMES}
        self.prog = {e: [] for e in self.NAMES}
        self.last_tok = {e: None for e in self.NAMES}
        self.last_w = {}
        self.reads = {}
        self.all_dma = {}
        self.pools = {}
        self.nins = 0

    def new_hw_sem(self):
        self.nsem += 1
        return self.stack.enter_context(self.nc.semaphore("s%d" % self.nsem))

    def _wait(self, e, deps):
        kn = self.known[e]
        best = {}
        for (h, v) in deps:
            k = id(h)
            if k not in best or v > best[k][1]:
                best[k] = (h, v)
        for k, (h, v) in best.items():
            if kn.get(k, 0) >= v:
                continue
            self.prog[e].append(("wait", h, v))
            kn[k] = v

    def _deps(self, reads, writes):
        deps = []
        for r in reads:
            if r in self.last_w:
                deps.append(self.last_w[r])
        for w in writes:
            if w in self.last_w:
                deps.append(self.last_w[w])
            deps.extend(self.reads.get(w, []))
        return deps

    def _commit(self, tok, reads, writes):
        for w in writes:
            self.last_w[w] = tok
            self.reads[w] = []
        for r in reads:
            if r not in writes:
                self.reads.setdefault(r, []).append(tok)

    def op(self, e, fn, reads=(), writes=()):
        self._wait(e, self._deps(reads, writes))
        tok = self.esem[e].next(1)
        self.prog[e].append(("ins", fn, tok[0], 1))
        self._commit(tok, reads, writes)
        self.last_tok[e] = tok
        self.nins += 1
        return tok

    def dma(self, e, fn, pool, reads=(), writes=(), npool=4):
        if pool not in self.pools:
            self.pools[pool] = [[Sem(self) for _ in range(npool)], 0]
        pl = self.pools[pool]
        sem = pl[0][pl[1] % len(pl[0])]
        pl[1] += 1
        deps = self._deps(reads, writes)
        if sem.val > 0:
            deps.append((sem.h, sem.val))
        self._wait(e, deps)
        tok = sem.next(16)
        self.prog[e].append(("ins", fn, tok[0], 16))
        self._commit(tok, reads, writes)
        self.all_dma[id(tok[0])] = tok
        self.nins += 1
        return tok

    def barrier(self):
        toks = [t for t in self.last_tok.values() if t is not None]
        toks += list(self.all_dma.values())
        for e in self.NAMES:
            self._wait(e, toks)

    def finish(self):
        self.barrier()

    def replay(self):
        nc = self.nc
        prog = self.prog

        def run(eng, lst):
            for a in lst:
                if a[0] == "wait":
                    eng.wait_ge(a[1], a[2])
                else:
                    ins = a[1](eng)
                    ins.then_inc(a[2], a[3])

        with nc.Block() as block:
            @block.tensor
            def _(eng):
                run(eng, prog["pe"])

            @block.scalar
            def _(eng):
                run(eng, prog["act"])

            @block.vector
            def _(eng):
                run(eng, prog["dve"])

            @block.gpsimd
            def _(eng):
                run(eng, prog["pool"])

            @block.sync
            def _(eng):
                run(eng, prog["sp"])


class Carver:
    def __init__(self, ov, nwords):
        self.ov = ov
        self.n = nwords
        self.off = 0

    def f32(self, n):
        assert self.off + n <= self.n, ("overlay overflow", self.off + n, self.n)
        v = self.ov[:, self.off:self.off + n]
        self.off += n
        return v

    def bf16(self, n):
        w = (n + 1) // 2
        assert self.off + w <= self.n, ("overlay overflow", self.off + w, self.n)
        v = self.ov[:, self.off:self.off + w].bitcast(BF16)
        self.off += w
        return v

    def u32(self, n):
        assert self.off + n <= self.n
        v = self.ov[:, self.off:self.off + n].bitcast(U32)
        self.off += n
        return v


def _const_tables():
    log_gamma = np.log(1.0 - 2.0 ** (-5.0 - np.arange(H, dtype=np.float32))).astype(np.float32)
    inv_freq = (np.float32(10000.0) ** (-np.arange(0, DK, 2, dtype=np.float32) / np.float32(DK))).astype(np.float32)

    def rope(pos):
        ang = pos.astype(np.float32)[:, None] * inv_freq[None, :]
        return np.cos(ang).astype(np.float32), np.sin(ang).astype(np.float32)

    ks = np.float32(DK ** -0.5)
    tab = np.zeros((NPT + 1, 128, 2, 256), np.float32)
    cp, sp_ = rope(np.arange(SEQ))
    for t in range(NPT):
        c = cp[t * 128:(t + 1) * 128]
        s = sp_[t * 128:(t + 1) * 128]
        tab[t, :, 0] = np.concatenate([c, c, -s, s], axis=1)
        tab[t, :, 1] = np.concatenate([c, c, -s, s], axis=1) * ks
    cs, ss = rope(PAST_LEN + np.arange(DEC_SEQ))
    rows = np.arange(128) % DEC_SEQ
    c = cs[rows]
    s = ss[rows]
    tab[NPT, :, 0] = np.concatenate([c, c, -s, s], axis=1)
    tab[NPT, :, 1] = np.concatenate([c, c, -s, s], axis=1) * ks

    lg = log_gamma.astype(np.float64)
    dm = np.zeros((2, 128, H, 128), np.float64)
    i = np.arange(128)
    rel = i[None, :] - i[:, None]
    for h in range(H):
        dm[0, :, h, :] = np.where(rel >= 0, np.exp(np.maximum(rel, 0) * lg[h]), 0.0)
        same = (i[None, :] // DEC_SEQ) == (i[:, None] // DEC_SEQ)
        rel4 = (i[None, :] % DEC_SEQ) - (i[:, None] % DEC_SEQ)
        dm[1, :, h, :] = np.where(same & (rel4 >= 0), np.exp(np.maximum(rel4, 0) * lg[h]), 0.0)
    din = np.zeros((128, 2, H), np.float64)
    dout = np.zeros((128, 2, H), np.float64)
    for h in range(H):
        din[:, 0, h] = np.exp((i + 1.0) * lg[h])
        dout[:, 0, h] = np.exp((128 - 1.0 - i) * lg[h])
        din[:, 1, h] = np.exp((i % DEC_SEQ + 1.0) * lg[h])
        dout[:, 1, h] = np.exp((DEC_SEQ - 1.0 - i % DEC_SEQ) * lg[h])
    cd = np.stack([np.exp(128 * lg), np.exp(DEC_SEQ * lg)])
    bmask = np.zeros((128, SB, 128), np.float32)
    for b in range(SB):
        bmask[:, b, b * DEC_SEQ:(b + 1) * DEC_SEQ] = 1.0
    rbm = np.zeros((128, SB), np.float32)
    for b in range(SB):
        rbm[b * DEC_SEQ:(b + 1) * DEC_SEQ, b] = 1.0
    iota16 = np.tile(np.arange(16, dtype=np.float32)[None, :], (128, 1))
    return dict(
        c_rope=tab,
        c_dm=np.ascontiguousarray(dm.transpose(1, 0, 2, 3)).astype(np.float32),
        c_din=din.astype(np.float32), c_dout=dout.astype(np.float32),
        c_bmask=bmask, c_rbm=rbm, c_iota=iota16,
        c_ident=np.eye(128, dtype=np.float32),
    ), cd.astype(np.float64)


def build_program():
    consts, cd = _const_tables()
    nc = bass.Bass("TRN2", target_bir_lowering=False)

    def din_(name, shape, dt=F32):
        return nc.dram_tensor(name, list(shape), dt, kind="ExternalInput").ap()

    def dout_(name, shape, dt=F32):
        return nc.dram_tensor(name, list(shape), dt, kind="ExternalOutput").ap()

    xp = din_("xp", [SEQ, D])
    xs = din_("xs", [SB * DEC_SEQ, D])
    st_in = din_("st", [DEPTH, SB, H, DK, DK])
    cc_in = din_("cc", [DEPTH, SB, CONV_K - 1, CW])
    w_in = din_("w_in", [DEPTH, D, INW])
    b_in = din_("b_in", [DEPTH, INW])
    dw_k = din_("dw_k", [DEPTH, CONV_K, CW])
    dw_b = din_("dw_b", [DEPTH, CW])
    cln_g = din_("cln_g", [DEPTH, CW])
    cln_b = din_("cln_b", [DEPTH, CW])
    w_out = din_("w_out", [DEPTH, D, D])
    ln1_g = din_("ln1_g", [DEPTH, D])
    ln1_b = din_("ln1_b", [DEPTH, D])
    w_q = din_("w_q", [DEPTH, D, 2 * D])
    sk1 = din_("sk1", [DEPTH, PH, NKEYS, 128])
    sk2 = din_("sk2", [DEPTH, PH, NKEYS, 128])
    pu = din_("pu", [DEPTH, NEXP, D])
    pv = din_("pv", [DEPTH, NEXP, D])
    ln2_g = din_("ln2_g", [DEPTH, D])
    pu_flat = pu.rearrange("l n d -> (l n) d")
    pv_flat = pv.rearrange("l n d -> (l n) d")
    ln2_b = din_("ln2_b", [DEPTH, D])
    c_rope = din_("c_rope", [NPT + 1, 128, 2, 256])
    c_dm = din_("c_dm", [128, 2, H, 128])
    c_din = din_("c_din", [128, 2, H])
    c_dout = din_("c_dout", [128, 2, H])
    c_bmask = din_("c_bmask", [128, SB, 128])
    c_rbm = din_("c_rbm", [128, SB])
    c_iota = din_("c_iota", [128, 16])
    c_ident = din_("c_ident", [128, 128])

    puv = nc.dram_tensor("puv", [DEPTH * NEXP, 2 * D], BF16, kind="Internal").ap()

    yp = dout_("yp", [SEQ, D])
    ys = dout_("ys", [SB * DEC_SEQ, D])
    nsp = dout_("nsp", [DEPTH, H, DK, DK])
    ncp = dout_("ncp", [DEPTH, CONV_K - 1, CW])
    nss = dout_("nss", [DEPTH, SB, H, DK, DK])
    ncs = dout_("ncs", [DEPTH, SB, CONV_K - 1, CW])

    per = NPT // NGROUPS
    groups = [list(range(g * per, (g + 1) * per)) for g in range(NGROUPS)]
    groups[-1] = groups[-1] + [STILE]
    MAXT = max(len(g) for g in groups)
    MAXP = per
    LMAX = MAXP * 128

    with ExitStack() as stk:
        T = Tracker(nc, stk)

        def sb(name, shape, dt=F32):
            return stk.enter_context(nc.sbuf_tensor(name, list(shape), dt))

        X = sb("X", [128, MAXT, D])
        W = sb("W", [128, 8 * INW], BF16)
        identf = sb("identf", [128, 128])
        identb = sb("identb", [128, 128], BF16)
        dmT = sb("dmT", [128, 2, H, 128])
        dinT = sb("dinT", [128, 2, H])
        doutT = sb("doutT", [128, 2, H])
        bmask = sb("bmask", [128, SB, 128], BF16)
        rbm = sb("rbm", [128, SB])
        iota16 = sb("iota16", [128, 16])
        R = sb("R", [128, DEPTH, H, 128])
        Rb = sb("Rb", [128, DEPTH, H, 128], BF16)
        UTAIL = sb("UTAIL", [128, DEPTH, 4, CONV_K - 1], BF16)
        xb = sb("xb", [128, D], BF16)
        xT = sb("xT", [128, 8, 128], BF16)
        UT = sb("UT", [128, 4, CONV_K - 1 + LMAX], BF16)
        UTS = sb("UTS", [128, 4, SB, CONV_K - 1 + DEC_SEQ], BF16)
        YT = sb("YT", [128, MAXT, CW], BF16)
        epsT = sb("epsT", [128, 1])
        OVN = 18432
        OV = sb("OV", [128, OVN])

        PS = [stk.enter_context(nc.psum_tensor("ps%d" % i, [128, 512], F32)) for i in range(8)]
        PSB = [p[:].bitcast(BF16) for p in PS]

        def dve(fn, r=(), w=()):
            return T.op("dve", fn, r, w)

        def act(fn, r=(), w=()):
            return T.op("act", fn, r, w)

        def pe(fn, r=(), w=()):
            return T.op("pe", fn, r, w)

        def pool(fn, r=(), w=()):
            return T.op("pool", fn, r, w)

        def ld(out, in_, r=(), w=(), pool_="ld"):
            return T.dma("sp", lambda e, out=out, in_=in_: e.dma_start(out=out, in_=in_), pool_, r, w)

        def stq(out, in_, r=(), w=(), pool_="st"):
            return T.dma("sp", lambda e, out=out, in_=in_: e.dma_start(out=out, in_=in_), pool_, r, w)

        def ldcast(out, in_, r=(), w=()):
            return T.dma("pool", lambda e, out=out, in_=in_: e.dma_start(out=out, in_=in_), "ldc", r, w, npool=2)

        ld(identf[:], c_ident, w=["identf"])
        ld(dmT[:], c_dm, w=["dmT"])
        ld(dinT[:], c_din, w=["dinT"])
        ld(doutT[:], c_dout, w=["doutT"])
        ldcast(bmask[:], c_bmask, w=["bmask"])
        ld(rbm[:], c_rbm, w=["rbm"])
        ld(iota16[:], c_iota, w=["iota16"])
        act(lambda e: e.activation(out=identb[:], in_=identf[:], func=AF.Copy), ["identf"], ["identb"])
        dve(lambda e: e.memset(R[:], 0.0), w=["R"])
        dve(lambda e: e.memset(Rb[:], 0.0), w=["Rb"])
        dve(lambda e: e.memset(UTAIL[:], 0.0), w=["UTAIL"])
        dve(lambda e: e.memset(epsT[:], LN_EPS), w=["epsT"])
        dve(lambda e: e.memset(YT[:], 0.0), w=["YT"])
        dve(lambda e: e.memset(X[:, MAXT - 1, :], 0.0), w=["X%d" % (MAXT - 1)])

        RPC = 8
        ncv = DEPTH * NEXP // (128 * RPC)
        cvbuf = [OV[:, k * RPC * 512:(k + 1) * RPC * 512].bitcast(BF16) for k in range(4)]
        dstv = puv.rearrange("(c p i) (two d) -> c p i two d", p=128, i=RPC, two=2)
        for c in range(ncv):
            for two, tsrc in enumerate((pu_flat, pv_flat)):
                srcv = tsrc.rearrange("(c p i) d -> c p (i d)", p=128, i=RPC)
                bi = (c % 2) * 2 + two
                bk = "cv%d" % bi
                ldcast(cvbuf[bi], srcv[c], w=[bk])
                stq(dstv[c][:, :, two, :], cvbuf[bi].rearrange("p (i d) -> p i d", i=RPC), r=[bk],
                    w=["puv%d_%d" % (c, two)])

        def tile_info(t):
            return (t == STILE), (1 if t == STILE else 0)

        def make_xT(slot, pref):
            xk = "X%d" % slot
            act(lambda e: e.activation(out=xb[:], in_=X[:, slot, :], func=AF.Copy), [xk], ["xb"])
            for k in range(8):
                pe(lambda e, k=k: e.transpose(out=PSB[6][:, k * 128:(k + 1) * 128], in_=xb[:, k * 128:(k + 1) * 128],
                                              identity=identb[:]), ["xb", "identb"], ["ps6"])
            dve(lambda e: e.tensor_copy(out=xT[:].rearrange("p k t -> p (k t)"), in_=PSB[6][:, :]), ["ps6"], ["xT"])

        def proj(bank, wview, col0, ncol=512):
            bk = "ps%d" % bank
            for k in range(8):
                pe(lambda e, k=k: e.matmul(PS[bank][:, 0:ncol], lhsT=xT[:, k, :], rhs=wview[:, k, col0:col0 + ncol],
                                           start=(k == 0), stop=(k == 7)), ["xT", "W"], [bk])

        def layer_norm_rows(src, srck, n, stats, mv, sd, rstd, pfx):
            nch = (n + 511) // 512
            for c in range(nch):
                dve(lambda e, c=c: e.bn_stats(out=stats[:, c, :], in_=src[:, c * 512:min(n, (c + 1) * 512)]),
                    [srck], [pfx + "stats"])
            dve(lambda e: e.bn_aggr(out=mv[:, :], in_=stats[:, 0:nch, :].rearrange("p c s -> p (c s)")),
                [pfx + "stats"], [pfx + "mv"])
            act(lambda e: e.activation(out=sd[:, :], in_=mv[:, 1:2], func=AF.Sqrt, bias=epsT[:, 0:1], scale=1.0),
                [pfx + "mv", "epsT"], [pfx + "sd"])
            dve(lambda e: e.reciprocal(out=rstd[:, :], in_=sd[:, :]), [pfx + "sd"], [pfx + "rstd"])

        def phase_a(l, tiles, first_layer):
            T.barrier()
            cv = Carver(OV, OVN)
            bcA = cv.f32(1024)
            t_cg = cv.f32(512)
            t_sig = cv.f32(512)
            t_ca = cv.f32(512)
            utile = [cv.f32(512), cv.f32(512)]
            ub = cv.bf16(512)
            cst = cv.f32(512)
            wA = W[:, 0:8 * 1024].rearrange("p (k e) -> p k e", k=8)
            ldcast(wA, w_in[l].rearrange("(k p) e -> p k e", p=128)[:, :, 2048:3072], w=["W"])
            ld(bcA, b_in[l, 2048:3072].partition_broadcast(128), w=["A_bc"])
            dve(lambda e: e.tensor_copy(out=UT[:, :, 0:CONV_K - 1], in_=UTAIL[:, l, :, :]), ["UTAIL"], ["UT"])
            ptiles = [t for t in tiles if t != STILE]
            for slot, t in enumerate(tiles):
                if "A" in DBG_BAR:
                    T.barrier()
                samp, _ = tile_info(t)
                xk = "X%d" % slot
                if first_layer:
                    if samp:
                        ld(X[0:SB * DEC_SEQ, slot, :], xs, w=[xk])
                    else:
                        ld(X[:, slot, :], xp[t * 128:(t + 1) * 128, :], w=[xk])
                make_xT(slot, "A")
                proj(0, wA, 0)
                proj(1, wA, 512)
                ut = utile[slot % 2]
                uk = "A_ut%d" % (slot % 2)
                dve(lambda e: e.tensor_tensor(out=t_cg, in0=PS[1][:, :], in1=bcA[:, 512:1024], op=ALU.add),
                    ["ps1", "A_bc"], ["A_cg"])
                act(lambda e: e.activation(out=t_sig, in_=t_cg, func=AF.Sigmoid), ["A_cg"], ["A_sig"])
                dve(lambda e: e.tensor_tensor(out=t_ca, in0=PS[0][:, :], in1=bcA[:, 0:512], op=ALU.add),
                    ["ps0", "A_bc"], ["A_ca"])
                dve(lambda e, ut=ut: e.tensor_tensor(out=ut, in0=t_ca, in1=t_sig, op=ALU.mult),
                    ["A_ca", "A_sig"], [uk])
                act(lambda e, ut=ut: e.activation(out=ub, in_=ut, func=AF.Copy), [uk], ["A_ub"])
                for c4 in range(4):
                    pe(lambda e, c4=c4: e.transpose(out=PSB[6][:, c4 * 128:(c4 + 1) * 128],
                                                    in_=ub[:, c4 * 128:(c4 + 1) * 128], identity=identb[:]),
                       ["A_ub", "identb"], ["ps6"])
                if not samp:
                    off = CONV_K - 1 + slot * 128
                    dve(lambda e, off=off: e.tensor_copy(
                        out=UT[:, :, off:off + 128],
                        in_=PSB[6][:, 0:512].rearrange("p (c t) -> p c t", c=4)), ["ps6"], ["UT"])
                    if t == NPT - 1:
                        stq(ncp[l], ut[128 - (CONV_K - 1):128, :], r=[uk])
                else:
                    for c4 in range(4):
                        dve(lambda e, c4=c4: e.tensor_copy(
                            out=UTS[:, c4, :, CONV_K - 1:CONV_K - 1 + DEC_SEQ],
                            in_=PSB[6][:, c4 * 128:c4 * 128 + SB * DEC_SEQ].rearrange("p (b r) -> p b r", r=DEC_SEQ)),
                            ["ps6"], ["UTS"])
                    for b in range(SB):
                        stq(ncs[l, b, CONV_K - 1 - DEC_SEQ:CONV_K - 1, :], ut[b * DEC_SEQ:(b + 1) * DEC_SEQ, :], r=[uk])
                    stq(ncs[l, :, 0:CONV_K - 1 - DEC_SEQ, :], cc_in[l, :, DEC_SEQ:CONV_K - 1, :])
                    nb = 4
                    rows = nb * (CONV_K - 1)
                    for g4 in range(SB // nb):
                        cstv = cst[0:rows, 0:512]
                        ld(cstv, cc_in[l, g4 * nb:(g4 + 1) * nb].rearrange("b r c -> (b r) c"), w=["A_cst"])
                        for c4 in range(4):
                            pe(lambda e, c4=c4: e.transpose(out=PS[7][:, c4 * 128:c4 * 128 + rows],
                                                            in_=cst[0:rows, c4 * 128:(c4 + 1) * 128],
                                                            identity=identf[0:rows, 0:rows]),
                               ["A_cst", "identf"], ["ps7"])
                        for c4 in range(4):
                            dve(lambda e, c4=c4, g4=g4: e.tensor_copy(
                                out=UTS[:, c4, g4 * nb:(g4 + 1) * nb, 0:CONV_K - 1],
                                in_=PS[7][:, c4 * 128:c4 * 128 + rows].rearrange("p (b r) -> p b r", r=CONV_K - 1)),
                                ["ps7"], ["UTS"])

        def phase_b(l, tiles):
            T.barrier()
            cv = Carver(OV, OVN)
            YCH = cv.f32(LMAX)
            YS = cv.f32(4 * SB * DEC_SEQ).rearrange("p (c b r) -> p c b r", c=4, b=SB)
            dwst = cv.f32(512)
            dwT = cv.f32(4 * 32).rearrange("p (c k) -> p c k", c=4)
            dwb = cv.f32(512)
            ptiles = [t for t in tiles if t != STILE]
            L = len(ptiles) * 128
            ld(dwst[0:CONV_K, :], dw_k[l], w=["B_dwst"])
            ld(dwb, dw_b[l].partition_broadcast(128), w=["B_dwb"])
            for c4 in range(4):
                pe(lambda e, c4=c4: e.transpose(out=PS[7][:, c4 * 32:c4 * 32 + CONV_K],
                                                in_=dwst[0:CONV_K, c4 * 128:(c4 + 1) * 128],
                                                identity=identf[0:CONV_K, 0:CONV_K]),
                   ["B_dwst", "identf"], ["ps7"])
            dve(lambda e: e.tensor_copy(out=dwT[:, :, 0:CONV_K],
                                        in_=PS[7][:, 0:128].rearrange("p (c k) -> p c k", c=4)[:, :, 0:CONV_K]),
                ["ps7"], ["B_dwT"])
            for c4 in range(4):
                for k in range(CONV_K):
                    if k == 0:
                        dve(lambda e, c4=c4: e.tensor_scalar(out=YCH[:, 0:L], in0=UT[:, c4, 0:L],
                                                             scalar1=dwT[:, c4, 0:1], scalar2=None, op0=ALU.mult),
                            ["UT", "B_dwT"], ["B_ych"])
                    else:
                        dve(lambda e, c4=c4, k=k: e.scalar_tensor_tensor(
                            out=YCH[:, 0:L], in0=UT[:, c4, k:k + L], scalar=dwT[:, c4, k:k + 1],
                            in1=YCH[:, 0:L], op0=ALU.mult, op1=ALU.add), ["UT", "B_dwT", "B_ych"], ["B_ych"])
                ntl = len(ptiles)
                for t0 in range(0, ntl, 4):
                    n4 = min(4, ntl - t0)
                    for i in range(n4):
                        pe(lambda e, i=i, t0=t0: e.transpose(out=PS[7][:, i * 128:(i + 1) * 128],
                                                             in_=YCH[:, (t0 + i) * 128:(t0 + i + 1) * 128],
                                                             identity=identf[:]), ["B_ych", "identf"], ["ps7"])
                    dve(lambda e, c4=c4, t0=t0, n4=n4: e.tensor_tensor(
                        out=YT[:, t0:t0 + n4, c4 * 128:(c4 + 1) * 128],
                        in0=PS[7][:, 0:n4 * 128].rearrange("p (t c) -> p t c", c=128),
                        in1=dwb[:, c4 * 128:(c4 + 1) * 128].unsqueeze(1).to_broadcast([128, n4, 128]),
                        op=ALU.add), ["ps7", "B_dwb"], ["YT"])
            dve(lambda e: e.tensor_copy(out=UTAIL[:, l, :, :], in_=UT[:, :, L:L + CONV_K - 1]), ["UT"], ["UTAIL"])
            if STILE in tiles:
                slot = tiles.index(STILE)
                for c4 in range(4):
                    for k in range(CONV_K):
                        if k == 0:
                            dve(lambda e, c4=c4: e.tensor_scalar(out=YS[:, c4, :, :], in0=UTS[:, c4, :, 0:DEC_SEQ],
                                                                 scalar1=dwT[:, c4, 0:1], scalar2=None, op0=ALU.mult),
                                ["UTS", "B_dwT"], ["B_ys"])
                        else:
                            dve(lambda e, c4=c4, k=k: e.scalar_tensor_tensor(
                                out=YS[:, c4, :, :], in0=UTS[:, c4, :, k:k + DEC_SEQ], scalar=dwT[:, c4, k:k + 1],
                                in1=YS[:, c4, :, :], op0=ALU.mult, op1=ALU.add), ["UTS", "B_dwT", "B_ys"], ["B_ys"])
                nst = SB * DEC_SEQ
                for c4 in range(4):
                    pe(lambda e, c4=c4: e.transpose(out=PS[7][0:nst, c4 * 128:(c4 + 1) * 128],
                                                    in_=YS[:, c4, :, :].rearrange("p b r -> p (b r)"),
                                                    identity=identf[:]), ["B_ys", "identf"], ["ps7"])
                dve(lambda e: e.tensor_tensor(out=YT[0:nst, slot, :], in0=PS[7][0:nst, :], in1=dwb[0:nst, :],
                                              op=ALU.add), ["ps7", "B_dwb"], ["YT"])

        def phase_c1(l, tiles):
            T.barrier()
            cv = Carver(OV, OVN)
            bc = cv.f32(2048)
            g1 = cv.f32(1024)
            b1 = cv.f32(1024)
            gc = cv.f32(512)
            bcn = cv.f32(512)
            rp = cv.f32(512).rearrange("p (a f) -> p a f", a=2)
            qf = cv.f32(512)
            kf = cv.f32(512)
            m1 = cv.f32(512)
            m2 = cv.f32(512)
            of = cv.f32(512)
            gs = cv.f32(512)
            y1 = cv.f32(1024)
            stats = cv.f32(24).rearrange("p (c s) -> p c s", s=6)
            mv = cv.f32(8).rearrange("p (h s) -> p h s", s=2)
            sd = cv.f32(4)
            rstd = cv.f32(4)
            qr = cv.bf16(512)
            kr = cv.bf16(512)
            qTt = cv.bf16(512).rearrange("p (h t) -> p h t", h=4)
            kTt = cv.bf16(512).rearrange("p (h t) -> p h t", h=4)
            vb = cv.bf16(512).rearrange("p (h e) -> p h e", h=4)
            vdec = cv.bf16(512).rearrange("p (h e) -> p h e", h=4)
            STt = cv.bf16(512).rearrange("p (h t) -> p h t", h=4)
            cat = cv.bf16(1024)
            catT = cv.bf16(1024).rearrange("p (k t) -> p k t", k=8)
            R0f = cv.f32(SB * 128).rearrange("p (b e) -> p b e", b=SB)
            R0b = cv.bf16(SB * 128).rearrange("p (b e) -> p b e", b=SB)
            qTx = cv.bf16(SB * 128).rearrange("p (b t) -> p b t", b=SB)
            rbig = cv.bf16(SB * 128).rearrange("p (b e) -> p b e", b=SB)

            wq4 = W[:, 0:8 * 2048].rearrange("p (k e) -> p k e", k=8)
            wo = W[:, 8 * 2048:8 * 3072].rearrange("p (k e) -> p k e", k=8)
            ldcast(wq4, w_in[l].rearrange("(k p) e -> p k e", p=128)[:, :, 0:2048], w=["W"])
            ldcast(wo, w_out[l].rearrange("(k p) e -> p k e", p=128), w=["W"])
            ld(bc, b_in[l, 0:2048].partition_broadcast(128), w=["C_bc"])
            ld(g1, ln1_g[l].partition_broadcast(128), w=["C_g1"])
            ld(b1, ln1_b[l].partition_broadcast(128), w=["C_b1"])
            ld(gc, cln_g[l].partition_broadcast(128), w=["C_gc"])
            ld(bcn, cln_b[l].partition_broadcast(128), w=["C_bcn"])

            for slot, t in enumerate(tiles):
                if "C" in DBG_BAR:
                    T.barrier()
                samp, gi = tile_info(t)
                xk = "X%d" % slot
                ld(rp[:].rearrange("p a f -> p (a f)"), c_rope[t].rearrange("p a f -> p (a f)"), w=["C_rp"])
                make_xT(slot, "C")

                def rope_pair(bank, boff, dstf, a, dst_bf, dk):
                    bk = "ps%d" % bank
                    dve(lambda e: e.tensor_tensor(out=dstf, in0=PS[bank][:, :], in1=bc[:, boff:boff + 512],
                                                  op=ALU.add), [bk, "C_bc"], [dk])
                    x4 = dstf.rearrange("p (h f) -> p h f", h=4)
                    m14 = m1.rearrange("p (h f) -> p h f", h=4)
                    m24 = m2.rearrange("p (h f) -> p h f", h=4)
                    dve(lambda e: e.tensor_tensor(out=m14, in0=x4,
                                                  in1=rp[:, a, 0:128].unsqueeze(1).to_broadcast([128, 4, 128]),
                                                  op=ALU.mult), [dk, "C_rp"], ["C_m1"])
                    dve(lambda e: e.tensor_tensor(out=m24[:, :, 0:64], in0=x4[:, :, 64:128],
                                                  in1=rp[:, a, 128:192].unsqueeze(1).to_broadcast([128, 4, 64]),
                                                  op=ALU.mult), [dk, "C_rp"], ["C_m2"])
                    dve(lambda e: e.tensor_tensor(out=m24[:, :, 64:128], in0=x4[:, :, 0:64],
                                                  in1=rp[:, a, 192:256].unsqueeze(1).to_broadcast([128, 4, 64]),
                                                  op=ALU.mult), [dk, "C_rp", "C_m2"], ["C_m2"])
                    dve(lambda e: e.tensor_tensor(out=dst_bf, in0=m1, in1=m2, op=ALU.add), ["C_m1", "C_m2"],
                        [dk + "r"])

                proj(0, wq4, 0)
                proj(1, wq4, 512)
                rope_pair(0, 0, qf, 0, qr, "C_qf")
                rope_pair(1, 512, kf, 1, kr, "C_kf")
                proj(0, wq4, 1024)
                proj(1, wq4, 1536)
                dve(lambda e: e.tensor_tensor(out=qf, in0=PS[0][:, :], in1=bc[:, 1024:1536], op=ALU.add),
                    ["ps0", "C_bc"], ["C_qf"])
                act(lambda e: e.activation(out=vb[:].rearrange("p h e -> p (h e)"), in_=qf, func=AF.Copy),
                    ["C_qf"], ["C_vb"])
                dve(lambda e, gi=gi: e.tensor_tensor(out=vdec[:], in0=qf.rearrange("p (h e) -> p h e", h=4),
                                                     in1=doutT[:, gi, :].unsqueeze(2).to_broadcast([128, 4, 128]),
                                                     op=ALU.mult), ["C_qf", "doutT"], ["C_vdec"])
                dve(lambda e: e.tensor_tensor(out=kf, in0=PS[1][:, :], in1=bc[:, 1536:2048], op=ALU.add),
                    ["ps1", "C_bc"], ["C_kf"])
                act(lambda e: e.activation(out=gs, in_=kf, func=AF.Silu), ["C_kf"], ["C_gs"])
                for h in range(4):
                    pe(lambda e, h=h: e.transpose(out=PSB[6][:, h * 128:(h + 1) * 128], in_=qr[:, h * 128:(h + 1) * 128],
                                                  identity=identb[:]), ["C_qfr", "identb"], ["ps6"])
                for h in range(4):
                    pe(lambda e, h=h: e.transpose(out=PSB[6][:, 512 + h * 128:512 + (h + 1) * 128],
                                                  in_=kr[:, h * 128:(h + 1) * 128], identity=identb[:]),
                       ["C_kfr", "identb"], ["ps6"])
                act(lambda e: e.activation(out=qTt[:].rearrange("p h t -> p (h t)"), in_=PSB[6][:, 0:512], func=AF.Copy),
                    ["ps6"], ["C_qT"])
                act(lambda e: e.activation(out=kTt[:].rearrange("p h t -> p (h t)"), in_=PSB[6][:, 512:1024],
                                           func=AF.Copy), ["ps6"], ["C_kT"])
                for h in range(4):
                    pe(lambda e, h=h: e.matmul(PS[2][:, h * 128:(h + 1) * 128], lhsT=kTt[:, h, :], rhs=qTt[:, h, :],
                                               start=True, stop=True), ["C_kT", "C_qT"], ["ps2"])
                dve(lambda e, gi=gi: e.tensor_tensor(out=STt[:].rearrange("p h t -> p (h t)"), in0=PS[2][:, :],
                                                     in1=dmT[:, gi, :, :].rearrange("p h t -> p (h t)"), op=ALU.mult),
                    ["ps2", "dmT"], ["C_ST"])
                for h in range(4):
                    pe(lambda e, h=h: e.matmul(PS[3][:, h * 128:(h + 1) * 128], lhsT=STt[:, h, :], rhs=vb[:, h, :],
                                               start=True, stop=True), ["C_ST", "C_vb"], ["ps3"])
                if not samp:
                    for h in range(4):
                        pe(lambda e, h=h: e.matmul(PS[4][:, h * 128:(h + 1) * 128], lhsT=qTt[:, h, :],
                                                   rhs=Rb[:, l, h, :], start=True, stop=True), ["C_qT", "Rb"], ["ps4"])
                    for h in range(4):
                        pe(lambda e, h=h: e.matmul(PS[5][:, h * 128:(h + 1) * 128], lhsT=kr[:, h * 128:(h + 1) * 128],
                                                   rhs=vdec[:, h, :], start=True, stop=True), ["C_kfr", "C_vdec"],
                           ["ps5"])
                    for h in range(4):
                        dve(lambda e, h=h: e.scalar_tensor_tensor(out=R[:, l, h, :], in0=R[:, l, h, :],
                                                                  scalar=float(cd[0, h]),
                                                                  in1=PS[5][:, h * 128:(h + 1) * 128],
                                                                  op0=ALU.mult, op1=ALU.add), ["R", "ps5"], ["R"])
                    act(lambda e: e.activation(out=Rb[:, l, :, :], in_=R[:, l, :, :], func=AF.Copy), ["R"], ["Rb"])
                    if t == NPT - 1:
                        stq(nsp[l].rearrange("h d e -> d h e"), R[:, l, :, :], r=["R"])
                else:
                    for h in range(4):
                        ld(R0f[:], st_in[l, :, h].rearrange("b d e -> d b e"), w=["C_R0f"])
                        act(lambda e: e.activation(out=R0b[:], in_=R0f[:], func=AF.Copy), ["C_R0f"], ["C_R0b"])
                        dve(lambda e, h=h: e.tensor_tensor(out=qTx[:],
                                                           in0=qTt[:, h, :].unsqueeze(1).to_broadcast([128, SB, 128]),
                                                           in1=bmask[:], op=ALU.mult), ["C_qT", "bmask"], ["C_qTx"])
                        for b in range(SB):
                            pe(lambda e, h=h, b=b: e.matmul(PS[4][:, h * 128:(h + 1) * 128], lhsT=qTx[:, b, :],
                                                            rhs=R0b[:, b, :], start=(b == 0), stop=(b == SB - 1)),
                               ["C_qTx", "C_R0b"], ["ps4"])
                        dve(lambda e, h=h: e.tensor_tensor(out=rbig[:],
                                                           in0=vdec[:, h, :].unsqueeze(1).to_broadcast([128, SB, 128]),
                                                           in1=rbm[:, :].unsqueeze(2).to_broadcast([128, SB, 128]),
                                                           op=ALU.mult), ["C_vdec", "rbm"], ["C_rbig"])
                        for g4 in range(SB // 4):
                            pe(lambda e, h=h, g4=g4: e.matmul(
                                PS[5][:, :], lhsT=kr[:, h * 128:(h + 1) * 128],
                                rhs=rbig[:, g4 * 4:(g4 + 1) * 4, :].rearrange("p b e -> p (b e)"),
                                start=True, stop=True), ["C_kfr", "C_rbig"], ["ps5"])
                            dve(lambda e, h=h, g4=g4: e.scalar_tensor_tensor(
                                out=R0f[:, g4 * 4:(g4 + 1) * 4, :].rearrange("p b e -> p (b e)"),
                                in0=R0f[:, g4 * 4:(g4 + 1) * 4, :].rearrange("p b e -> p (b e)"),
                                scalar=float(cd[1, h]), in1=PS[5][:, :], op0=ALU.mult, op1=ALU.add),
                                ["C_R0f", "ps5"], ["C_R0f"])
                        stq(nss[l, :, h].rearrange("b d e -> d b e"), R0f[:], r=["C_R0f"])
                dve(lambda e, gi=gi: e.tensor_tensor(out=m1.rearrange("p (h e) -> p h e", h=4),
                                                     in0=PS[4][:, :].rearrange("p (h e) -> p h e", h=4),
                                                     in1=dinT[:, gi, :].unsqueeze(2).to_broadcast([128, 4, 128]),
                                                     op=ALU.mult), ["ps4", "dinT"], ["C_m1"])
                dve(lambda e: e.tensor_tensor(out=of, in0=PS[3][:, :], in1=m1, op=ALU.add), ["ps3", "C_m1"], ["C_of"])
                for h in range(4):
                    dve(lambda e, h=h: e.bn_stats(out=stats[:, h, :], in_=of[:, h * 128:(h + 1) * 128]),
                        ["C_of"], ["C_stats"])
                for h in range(4):
                    dve(lambda e, h=h: e.bn_aggr(out=mv[:, h, :], in_=stats[:, h, :]), ["C_stats"], ["C_mv"])
                act(lambda e: e.activation(out=sd[:, :], in_=mv[:, :, 1], func=AF.Sqrt, bias=epsT[:, 0:1], scale=1.0),
                    ["C_mv", "epsT"], ["C_sd"])
                dve(lambda e: e.reciprocal(out=rstd[:, :], in_=sd[:, :]), ["C_sd"], ["C_rstd"])
                of4 = of.rearrange("p (h e) -> p h e", h=4)
                dve(lambda e: e.tensor_tensor(out=of4, in0=of4, in1=mv[:, :, 0:1].to_broadcast([128, 4, 128]),
                                              op=ALU.subtract), ["C_of", "C_mv"], ["C_of"])
                dve(lambda e: e.tensor_tensor(out=of4, in0=of4, in1=rstd[:, :].unsqueeze(2).to_broadcast([128, 4, 128]),
                                              op=ALU.mult), ["C_of", "C_rstd"], ["C_of"])
                dve(lambda e: e.tensor_tensor(out=cat[:, 0:512], in0=of, in1=gs, op=ALU.mult), ["C_of", "C_gs"],
                    ["C_cat"])
                dve(lambda e, slot=slot: e.bn_stats(out=stats[:, 0, :], in_=YT[:, slot, :]), ["YT"], ["C_stats"])
                dve(lambda e: e.bn_aggr(out=mv[:, 0, :], in_=stats[:, 0, :]), ["C_stats"], ["C_mv"])
                act(lambda e: e.activation(out=sd[:, 0:1], in_=mv[:, 0, 1:2], func=AF.Sqrt, bias=epsT[:, 0:1], scale=1.0),
                    ["C_mv", "epsT"], ["C_sd"])
                dve(lambda e: e.reciprocal(out=rstd[:, 0:1], in_=sd[:, 0:1]), ["C_sd"], ["C_rstd"])
                dve(lambda e, slot=slot: e.tensor_scalar(out=m2, in0=YT[:, slot, :], scalar1=mv[:, 0, 0:1],
                                                         scalar2=rstd[:, 0:1], op0=ALU.subtract, op1=ALU.mult),
                    ["YT", "C_mv", "C_rstd"], ["C_m2"])
                dve(lambda e: e.tensor_tensor(out=m2, in0=m2, in1=gc, op=ALU.mult), ["C_m2", "C_gc"], ["C_m2"])
                dve(lambda e: e.tensor_tensor(out=m2, in0=m2, in1=bcn, op=ALU.add), ["C_m2", "C_bcn"], ["C_m2"])
                act(lambda e: e.activation(out=cat[:, 512:1024], in_=m2, func=AF.Silu), ["C_m2"], ["C_cat"])
                for k in range(8):
                    pe(lambda e, k=k: e.transpose(out=PSB[6][:, k * 128:(k + 1) * 128], in_=cat[:, k * 128:(k + 1) * 128],
                                                  identity=identb[:]), ["C_cat", "identb"], ["ps6"])
                act(lambda e: e.activation(out=catT[:].rearrange("p k t -> p (k t)"), in_=PSB[6][:, :], func=AF.Copy),
                    ["ps6"], ["C_catT"])
                for half in range(2):
                    bk = "ps%d" % half
                    for k in range(8):
                        pe(lambda e, k=k, half=half: e.matmul(PS[half][:, :], lhsT=catT[:, k, :],
                                                              rhs=wo[:, k, half * 512:(half + 1) * 512],
                                                              start=(k == 0), stop=(k == 7)), ["C_catT", "W"], [bk])
                    dve(lambda e, half=half, slot=slot: e.scalar_tensor_tensor(
                        out=y1[:, half * 512:(half + 1) * 512], in0=X[:, slot, half * 512:(half + 1) * 512],
                        scalar=float(ALPHA), in1=PS[half][:, :], op0=ALU.mult, op1=ALU.add), [xk, bk], ["C_y1"])
                layer_norm_rows(y1, "C_y1", D, stats, mv[:, 0, :], sd[:, 0:1], rstd[:, 0:1], "C_")
                dve(lambda e: e.tensor_scalar(out=y1, in0=y1, scalar1=mv[:, 0, 0:1], scalar2=rstd[:, 0:1],
                                              op0=ALU.subtract, op1=ALU.mult), ["C_y1", "C_mv", "C_rstd"], ["C_y1"])
                dve(lambda e: e.tensor_tensor(out=y1, in0=y1, in1=g1, op=ALU.mult), ["C_y1", "C_g1"], ["C_y1"])
                dve(lambda e, slot=slot: e.tensor_tensor(out=X[:, slot, :], in0=y1, in1=b1, op=ALU.add),
                    ["C_y1", "C_b1"], [xk])

        def phase_c2(l, tiles, last_layer):
            T.barrier()
            cv = Carver(OV, OVN)
            A8 = cv.f32(2048)
            B8 = cv.f32(2048)
            C8 = cv.f32(2048)
            UTflat = UT[:].rearrange("p c n -> p (c n)")
            UTSflat = UTS[:].rearrange("p c b n -> p (c b n)")
            YTflat = YT[:].rearrange("p t c -> p (t c)")
            UG = [cv.bf16(2048) for _ in range(4)] + [UTflat[:, 0:2048], UTflat[:, 2048:4096], UTSflat[:, 0:2048]]
            UG += [W[:, 8 * 2048 + k * 2048:8 * 2048 + (k + 1) * 2048] for k in range(4)]
            NG = len(UG)
            y2 = YTflat[:, 0:2048].bitcast(F32)
            junk = YTflat[:, 2048:3072]
            xbb = [xb[:, :], YTflat[:, 3072:4096]]
            xbk = ["xb", "P_xb2"]
            gact = cv.f32(128)
            g2 = cv.f32(1024)
            b2 = cv.f32(1024)
            vals = cv.f32(256).rearrange("p (h s k) -> p h s k", h=PH, s=2)
            idxu = cv.u32(256).rearrange("p (h s k) -> p h s k", h=PH, s=2)
            idxf = cv.f32(256).rearrange("p (h s k) -> p h s k", h=PH, s=2)
            sv = cv.f32(128).rearrange("p (h k) -> p h k", h=PH)
            ciu = cv.u32(128).rearrange("p (h k) -> p h k", h=PH)
            a16u = cv.u32(128).rearrange("p (h k) -> p h k", h=PH)
            b16u = cv.u32(128).rearrange("p (h k) -> p h k", h=PH)
            a16f = cv.f32(128).rearrange("p (h k) -> p h k", h=PH)
            b16f = cv.f32(128).rearrange("p (h k) -> p h k", h=PH)
            i1s = cv.f32(128).rearrange("p (h k) -> p h k", h=PH)
            i2s = cv.f32(128).rearrange("p (h k) -> p h k", h=PH)
            eidf = cv.f32(128)
            eidu2 = [cv.u32(128), cv.u32(128)]
            ex = cv.f32(128).rearrange("p (h k) -> p h k", h=PH)
            zs = cv.f32(8)
            rz = cv.f32(8)
            gate2 = [cv.f32(128).rearrange("p (h k) -> p h k", h=PH), cv.f32(128).rearrange("p (h k) -> p h k", h=PH)]
            actv = cv.f32(128)
            coef = cv.f32(128)
            dg = [cv.bf16(128) for _ in range(4)]
            stats = cv.f32(12).rearrange("p (c s) -> p c s", s=6)
            mv = cv.f32(2)
            sd = cv.f32(1)
            rstd = cv.f32(1)
            qT16 = cv.bf16(2048).rearrange("p (c t) -> p c t", c=16)
            keysT = cv.bf16(2048).rearrange("p (c t) -> p c t", c=16)
            kst = cv.bf16(1024)

            wqv = W[:, 0:8 * 2048].rearrange("p (k e) -> p k e", k=8)
            ldcast(wqv, w_q[l].rearrange("(k p) e -> p k e", p=128), w=["W"])
            ld(g2, ln2_g[l].partition_broadcast(128), w=["P_g2"])
            ld(b2, ln2_b[l].partition_broadcast(128), w=["P_b2"])
            for side, skd in enumerate((sk1, sk2)):
                ld(A8[:, 0:PH * 128].rearrange("p (h d) -> p h d", h=PH), skd[l].rearrange("h k d -> k h d"), w=["P_A8"])
                act(lambda e: e.activation(out=kst, in_=A8[:, 0:PH * 128], func=AF.Copy), ["P_A8"], ["P_kst"])
                for h in range(PH):
                    pe(lambda e, h=h: e.transpose(out=PSB[6][:, h * 128:(h + 1) * 128], in_=kst[:, h * 128:(h + 1) * 128],
                                                  identity=identb[:]), ["P_kst", "identb"], ["ps6"])
                for h in range(PH):
                    dve(lambda e, h=h, side=side: e.tensor_copy(out=keysT[:, 2 * h + side, :],
                                                                in_=PSB[6][:, h * 128:(h + 1) * 128]),
                        ["ps6"], ["P_keysT"])

            QBANK = [0, 3, 0, 3]
            SBANK = [4, 5, 7, 4]

            def sel_ops(slot, pb):
                ops = []

                def D_(fn, r=(), w=()):
                    ops.append(lambda: T.op("dve", fn, r, w))

                def A_(fn, r=(), w=()):
                    ops.append(lambda: T.op("act", fn, r, w))

                def P_(fn, r=(), w=()):
                    ops.append(lambda: T.op("pe", fn, r, w))

                xk = "X%d" % slot
                xbt = xbb[pb]
                xbtk = xbk[pb]
                eidu = eidu2[pb]
                eiduk = "P_eidu%d" % pb
                gate = gate2[pb]
                gatek = "P_gate%d" % pb
                A_(lambda e: e.activation(out=xbt, in_=X[:, slot, :], func=AF.Copy), [xk], [xbtk])
                for k in range(8):
                    P_(lambda e, k=k: e.transpose(out=PSB[6][:, k * 128:(k + 1) * 128], in_=xbt[:, k * 128:(k + 1) * 128],
                                                  identity=identb[:]), [xbtk, "identb"], ["ps6"])
                D_(lambda e: e.tensor_copy(out=xT[:].rearrange("p k t -> p (k t)"), in_=PSB[6][:, :]), ["ps6"], ["xT"])
                for q4 in range(4):
                    bank = QBANK[q4]
                    bk = "ps%d" % bank
                    for c in range(q4 * 4, q4 * 4 + 4):
                        for k in range(8):
                            P_(lambda e, c=c, k=k, bank=bank: e.matmul(
                                PS[bank][:, (c % 4) * 128:(c % 4 + 1) * 128], lhsT=wqv[:, k, c * 128:(c + 1) * 128],
                                rhs=xT[:, k, :], start=(k == 0), stop=(k == 7)), ["xT", "W"], [bk])
                    A_(lambda e, q4=q4, bank=bank: e.activation(
                        out=qT16[:, q4 * 4:(q4 + 1) * 4, :].rearrange("p c t -> p (c t)"), in_=PS[bank][:, :],
                        func=AF.Copy), [bk], ["P_qT16"])
                Sv = A8.rearrange("p (c k) -> p c k", c=16)
                S2v = B8.rearrange("p (c k) -> p c k", c=16)
                for q4 in range(4):
                    bank = SBANK[q4]
                    bk = "ps%d" % bank
                    for c in range(q4 * 4, q4 * 4 + 4):
                        P_(lambda e, c=c, bank=bank: e.matmul(PS[bank][:, (c % 4) * 128:(c % 4 + 1) * 128],
                                                              lhsT=qT16[:, c, :], rhs=keysT[:, c, :], start=True,
                                                              stop=True), ["P_qT16", "P_keysT"], [bk])
                    A_(lambda e, q4=q4, bank=bank: e.activation(out=A8[:, q4 * 512:(q4 + 1) * 512], in_=PS[bank][:, :],
                                                                func=AF.Copy), [bk], ["P_A8"])
                for c in range(16):
                    h, s_ = c // 2, c % 2
                    D_(lambda e, c=c, h=h, s_=s_: e.max(out=vals[:, h, s_, 0:8], in_=Sv[:, c, :]), ["P_A8"], ["P_vals"])
                    D_(lambda e, c=c, h=h, s_=s_: e.max_index(out=idxu[:, h, s_, 0:8], in_max=vals[:, h, s_, 0:8],
                                                              in_values=Sv[:, c, :]), ["P_A8", "P_vals"], ["P_idxu"])
                    D_(lambda e, c=c, h=h, s_=s_: e.match_replace(out=S2v[:, c, :], in_to_replace=vals[:, h, s_, 0:8],
                                                                  in_values=Sv[:, c, :], imm_value=-1e30),
                       ["P_A8", "P_vals"], ["P_B8"])
                    D_(lambda e, c=c, h=h, s_=s_: e.max(out=vals[:, h, s_, 8:16], in_=S2v[:, c, :]), ["P_B8"], ["P_vals"])
                    D_(lambda e, c=c, h=h, s_=s_: e.max_index(out=idxu[:, h, s_, 8:16], in_max=vals[:, h, s_, 8:16],
                                                              in_values=S2v[:, c, :]), ["P_B8", "P_vals"], ["P_idxu"])
                D_(lambda e: e.tensor_copy(out=idxf[:], in_=idxu[:]), ["P_idxu"], ["P_idxf"])
                cand = A8.rearrange("p (h a b) -> p h a b", h=PH, a=16)
                D_(lambda e: e.tensor_tensor(out=cand, in0=vals[:, :, 0, :].unsqueeze(3).to_broadcast([128, PH, 16, 16]),
                                             in1=vals[:, :, 1, :].unsqueeze(2).to_broadcast([128, PH, 16, 16]),
                                             op=ALU.add), ["P_vals"], ["P_A8"])
                cf = A8.rearrange("p (h n) -> p h n", h=PH)
                cf2 = B8.rearrange("p (h n) -> p h n", h=PH)
                for h in range(PH):
                    D_(lambda e, h=h: e.max(out=sv[:, h, 0:8], in_=cf[:, h, :]), ["P_A8"], ["P_sv"])
                    D_(lambda e, h=h: e.max_index(out=ciu[:, h, 0:8], in_max=sv[:, h, 0:8], in_values=cf[:, h, :]),
                       ["P_A8", "P_sv"], ["P_ciu"])
                    D_(lambda e, h=h: e.match_replace(out=cf2[:, h, :], in_to_replace=sv[:, h, 0:8],
                                                      in_values=cf[:, h, :], imm_value=-1e30), ["P_A8", "P_sv"], ["P_B8"])
                    D_(lambda e, h=h: e.max(out=sv[:, h, 8:16], in_=cf2[:, h, :]), ["P_B8"], ["P_sv"])
                    D_(lambda e, h=h: e.max_index(out=ciu[:, h, 8:16], in_max=sv[:, h, 8:16], in_values=cf2[:, h, :]),
                       ["P_B8", "P_sv"], ["P_ciu"])
                D_(lambda e: e.tensor_single_scalar(out=a16u[:], in_=ciu[:], scalar=4, op=ALU.logical_shift_right),
                   ["P_ciu"], ["P_a16u"])
                D_(lambda e: e.tensor_single_scalar(out=b16u[:], in_=ciu[:], scalar=15, op=ALU.bitwise_and),
                   ["P_ciu"], ["P_b16u"])
                D_(lambda e: e.tensor_copy(out=a16f[:], in_=a16u[:]), ["P_a16u"], ["P_a16f"])
                D_(lambda e: e.tensor_copy(out=b16f[:], in_=b16u[:]), ["P_b16u"], ["P_b16f"])
                eq = C8.rearrange("p (h k a) -> p h k a", h=PH, k=16)
                io_b = iota16[:, :].unsqueeze(1).unsqueeze(1).to_broadcast([128, PH, 16, 16])
                for (pf, sidx, dst, nm) in ((a16f, 0, i1s, "P_i1s"), (b16f, 1, i2s, "P_i2s")):
                    D_(lambda e, pf=pf: e.tensor_tensor(out=eq, in0=io_b,
                                                        in1=pf[:].unsqueeze(3).to_broadcast([128, PH, 16, 16]),
                                                        op=ALU.is_equal), ["iota16", "P_a16f", "P_b16f"], ["P_C8"])
                    D_(lambda e, sidx=sidx: e.tensor_tensor(
                        out=eq, in0=eq, in1=idxf[:, :, sidx, :].unsqueeze(2).to_broadcast([128, PH, 16, 16]),
                        op=ALU.mult), ["P_C8", "P_idxf"], ["P_C8"])
                    D_(lambda e, dst=dst: e.tensor_reduce(out=dst[:], in_=eq, axis=AX.X, op=ALU.add), ["P_C8"], [nm])
                D_(lambda e: e.scalar_tensor_tensor(out=eidf, in0=i1s[:].rearrange("p h k -> p (h k)"),
                                                    scalar=float(NKEYS), in1=i2s[:].rearrange("p h k -> p (h k)"),
                                                    op0=ALU.mult, op1=ALU.add), ["P_i1s", "P_i2s"], ["P_eidf"])
                if l > 0:
                    D_(lambda e: e.tensor_scalar(out=eidf, in0=eidf, scalar1=float(l * NEXP), scalar2=None,
                                                 op0=ALU.add), ["P_eidf"], ["P_eidf"])
                D_(lambda e: e.tensor_copy(out=eidu, in_=eidf), ["P_eidf"], [eiduk])
                D_(lambda e: e.tensor_tensor(out=ex[:], in0=sv[:], in1=sv[:, :, 0:1].to_broadcast([128, PH, 16]),
                                             op=ALU.subtract), ["P_sv"], ["P_ex"])
                A_(lambda e: e.activation(out=ex[:], in_=ex[:], func=AF.Exp), ["P_ex"], ["P_ex"])
                D_(lambda e: e.tensor_reduce(out=zs, in_=ex[:], axis=AX.X, op=ALU.add), ["P_ex"], ["P_zs"])
                D_(lambda e: e.reciprocal(out=rz, in_=zs), ["P_zs"], ["P_rz"])
                D_(lambda e: e.tensor_tensor(out=gate[:], in0=ex[:], in1=rz.unsqueeze(2).to_broadcast([128, PH, 16]),
                                             op=ALU.mult), ["P_ex", "P_rz"], [gatek])
                return ops

            def run_ops(ops, n):
                for _ in range(min(n, len(ops))):
                    ops.pop(0)()

            NSLOT = PH * TOPK
            cur = sel_ops(0, 0)
            run_ops(cur, len(cur))
            for i, t in enumerate(tiles):
                slot = i
                pb = i % 2
                samp, gi = tile_info(t)
                xk = "X%d" % slot
                xbt = xbb[pb]
                xbtk = xbk[pb]
                eidu = eidu2[pb]
                eiduk = "P_eidu%d" % pb
                gflat = gate2[pb][:].rearrange("p h k -> p (h k)")
                gatek = "P_gate%d" % pb
                nxt = sel_ops(i + 1, (i + 1) % 2) if i + 1 < len(tiles) else []
                per = (len(nxt) + NSLOT - 1) // NSLOT if nxt else 0
                for j in range(NSLOT):
                    ug = UG[j % NG]
                    ugk = "P_ug%d" % (j % NG)
                    d = dg[j % 4]
                    dk = "P_dg%d" % (j % 4)
                    ak = "P_ac%d" % j
                    gk = "P_gc%d" % j
                    ck = "P_cc%d" % j
                    T.dma("pool", lambda e, ug=ug, j=j, eidu=eidu: e.indirect_dma_start(
                        out=ug, out_offset=None, in_=puv,
                        in_offset=bass.IndirectOffsetOnAxis(ap=eidu[:, j:j + 1], axis=0)), "gat",
                        reads=[eiduk], writes=[ugk], npool=NG)
                    dve(lambda e, ug=ug, j=j, xbt=xbt: e.scalar_tensor_tensor(
                        out=ug[:, 0:D], in0=ug[:, 0:D], scalar=1.0, in1=xbt, op0=ALU.mult, op1=ALU.mult,
                        accum_out=actv[:, j:j + 1]), [ugk, xbtk], [ugk, ak])
                    act(lambda e, j=j: e.activation(out=gact[:, j:j + 1], in_=actv[:, j:j + 1], func=AF.Gelu),
                        [ak], [gk])
                    act(lambda e, j=j, gflat=gflat: e.activation(out=coef[:, j:j + 1], in_=gact[:, j:j + 1],
                                                                 func=AF.Copy, scale=gflat[:, j:j + 1]),
                        [gk, gatek], [ck])
                    act(lambda e, d=d, j=j: e.activation(out=d, in_=identb[:], func=AF.Copy, scale=coef[:, j:j + 1]),
                        ["identb", ck], [dk])
                    for half in range(2):
                        pe(lambda e, d=d, ug=ug, j=j, half=half: e.matmul(
                            PS[1 + half][:, :], lhsT=d, rhs=ug[:, D + half * 512:D + (half + 1) * 512],
                            start=(j == 0), stop=(j == NSLOT - 1)), [dk, ugk], ["ps%d" % (1 + half)])
                    if nxt and j >= 2:
                        run_ops(nxt, per)
                for half in range(2):
                    dve(lambda e, half=half, slot=slot: e.scalar_tensor_tensor(
                        out=y2[:, half * 512:(half + 1) * 512], in0=X[:, slot, half * 512:(half + 1) * 512],
                        scalar=float(ALPHA), in1=PS[1 + half][:, :], op0=ALU.mult, op1=ALU.add),
                        [xk, "ps%d" % (1 + half)], ["P_y2"])
                layer_norm_rows(y2, "P_y2", D, stats, mv, sd, rstd, "P_")
                dve(lambda e: e.tensor_scalar(out=y2, in0=y2, scalar1=mv[:, 0:1], scalar2=rstd[:, 0:1],
                                              op0=ALU.subtract, op1=ALU.mult), ["P_y2", "P_mv", "P_rstd"], ["P_y2"])
                dve(lambda e: e.tensor_tensor(out=y2, in0=y2, in1=g2, op=ALU.mult), ["P_y2", "P_g2"], ["P_y2"])
                dve(lambda e, slot=slot: e.tensor_tensor(out=X[:, slot, :], in0=y2, in1=b2, op=ALU.add),
                    ["P_y2", "P_b2"], [xk])
                if last_layer:
                    if samp:
                        stq(ys, X[0:SB * DEC_SEQ, slot, :], r=[xk])
                    else:
                        stq(yp[t * 128:(t + 1) * 128, :], X[:, slot, :], r=[xk])
                if nxt:
                    run_ops(nxt, len(nxt))

        for tiles in groups:
            for l in range(DEPTH):
                phase_a(l, tiles, l == 0)
                phase_b(l, tiles)
                phase_c1(l, tiles)
                phase_c2(l, tiles, l == DEPTH - 1)
        T.finish()
        T.replay()
        stats_ = dict(nins=T.nins, nsem=T.nsem, nprog={e: len(T.prog[e]) for e in T.NAMES})
    return nc, consts, stats_


_CACHE = {}


def kernel(**inputs):
    if "prog" not in _CACHE:
        _CACHE["prog"] = build_program()
    nc, consts, _ = _CACHE["prog"]
    f = lambda a: np.ascontiguousarray(np.asarray(a, dtype=np.float32))
    x_prompt = f(inputs["x_prompt"])
    x_sample = f(inputs["x_sample"])
    st = f(inputs["state_retention"])
    cc = f(inputs["cache_conv"])
    shared = {
        "w_in": f(inputs["w_in"]), "b_in": f(inputs["b_in"]), "dw_k": f(inputs["dw_kernel"]),
        "dw_b": f(inputs["dw_bias"]), "cln_g": f(inputs["conv_ln_g"]), "cln_b": f(inputs["conv_ln_b"]),
        "w_out": f(inputs["w_out"]), "ln1_g": f(inputs["ln1_g"]), "ln1_b": f(inputs["ln1_b"]),
        "w_q": f(inputs["w_query"]), "sk1": f(inputs["sub_keys_1"]), "sk2": f(inputs["sub_keys_2"]),
        "pu": f(inputs["peer_u"]), "pv": f(inputs["peer_v"]), "ln2_g": f(inputs["ln2_g"]),
        "ln2_b": f(inputs["ln2_b"]),
    }
    shared.update(consts)
    in_maps = []
    for c in range(N_CORES):
        m = dict(shared)
        m["xp"] = x_prompt[c]
        m["xs"] = np.ascontiguousarray(x_sample[c * SB:(c + 1) * SB].reshape(SB * DEC_SEQ, D))
        m["st"] = np.ascontiguousarray(st[:, c * SB:(c + 1) * SB])
        m["cc"] = np.ascontiguousarray(cc[:, c * SB:(c + 1) * SB])
        in_maps.append(m)
    res = run_bass_kernel_spmd(nc, in_maps, core_ids=list(range(N_CORES)))
    rs = res.results
    y_p = np.stack([np.asarray(r["yp"]) for r in rs], axis=0).astype(np.float32)
    y_s = np.concatenate([np.asarray(r["ys"]).reshape(SB, DEC_SEQ, D) for r in rs], axis=0).astype(np.float32)
    n_sp = np.stack([np.asarray(r["nsp"]) for r in rs], axis=1).astype(np.float32)
    n_cp = np.stack([np.asarray(r["ncp"]) for r in rs], axis=1).astype(np.float32)
    n_ss = np.concatenate([np.asarray(r["nss"]) for r in rs], axis=1).astype(np.float32)
    n_cs = np.concatenate([np.asarray(r["ncs"]) for r in rs], axis=1).astype(np.float32)
    return (y_p, y_s, n_sp, n_cp, n_ss, n_cs)
```

```python
import math
from contextlib import ExitStack

import numpy as np
import concourse.bass as bass
import concourse.mybir as mybir
from concourse.bass_utils import run_bass_kernel_spmd
from concourse.alu_op_type import AluOpType as ALU

F32 = mybir.dt.float32
BF16 = mybir.dt.bfloat16
I32 = mybir.dt.int32
U32 = mybir.dt.uint32
AF = mybir.ActivationFunctionType
AX = mybir.AxisListType

N_CORES = 8
D = 1024
SEQ = 2048
DEPTH = 2
SB = 16
DEC_SEQ = 4
PAST_LEN = 16384
RW = 512
CW = 512
H = 4
DK = 128
CONV_K = 31
INW = 3072
PH = 8
NKEYS = 128
NEXP = NKEYS * NKEYS
TOPK = 16
ALPHA = (2.0 * DEPTH) ** 0.25
LN_EPS = 1e-5
NPT = SEQ // 128
STILE = NPT
NGROUPS = 2
NUG = 6
import os
DBG_BAR = os.environ.get("DBG_BAR", "")
SEM_CAP = 30000


class Sem:
    def __init__(self, tr):
        self.tr = tr
        self.h = tr.new_hw_sem()
        self.val = 0

    def next(self, inc):
        if self.val + inc > SEM_CAP:
            self.h = self.tr.new_hw_sem()
            self.val = 0
        self.val += inc
        return (self.h, self.val)


class Tracker:
    NAMES = ["pe", "act", "dve", "pool", "sp"]

    def __init__(self, nc, stack):
        self.nc = nc
        self.stack = stack
        self.nsem = 0
        self.esem = {e: Sem(self) for e in self.NAMES}
        self.known = {e: {} for e in self.NAMES}
        self.prog = {e: [] for e in self.NAMES}
        self.last_tok = {e: None for e in self.NAMES}
        self.last_w = {}
        self.reads = {}
        self.all_dma = {}
        self.bg_dma = {}
        self.pools = {}
        self.nins = 0

    def new_hw_sem(self):
        self.nsem += 1
        return self.stack.enter_context(self.nc.semaphore("s%d" % self.nsem))

    def _wait(self, e, deps):
        kn = self.known[e]
        best = {}
        for (h, v) in deps:
            k = id(h)
            if k not in best or v > best[k][1]:
                best[k] = (h, v)
        for k, (h, v) in best.items():
            if kn.get(k, 0) >= v:
                continue
            self.prog[e].append(("wait", h, v))
            kn[k] = v

    def _deps(self, reads, writes):
        deps = []
        for r in reads:
            if r in self.last_w:
                deps.append(self.last_w[r])
        for w in writes:
            if w in self.last_w:
                deps.append(self.last_w[w])
            deps.extend(self.reads.get(w, []))
        return deps

    def _commit(self, tok, reads, writes):
        for w in writes:
            self.last_w[w] = tok
            self.reads[w] = []
        for r in reads:
            if r not in writes:
                self.reads.setdefault(r, []).append(tok)

    def op(self, e, fn, reads=(), writes=()):
        self._wait(e, self._deps(reads, writes))
        tok = self.esem[e].next(1)
        self.prog[e].append(("ins", fn, tok[0], 1))
        self._commit(tok, reads, writes)
        self.last_tok[e] = tok
        self.nins += 1
        return tok

    def dma(self, e, fn, pool, reads=(), writes=(), npool=4, bg=False):
        if pool not in self.pools:
            self.pools[pool] = [[Sem(self) for _ in range(npool)], 0]
        pl = self.pools[pool]
        sem = pl[0][pl[1] % len(pl[0])]
        pl[1] += 1
        deps = self._deps(reads, writes)
        if sem.val > 0:
            deps.append((sem.h, sem.val))
        self._wait(e, deps)
        tok = sem.next(16)
        self.prog[e].append(("ins", fn, tok[0], 16))
        self._commit(tok, reads, writes)
        if bg:
            self.bg_dma[id(tok[0])] = tok
        else:
            self.all_dma[id(tok[0])] = tok
        self.nins += 1
        return tok

    def join_bg(self):
        self.all_dma.update(self.bg_dma)
        self.bg_dma = {}

    def barrier(self):
        toks = [t for t in self.last_tok.values() if t is not None]
        toks += list(self.all_dma.values())
        for e in self.NAMES:
            self._wait(e, toks)

    def finish(self):
        self.barrier()

    def replay(self):
        nc = self.nc
        prog = self.prog

        def run(eng, lst):
            for a in lst:
                if a[0] == "wait":
                    eng.wait_ge(a[1], a[2])
                else:
                    ins = a[1](eng)
                    ins.then_inc(a[2], a[3])

        with nc.Block() as block:
            @block.tensor
            def _(eng):
                run(eng, prog["pe"])

            @block.scalar
            def _(eng):
                run(eng, prog["act"])

            @block.vector
            def _(eng):
                run(eng, prog["dve"])

            @block.gpsimd
            def _(eng):
                run(eng, prog["pool"])

            @block.sync
            def _(eng):
                run(eng, prog["sp"])


class Carver:
    def __init__(self, ov, nwords):
        self.ov = ov
        self.n = nwords
        self.off = 0

    def f32(self, n):
        assert self.off + n <= self.n, ("overlay overflow", self.off + n, self.n)
        v = self.ov[:, self.off:self.off + n]
        self.off += n
        return v

    def bf16(self, n):
        w = (n + 1) // 2
        assert self.off + w <= self.n, ("overlay overflow", self.off + w, self.n)
        v = self.ov[:, self.off:self.off + w].bitcast(BF16)
        self.off += w
        return v

    def u32(self, n):
        assert self.off + n <= self.n
        v = self.ov[:, self.off:self.off + n].bitcast(U32)
        self.off += n
        return v


def _const_tables():
    log_gamma = np.log(1.0 - 2.0 ** (-5.0 - np.arange(H, dtype=np.float32))).astype(np.float32)
    inv_freq = (np.float32(10000.0) ** (-np.arange(0, DK, 2, dtype=np.float32) / np.float32(DK))).astype(np.float32)

    def rope(pos):
        ang = pos.astype(np.float32)[:, None] * inv_freq[None, :]
        return np.cos(ang).astype(np.float32), np.sin(ang).astype(np.float32)

    ks = np.float32(DK ** -0.5)
    tab = np.zeros((NPT + 1, 128, 2, 256), np.float32)
    cp, sp_ = rope(np.arange(SEQ))
    for t in range(NPT):
        c = cp[t * 128:(t + 1) * 128]
        s = sp_[t * 128:(t + 1) * 128]
        tab[t, :, 0] = np.concatenate([c, c, -s, s], axis=1)
        tab[t, :, 1] = np.concatenate([c, c, -s, s], axis=1) * ks
    cs, ss = rope(PAST_LEN + np.arange(DEC_SEQ))
    rows = np.arange(128) % DEC_SEQ
    c = cs[rows]
    s = ss[rows]
    tab[NPT, :, 0] = np.concatenate([c, c, -s, s], axis=1)
    tab[NPT, :, 1] = np.concatenate([c, c, -s, s], axis=1) * ks

    lg = log_gamma.astype(np.float64)
    dm = np.zeros((2, 128, H, 128), np.float64)
    i = np.arange(128)
    rel = i[None, :] - i[:, None]
    for h in range(H):
        dm[0, :, h, :] = np.where(rel >= 0, np.exp(np.maximum(rel, 0) * lg[h]), 0.0)
        same = (i[None, :] // DEC_SEQ) == (i[:, None] // DEC_SEQ)
        rel4 = (i[None, :] % DEC_SEQ) - (i[:, None] % DEC_SEQ)
        dm[1, :, h, :] = np.where(same & (rel4 >= 0), np.exp(np.maximum(rel4, 0) * lg[h]), 0.0)
    din = np.zeros((128, 2, H), np.float64)
    dout = np.zeros((128, 2, H), np.float64)
    for h in range(H):
        din[:, 0, h] = np.exp((i + 1.0) * lg[h])
        dout[:, 0, h] = np.exp((128 - 1.0 - i) * lg[h])
        din[:, 1, h] = np.exp((i % DEC_SEQ + 1.0) * lg[h])
        dout[:, 1, h] = np.exp((DEC_SEQ - 1.0 - i % DEC_SEQ) * lg[h])
    cd = np.stack([np.exp(128 * lg), np.exp(DEC_SEQ * lg)])
    bmask = np.zeros((128, SB, 128), np.float32)
    for b in range(SB):
        bmask[:, b, b * DEC_SEQ:(b + 1) * DEC_SEQ] = 1.0
    rbm = np.zeros((128, SB), np.float32)
    for b in range(SB):
        rbm[b * DEC_SEQ:(b + 1) * DEC_SEQ, b] = 1.0
    iota16 = np.tile(np.arange(16, dtype=np.float32)[None, :], (128, 1))
    return dict(
        c_rope=tab,
        c_dm=np.ascontiguousarray(dm.transpose(1, 0, 2, 3)).astype(np.float32),
        c_din=din.astype(np.float32), c_dout=dout.astype(np.float32),
        c_bmask=bmask, c_rbm=rbm, c_iota=iota16,
        c_ident=np.eye(128, dtype=np.float32),
    ), cd.astype(np.float64)


def build_program():
    consts, cd = _const_tables()
    nc = bass.Bass("TRN2", target_bir_lowering=False)

    def din_(name, shape, dt=F32):
        return nc.dram_tensor(name, list(shape), dt, kind="ExternalInput").ap()

    def dout_(name, shape, dt=F32):
        return nc.dram_tensor(name, list(shape), dt, kind="ExternalOutput").ap()

    xp = din_("xp", [SEQ, D])
    xs = din_("xs", [SB * DEC_SEQ, D])
    st_in = din_("st", [DEPTH, SB, H, DK, DK])
    cc_in = din_("cc", [DEPTH, SB, CONV_K - 1, CW])
    w_in = din_("w_in", [DEPTH, D, INW])
    b_in = din_("b_in", [DEPTH, INW])
    dw_k = din_("dw_k", [DEPTH, CONV_K, CW])
    dw_b = din_("dw_b", [DEPTH, CW])
    cln_g = din_("cln_g", [DEPTH, CW])
    cln_b = din_("cln_b", [DEPTH, CW])
    w_out = din_("w_out", [DEPTH, D, D])
    ln1_g = din_("ln1_g", [DEPTH, D])
    ln1_b = din_("ln1_b", [DEPTH, D])
    w_q = din_("w_q", [DEPTH, D, 2 * D])
    sk1 = din_("sk1", [DEPTH, PH, NKEYS, 128])
    sk2 = din_("sk2", [DEPTH, PH, NKEYS, 128])
    pu = din_("pu", [DEPTH, NEXP, D])
    pv = din_("pv", [DEPTH, NEXP, D])
    ln2_g = din_("ln2_g", [DEPTH, D])
    pu_flat = pu.rearrange("l n d -> (l n) d")
    pv_flat = pv.rearrange("l n d -> (l n) d")
    ln2_b = din_("ln2_b", [DEPTH, D])
    c_rope = din_("c_rope", [NPT + 1, 128, 2, 256])
    c_dm = din_("c_dm", [128, 2, H, 128])
    c_din = din_("c_din", [128, 2, H])
    c_dout = din_("c_dout", [128, 2, H])
    c_bmask = din_("c_bmask", [128, SB, 128])
    c_rbm = din_("c_rbm", [128, SB])
    c_iota = din_("c_iota", [128, 16])
    c_ident = din_("c_ident", [128, 128])

    puv = nc.dram_tensor("puv", [DEPTH * NEXP, 2 * D], BF16, kind="Internal").ap()

    yp = dout_("yp", [SEQ, D])
    ys = dout_("ys", [SB * DEC_SEQ, D])
    nsp = dout_("nsp", [DEPTH, H, DK, DK])
    ncp = dout_("ncp", [DEPTH, CONV_K - 1, CW])
    nss = dout_("nss", [DEPTH, SB, H, DK, DK])
    ncs = dout_("ncs", [DEPTH, SB, CONV_K - 1, CW])

    per = NPT // NGROUPS
    groups = [list(range(g * per, (g + 1) * per)) for g in range(NGROUPS)]
    groups[-1] = groups[-1] + [STILE]
    MAXT = max(len(g) for g in groups)
    MAXP = per
    LMAX = MAXP * 128

    with ExitStack() as stk:
        T = Tracker(nc, stk)

        def sb(name, shape, dt=F32):
            return stk.enter_context(nc.sbuf_tensor(name, list(shape), dt))

        X = sb("X", [128, MAXT, D])
        W = sb("W", [128, 8 * INW], BF16)
        identf = sb("identf", [128, 128])
        identb = sb("identb", [128, 128], BF16)
        dmT = sb("dmT", [128, 2, H, 128])
        dinT = sb("dinT", [128, 2, H])
        doutT = sb("doutT", [128, 2, H])
        bmask = sb("bmask", [128, SB, 128], BF16)
        rbm = sb("rbm", [128, SB])
        iota16 = sb("iota16", [128, 16])
        R = sb("R", [128, DEPTH, H, 128])
        Rb = sb("Rb", [128, DEPTH, H, 128], BF16)
        UTAIL = sb("UTAIL", [128, DEPTH, 4, CONV_K - 1], BF16)
        xb = sb("xb", [128, D], BF16)
        xT = sb("xT", [128, 8, 128], BF16)
        UT = sb("UT", [128, 4, CONV_K - 1 + LMAX], BF16)
        UTS = sb("UTS", [128, 4, SB, CONV_K - 1 + DEC_SEQ], BF16)
        YT = sb("YT", [128, MAXT, CW], BF16)
        epsT = sb("epsT", [128, 1])
        OVN = 18432
        OV = sb("OV", [128, OVN])

        PS = [stk.enter_context(nc.psum_tensor("ps%d" % i, [128, 512], F32)) for i in range(8)]
        PSB = [p[:].bitcast(BF16) for p in PS]

        def dve(fn, r=(), w=()):
            return T.op("dve", fn, r, w)

        def act(fn, r=(), w=()):
            return T.op("act", fn, r, w)

        def pe(fn, r=(), w=()):
            return T.op("pe", fn, r, w)

        def pool(fn, r=(), w=()):
            return T.op("pool", fn, r, w)

        def ld(out, in_, r=(), w=(), pool_="ld"):
            return T.dma("sp", lambda e, out=out, in_=in_: e.dma_start(out=out, in_=in_), pool_, r, w)

        def stq(out, in_, r=(), w=(), pool_="st"):
            return T.dma("sp", lambda e, out=out, in_=in_: e.dma_start(out=out, in_=in_), pool_, r, w)

        def ldcast(out, in_, r=(), w=()):
            return T.dma("pool", lambda e, out=out, in_=in_: e.dma_start(out=out, in_=in_), "ldc", r, w, npool=2)

        ld(identf[:], c_ident, w=["identf"])
        ld(dmT[:], c_dm, w=["dmT"])
        ld(dinT[:], c_din, w=["dinT"])
        ld(doutT[:], c_dout, w=["doutT"])
        ldcast(bmask[:], c_bmask, w=["bmask"])
        ld(rbm[:], c_rbm, w=["rbm"])
        ld(iota16[:], c_iota, w=["iota16"])
        act(lambda e: e.activation(out=identb[:], in_=identf[:], func=AF.Copy), ["identf"], ["identb"])
        dve(lambda e: e.memset(R[:], 0.0), w=["R"])
        dve(lambda e: e.memset(Rb[:], 0.0), w=["Rb"])
        dve(lambda e: e.memset(UTAIL[:], 0.0), w=["UTAIL"])
        dve(lambda e: e.memset(epsT[:], LN_EPS), w=["epsT"])
        dve(lambda e: e.memset(YT[:], 0.0), w=["YT"])
        dve(lambda e: e.memset(X[:, MAXT - 1, :], 0.0), w=["X%d" % (MAXT - 1)])

        CV_ROWS = 1024
        puv_v = puv.rearrange("n (two d) -> n two d", two=2)
        cv_pieces = []
        for r0 in range(0, DEPTH * NEXP, CV_ROWS):
            for two, tsrc in enumerate((pu_flat, pv_flat)):
                cv_pieces.append((puv_v[r0:r0 + CV_ROWS, two, :], tsrc[r0:r0 + CV_ROWS, :]))

        def emit_cv(n):
            for _ in range(min(n, len(cv_pieces))):
                dst, src = cv_pieces.pop(0)
                T.dma("pool", lambda e, dst=dst, src=src: e.dma_start(out=dst, in_=src), "cvt", npool=4, bg=True)

        def tile_info(t):
            return (t == STILE), (1 if t == STILE else 0)

        def make_xT(slot, pref):
            xk = "X%d" % slot
            act(lambda e: e.activation(out=xb[:], in_=X[:, slot, :], func=AF.Copy), [xk], ["xb"])
            for k in range(8):
                pe(lambda e, k=k: e.transpose(out=PSB[6][:, k * 128:(k + 1) * 128], in_=xb[:, k * 128:(k + 1) * 128],
                                              identity=identb[:]), ["xb", "identb"], ["ps6"])
            dve(lambda e: e.tensor_copy(out=xT[:].rearrange("p k t -> p (k t)"), in_=PSB[6][:, :]), ["ps6"], ["xT"])

        def proj(bank, wview, col0, ncol=512):
            bk = "ps%d" % bank
            for k in range(8):
                pe(lambda e, k=k: e.matmul(PS[bank][:, 0:ncol], lhsT=xT[:, k, :], rhs=wview[:, k, col0:col0 + ncol],
                                           start=(k == 0), stop=(k == 7)), ["xT", "W"], [bk])

        def layer_norm_rows(src, srck, n, stats, mv, sd, rstd, pfx):
            nch = (n + 511) // 512
            for c in range(nch):
                dve(lambda e, c=c: e.bn_stats(out=stats[:, c, :], in_=src[:, c * 512:min(n, (c + 1) * 512)]),
                    [srck], [pfx + "stats"])
            dve(lambda e: e.bn_aggr(out=mv[:, :], in_=stats[:, 0:nch, :].rearrange("p c s -> p (c s)")),
                [pfx + "stats"], [pfx + "mv"])
            act(lambda e: e.activation(out=sd[:, :], in_=mv[:, 1:2], func=AF.Sqrt, bias=epsT[:, 0:1], scale=1.0),
                [pfx + "mv", "epsT"], [pfx + "sd"])
            dve(lambda e: e.reciprocal(out=rstd[:, :], in_=sd[:, :]), [pfx + "sd"], [pfx + "rstd"])

        def phase_a(l, tiles, first_layer):
            T.barrier()
            cv = Carver(OV, OVN)
            bcA = cv.f32(1024)
            t_cg = cv.f32(512)
            t_sig = cv.f32(512)
            t_ca = cv.f32(512)
            utile = [cv.f32(512), cv.f32(512)]
            ub = cv.bf16(512)
            cst = cv.f32(512)
            wA = W[:, 0:8 * 1024].rearrange("p (k e) -> p k e", k=8)
            ldcast(wA, w_in[l].rearrange("(k p) e -> p k e", p=128)[:, :, 2048:3072], w=["W"])
            ld(bcA, b_in[l, 2048:3072].partition_broadcast(128), w=["A_bc"])
            dve(lambda e: e.tensor_copy(out=UT[:, :, 0:CONV_K - 1], in_=UTAIL[:, l, :, :]), ["UTAIL"], ["UT"])
            ptiles = [t for t in tiles if t != STILE]
            for slot, t in enumerate(tiles):
                emit_cv(4)
                if "A" in DBG_BAR:
                    T.barrier()
                samp, _ = tile_info(t)
                xk = "X%d" % slot
                if first_layer:
                    if samp:
                        ld(X[0:SB * DEC_SEQ, slot, :], xs, w=[xk])
                    else:
                        ld(X[:, slot, :], xp[t * 128:(t + 1) * 128, :], w=[xk])
                make_xT(slot, "A")
                proj(0, wA, 0)
                proj(1, wA, 512)
                ut = utile[slot % 2]
                uk = "A_ut%d" % (slot % 2)
                dve(lambda e: e.tensor_tensor(out=t_cg, in0=PS[1][:, :], in1=bcA[:, 512:1024], op=ALU.add),
                    ["ps1", "A_bc"], ["A_cg"])
                act(lambda e: e.activation(out=t_sig, in_=t_cg, func=AF.Sigmoid), ["A_cg"], ["A_sig"])
                dve(lambda e: e.tensor_tensor(out=t_ca, in0=PS[0][:, :], in1=bcA[:, 0:512], op=ALU.add),
                    ["ps0", "A_bc"], ["A_ca"])
                dve(lambda e, ut=ut: e.tensor_tensor(out=ut, in0=t_ca, in1=t_sig, op=ALU.mult),
                    ["A_ca", "A_sig"], [uk])
                act(lambda e, ut=ut: e.activation(out=ub, in_=ut, func=AF.Copy), [uk], ["A_ub"])
                for c4 in range(4):
                    pe(lambda e, c4=c4: e.transpose(out=PSB[6][:, c4 * 128:(c4 + 1) * 128],
                                                    in_=ub[:, c4 * 128:(c4 + 1) * 128], identity=identb[:]),
                       ["A_ub", "identb"], ["ps6"])
                if not samp:
                    off = CONV_K - 1 + slot * 128
                    dve(lambda e, off=off: e.tensor_copy(
                        out=UT[:, :, off:off + 128],
                        in_=PSB[6][:, 0:512].rearrange("p (c t) -> p c t", c=4)), ["ps6"], ["UT"])
                    if t == NPT - 1:
                        stq(ncp[l], ut[128 - (CONV_K - 1):128, :], r=[uk])
                else:
                    for c4 in range(4):
                        dve(lambda e, c4=c4: e.tensor_copy(
                            out=UTS[:, c4, :, CONV_K - 1:CONV_K - 1 + DEC_SEQ],
                            in_=PSB[6][:, c4 * 128:c4 * 128 + SB * DEC_SEQ].rearrange("p (b r) -> p b r", r=DEC_SEQ)),
                            ["ps6"], ["UTS"])
                    for b in range(SB):
                        stq(ncs[l, b, CONV_K - 1 - DEC_SEQ:CONV_K - 1, :], ut[b * DEC_SEQ:(b + 1) * DEC_SEQ, :], r=[uk])
                    stq(ncs[l, :, 0:CONV_K - 1 - DEC_SEQ, :], cc_in[l, :, DEC_SEQ:CONV_K - 1, :])
                    nb = 4
                    rows = nb * (CONV_K - 1)
                    for g4 in range(SB // nb):
                        cstv = cst[0:rows, 0:512]
                        ld(cstv, cc_in[l, g4 * nb:(g4 + 1) * nb].rearrange("b r c -> (b r) c"), w=["A_cst"])
                        for c4 in range(4):
                            pe(lambda e, c4=c4: e.transpose(out=PS[7][:, c4 * 128:c4 * 128 + rows],
                                                            in_=cst[0:rows, c4 * 128:(c4 + 1) * 128],
                                                            identity=identf[0:rows, 0:rows]),
                               ["A_cst", "identf"], ["ps7"])
                        for c4 in range(4):
                            dve(lambda e, c4=c4, g4=g4: e.tensor_copy(
                                out=UTS[:, c4, g4 * nb:(g4 + 1) * nb, 0:CONV_K - 1],
                                in_=PS[7][:, c4 * 128:c4 * 128 + rows].rearrange("p (b r) -> p b r", r=CONV_K - 1)),
                                ["ps7"], ["UTS"])

        def phase_b(l, tiles):
            T.barrier()
            cv = Carver(OV, OVN)
            YCH = cv.f32(LMAX)
            YS = cv.f32(4 * SB * DEC_SEQ).rearrange("p (c b r) -> p c b r", c=4, b=SB)
            dwst = cv.f32(512)
            dwT = cv.f32(4 * 32).rearrange("p (c k) -> p c k", c=4)
            dwb = cv.f32(512)
            ptiles = [t for t in tiles if t != STILE]
            L = len(ptiles) * 128
            ld(dwst[0:CONV_K, :], dw_k[l], w=["B_dwst"])
            ld(dwb, dw_b[l].partition_broadcast(128), w=["B_dwb"])
            for c4 in range(4):
                pe(lambda e, c4=c4: e.transpose(out=PS[7][:, c4 * 32:c4 * 32 + CONV_K],
                                                in_=dwst[0:CONV_K, c4 * 128:(c4 + 1) * 128],
                                                identity=identf[0:CONV_K, 0:CONV_K]),
                   ["B_dwst", "identf"], ["ps7"])
            dve(lambda e: e.tensor_copy(out=dwT[:, :, 0:CONV_K],
                                        in_=PS[7][:, 0:128].rearrange("p (c k) -> p c k", c=4)[:, :, 0:CONV_K]),
                ["ps7"], ["B_dwT"])
            for c4 in range(4):
                for k in range(CONV_K):
                    if k == 0:
                        dve(lambda e, c4=c4: e.tensor_scalar(out=YCH[:, 0:L], in0=UT[:, c4, 0:L],
                                                             scalar1=dwT[:, c4, 0:1], scalar2=None, op0=ALU.mult),
                            ["UT", "B_dwT"], ["B_ych"])
                    else:
                        dve(lambda e, c4=c4, k=k: e.scalar_tensor_tensor(
                            out=YCH[:, 0:L], in0=UT[:, c4, k:k + L], scalar=dwT[:, c4, k:k + 1],
                            in1=YCH[:, 0:L], op0=ALU.mult, op1=ALU.add), ["UT", "B_dwT", "B_ych"], ["B_ych"])
                ntl = len(ptiles)
                for t0 in range(0, ntl, 4):
                    n4 = min(4, ntl - t0)
                    for i in range(n4):
                        pe(lambda e, i=i, t0=t0: e.transpose(out=PS[7][:, i * 128:(i + 1) * 128],
                                                             in_=YCH[:, (t0 + i) * 128:(t0 + i + 1) * 128],
                                                             identity=identf[:]), ["B_ych", "identf"], ["ps7"])
                    dve(lambda e, c4=c4, t0=t0, n4=n4: e.tensor_tensor(
                        out=YT[:, t0:t0 + n4, c4 * 128:(c4 + 1) * 128],
                        in0=PS[7][:, 0:n4 * 128].rearrange("p (t c) -> p t c", c=128),
                        in1=dwb[:, c4 * 128:(c4 + 1) * 128].unsqueeze(1).to_broadcast([128, n4, 128]),
                        op=ALU.add), ["ps7", "B_dwb"], ["YT"])
            dve(lambda e: e.tensor_copy(out=UTAIL[:, l, :, :], in_=UT[:, :, L:L + CONV_K - 1]), ["UT"], ["UTAIL"])
            if STILE in tiles:
                slot = tiles.index(STILE)
                for c4 in range(4):
                    for k in range(CONV_K):
                        if k == 0:
                            dve(lambda e, c4=c4: e.tensor_scalar(out=YS[:, c4, :, :], in0=UTS[:, c4, :, 0:DEC_SEQ],
                                                                 scalar1=dwT[:, c4, 0:1], scalar2=None, op0=ALU.mult),
                                ["UTS", "B_dwT"], ["B_ys"])
                        else:
                            dve(lambda e, c4=c4, k=k: e.scalar_tensor_tensor(
                                out=YS[:, c4, :, :], in0=UTS[:, c4, :, k:k + DEC_SEQ], scalar=dwT[:, c4, k:k + 1],
                                in1=YS[:, c4, :, :], op0=ALU.mult, op1=ALU.add), ["UTS", "B_dwT", "B_ys"], ["B_ys"])
                nst = SB * DEC_SEQ
                for c4 in range(4):
                    pe(lambda e, c4=c4: e.transpose(out=PS[7][0:nst, c4 * 128:(c4 + 1) * 128],
                                                    in_=YS[:, c4, :, :].rearrange("p b r -> p (b r)"),
                                                    identity=identf[:]), ["B_ys", "identf"], ["ps7"])
                dve(lambda e: e.tensor_tensor(out=YT[0:nst, slot, :], in0=PS[7][0:nst, :], in1=dwb[0:nst, :],
                                              op=ALU.add), ["ps7", "B_dwb"], ["YT"])

        def phase_c1(l, tiles):
            T.barrier()
            cv = Carver(OV, OVN)
            bc = cv.f32(2048)
            g1 = cv.f32(1024)
            b1 = cv.f32(1024)
            gc = cv.f32(512)
            bcn = cv.f32(512)
            rp = cv.f32(512).rearrange("p (a f) -> p a f", a=2)
            qf = cv.f32(512)
            kf = cv.f32(512)
            m1 = cv.f32(512)
            m2 = cv.f32(512)
            of = cv.f32(512)
            gs = cv.f32(512)
            y1 = cv.f32(1024)
            stats = cv.f32(24).rearrange("p (c s) -> p c s", s=6)
            mv = cv.f32(8).rearrange("p (h s) -> p h s", s=2)
            sd = cv.f32(4)
            rstd = cv.f32(4)
            qr = cv.bf16(512)
            kr = cv.bf16(512)
            qTt = cv.bf16(512).rearrange("p (h t) -> p h t", h=4)
            kTt = cv.bf16(512).rearrange("p (h t) -> p h t", h=4)
            vb = cv.bf16(512).rearrange("p (h e) -> p h e", h=4)
            vdec = cv.bf16(512).rearrange("p (h e) -> p h e", h=4)
            STt = cv.bf16(512).rearrange("p (h t) -> p h t", h=4)
            cat = cv.bf16(1024)
            catT = cv.bf16(1024).rearrange("p (k t) -> p k t", k=8)
            R0f = cv.f32(SB * 128).rearrange("p (b e) -> p b e", b=SB)
            R0b = cv.bf16(SB * 128).rearrange("p (b e) -> p b e", b=SB)
            qTx = cv.bf16(SB * 128).rearrange("p (b t) -> p b t", b=SB)
            rbig = cv.bf16(SB * 128).rearrange("p (b e) -> p b e", b=SB)

            wq4 = W[:, 0:8 * 2048].rearrange("p (k e) -> p k e", k=8)
            wo = W[:, 8 * 2048:8 * 3072].rearrange("p (k e) -> p k e", k=8)
            ldcast(wq4, w_in[l].rearrange("(k p) e -> p k e", p=128)[:, :, 0:2048], w=["W"])
            ldcast(wo, w_out[l].rearrange("(k p) e -> p k e", p=128), w=["W"])
            ld(bc, b_in[l, 0:2048].partition_broadcast(128), w=["C_bc"])
            ld(g1, ln1_g[l].partition_broadcast(128), w=["C_g1"])
            ld(b1, ln1_b[l].partition_broadcast(128), w=["C_b1"])
            ld(gc, cln_g[l].partition_broadcast(128), w=["C_gc"])
            ld(bcn, cln_b[l].partition_broadcast(128), w=["C_bcn"])

            for slot, t in enumerate(tiles):
                emit_cv(4)
                if "C" in DBG_BAR:
                    T.barrier()
                samp, gi = tile_info(t)
                xk = "X%d" % slot
                ld(rp[:].rearrange("p a f -> p (a f)"), c_rope[t].rearrange("p a f -> p (a f)"), w=["C_rp"])
                make_xT(slot, "C")

                def rope_pair(bank, boff, dstf, a, dst_bf, dk):
                    bk = "ps%d" % bank
                    dve(lambda e: e.tensor_tensor(out=dstf, in0=PS[bank][:, :], in1=bc[:, boff:boff + 512],
                                                  op=ALU.add), [bk, "C_bc"], [dk])
                    x4 = dstf.rearrange("p (h f) -> p h f", h=4)
                    m14 = m1.rearrange("p (h f) -> p h f", h=4)
                    m24 = m2.rearrange("p (h f) -> p h f", h=4)
                    dve(lambda e: e.tensor_tensor(out=m14, in0=x4,
                                                  in1=rp[:, a, 0:128].unsqueeze(1).to_broadcast([128, 4, 128]),
                                                  op=ALU.mult), [dk, "C_rp"], ["C_m1"])
                    dve(lambda e: e.tensor_tensor(out=m24[:, :, 0:64], in0=x4[:, :, 64:128],
                                                  in1=rp[:, a, 128:192].unsqueeze(1).to_broadcast([128, 4, 64]),
                                                  op=ALU.mult), [dk, "C_rp"], ["C_m2"])
                    dve(lambda e: e.tensor_tensor(out=m24[:, :, 64:128], in0=x4[:, :, 0:64],
                                                  in1=rp[:, a, 192:256].unsqueeze(1).to_broadcast([128, 4, 64]),
                                                  op=ALU.mult), [dk, "C_rp", "C_m2"], ["C_m2"])
                    dve(lambda e: e.tensor_tensor(out=dst_bf, in0=m1, in1=m2, op=ALU.add), ["C_m1", "C_m2"],
                        [dk + "r"])

                proj(0, wq4, 0)
                proj(1, wq4, 512)
                rope_pair(0, 0, qf, 0, qr, "C_qf")
                rope_pair(1, 512, kf, 1, kr, "C_kf")
                proj(0, wq4, 1024)
                proj(1, wq4, 1536)
                dve(lambda e: e.tensor_tensor(out=qf, in0=PS[0][:, :], in1=bc[:, 1024:1536], op=ALU.add),
                    ["ps0", "C_bc"], ["C_qf"])
                act(lambda e: e.activation(out=vb[:].rearrange("p h e -> p (h e)"), in_=qf, func=AF.Copy),
                    ["C_qf"], ["C_vb"])
                dve(lambda e, gi=gi: e.tensor_tensor(out=vdec[:], in0=qf.rearrange("p (h e) -> p h e", h=4),
                                                     in1=doutT[:, gi, :].unsqueeze(2).to_broadcast([128, 4, 128]),
                                                     op=ALU.mult), ["C_qf", "doutT"], ["C_vdec"])
                dve(lambda e: e.tensor_tensor(out=kf, in0=PS[1][:, :], in1=bc[:, 1536:2048], op=ALU.add),
                    ["ps1", "C_bc"], ["C_kf"])
                act(lambda e: e.activation(out=gs, in_=kf, func=AF.Silu), ["C_kf"], ["C_gs"])
                for h in range(4):
                    pe(lambda e, h=h: e.transpose(out=PSB[6][:, h * 128:(h + 1) * 128], in_=qr[:, h * 128:(h + 1) * 128],
                                                  identity=identb[:]), ["C_qfr", "identb"], ["ps6"])
                for h in range(4):
                    pe(lambda e, h=h: e.transpose(out=PSB[6][:, 512 + h * 128:512 + (h + 1) * 128],
                                                  in_=kr[:, h * 128:(h + 1) * 128], identity=identb[:]),
                       ["C_kfr", "identb"], ["ps6"])
                act(lambda e: e.activation(out=qTt[:].rearrange("p h t -> p (h t)"), in_=PSB[6][:, 0:512], func=AF.Copy),
                    ["ps6"], ["C_qT"])
                act(lambda e: e.activation(out=kTt[:].rearrange("p h t -> p (h t)"), in_=PSB[6][:, 512:1024],
                                           func=AF.Copy), ["ps6"], ["C_kT"])
                for h in range(4):
                    pe(lambda e, h=h: e.matmul(PS[2][:, h * 128:(h + 1) * 128], lhsT=kTt[:, h, :], rhs=qTt[:, h, :],
                                               start=True, stop=True), ["C_kT", "C_qT"], ["ps2"])
                dve(lambda e, gi=gi: e.tensor_tensor(out=STt[:].rearrange("p h t -> p (h t)"), in0=PS[2][:, :],
                                                     in1=dmT[:, gi, :, :].rearrange("p h t -> p (h t)"), op=ALU.mult),
                    ["ps2", "dmT"], ["C_ST"])
                for h in range(4):
                    pe(lambda e, h=h: e.matmul(PS[3][:, h * 128:(h + 1) * 128], lhsT=STt[:, h, :], rhs=vb[:, h, :],
                                               start=True, stop=True), ["C_ST", "C_vb"], ["ps3"])
                if not samp:
                    for h in range(4):
                        pe(lambda e, h=h: e.matmul(PS[4][:, h * 128:(h + 1) * 128], lhsT=qTt[:, h, :],
                                                   rhs=Rb[:, l, h, :], start=True, stop=True), ["C_qT", "Rb"], ["ps4"])
                    for h in range(4):
                        pe(lambda e, h=h: e.matmul(PS[5][:, h * 128:(h + 1) * 128], lhsT=kr[:, h * 128:(h + 1) * 128],
                                                   rhs=vdec[:, h, :], start=True, stop=True), ["C_kfr", "C_vdec"],
                           ["ps5"])
                    for h in range(4):
                        dve(lambda e, h=h: e.scalar_tensor_tensor(out=R[:, l, h, :], in0=R[:, l, h, :],
                                                                  scalar=float(cd[0, h]),
                                                                  in1=PS[5][:, h * 128:(h + 1) * 128],
                                                                  op0=ALU.mult, op1=ALU.add), ["R", "ps5"], ["R"])
                    act(lambda e: e.activation(out=Rb[:, l, :, :], in_=R[:, l, :, :], func=AF.Copy), ["R"], ["Rb"])
                    if t == NPT - 1:
                        stq(nsp[l].rearrange("h d e -> d h e"), R[:, l, :, :], r=["R"])
                else:
                    for h in range(4):
                        ld(R0f[:], st_in[l, :, h].rearrange("b d e -> d b e"), w=["C_R0f"])
                        act(lambda e: e.activation(out=R0b[:], in_=R0f[:], func=AF.Copy), ["C_R0f"], ["C_R0b"])
                        dve(lambda e, h=h: e.tensor_tensor(out=qTx[:],
                                                           in0=qTt[:, h, :].unsqueeze(1).to_broadcast([128, SB, 128]),
                                                           in1=bmask[:], op=ALU.mult), ["C_qT", "bmask"], ["C_qTx"])
                        for b in range(SB):
                            pe(lambda e, h=h, b=b: e.matmul(PS[4][:, h * 128:(h + 1) * 128], lhsT=qTx[:, b, :],
                                                            rhs=R0b[:, b, :], start=(b == 0), stop=(b == SB - 1)),
                               ["C_qTx", "C_R0b"], ["ps4"])
                        dve(lambda e, h=h: e.tensor_tensor(out=rbig[:],
                                                           in0=vdec[:, h, :].unsqueeze(1).to_broadcast([128, SB, 128]),
                                                           in1=rbm[:, :].unsqueeze(2).to_broadcast([128, SB, 128]),
                                                           op=ALU.mult), ["C_vdec", "rbm"], ["C_rbig"])
                        for g4 in range(SB // 4):
                            pe(lambda e, h=h, g4=g4: e.matmul(
                                PS[5][:, :], lhsT=kr[:, h * 128:(h + 1) * 128],
                                rhs=rbig[:, g4 * 4:(g4 + 1) * 4, :].rearrange("p b e -> p (b e)"),
                                start=True, stop=True), ["C_kfr", "C_rbig"], ["ps5"])
                            dve(lambda e, h=h, g4=g4: e.scalar_tensor_tensor(
                                out=R0f[:, g4 * 4:(g4 + 1) * 4, :].rearrange("p b e -> p (b e)"),
                                in0=R0f[:, g4 * 4:(g4 + 1) * 4, :].rearrange("p b e -> p (b e)"),
                                scalar=float(cd[1, h]), in1=PS[5][:, :], op0=ALU.mult, op1=ALU.add),
                                ["C_R0f", "ps5"], ["C_R0f"])
                        stq(nss[l, :, h].rearrange("b d e -> d b e"), R0f[:], r=["C_R0f"])
                dve(lambda e, gi=gi: e.tensor_tensor(out=m1.rearrange("p (h e) -> p h e", h=4),
                                                     in0=PS[4][:, :].rearrange("p (h e) -> p h e", h=4),
                                                     in1=dinT[:, gi, :].unsqueeze(2).to_broadcast([128, 4, 128]),
                                                     op=ALU.mult), ["ps4", "dinT"], ["C_m1"])
                dve(lambda e: e.tensor_tensor(out=of, in0=PS[3][:, :], in1=m1, op=ALU.add), ["ps3", "C_m1"], ["C_of"])
                for h in range(4):
                    dve(lambda e, h=h: e.bn_stats(out=stats[:, h, :], in_=of[:, h * 128:(h + 1) * 128]),
                        ["C_of"], ["C_stats"])
                for h in range(4):
                    dve(lambda e, h=h: e.bn_aggr(out=mv[:, h, :], in_=stats[:, h, :]), ["C_stats"], ["C_mv"])
                act(lambda e: e.activation(out=sd[:, :], in_=mv[:, :, 1], func=AF.Sqrt, bias=epsT[:, 0:1], scale=1.0),
                    ["C_mv", "epsT"], ["C_sd"])
                dve(lambda e: e.reciprocal(out=rstd[:, :], in_=sd[:, :]), ["C_sd"], ["C_rstd"])
                of4 = of.rearrange("p (h e) -> p h e", h=4)
                dve(lambda e: e.tensor_tensor(out=of4, in0=of4, in1=mv[:, :, 0:1].to_broadcast([128, 4, 128]),
                                              op=ALU.subtract), ["C_of", "C_mv"], ["C_of"])
                dve(lambda e: e.tensor_tensor(out=of4, in0=of4, in1=rstd[:, :].unsqueeze(2).to_broadcast([128, 4, 128]),
                                              op=ALU.mult), ["C_of", "C_rstd"], ["C_of"])
                dve(lambda e: e.tensor_tensor(out=cat[:, 0:512], in0=of, in1=gs, op=ALU.mult), ["C_of", "C_gs"],
                    ["C_cat"])
                dve(lambda e, slot=slot: e.bn_stats(out=stats[:, 0, :], in_=YT[:, slot, :]), ["YT"], ["C_stats"])
                dve(lambda e: e.bn_aggr(out=mv[:, 0, :], in_=stats[:, 0, :]), ["C_stats"], ["C_mv"])
                act(lambda e: e.activation(out=sd[:, 0:1], in_=mv[:, 0, 1:2], func=AF.Sqrt, bias=epsT[:, 0:1], scale=1.0),
                    ["C_mv", "epsT"], ["C_sd"])
                dve(lambda e: e.reciprocal(out=rstd[:, 0:1], in_=sd[:, 0:1]), ["C_sd"], ["C_rstd"])
                dve(lambda e, slot=slot: e.tensor_scalar(out=m2, in0=YT[:, slot, :], scalar1=mv[:, 0, 0:1],
                                                         scalar2=rstd[:, 0:1], op0=ALU.subtract, op1=ALU.mult),
                    ["YT", "C_mv", "C_rstd"], ["C_m2"])
                dve(lambda e: e.tensor_tensor(out=m2, in0=m2, in1=gc, op=ALU.mult), ["C_m2", "C_gc"], ["C_m2"])
                dve(lambda e: e.tensor_tensor(out=m2, in0=m2, in1=bcn, op=ALU.add), ["C_m2", "C_bcn"], ["C_m2"])
                act(lambda e: e.activation(out=cat[:, 512:1024], in_=m2, func=AF.Silu), ["C_m2"], ["C_cat"])
                for k in range(8):
                    pe(lambda e, k=k: e.transpose(out=PSB[6][:, k * 128:(k + 1) * 128], in_=cat[:, k * 128:(k + 1) * 128],
                                                  identity=identb[:]), ["C_cat", "identb"], ["ps6"])
                act(lambda e: e.activation(out=catT[:].rearrange("p k t -> p (k t)"), in_=PSB[6][:, :], func=AF.Copy),
                    ["ps6"], ["C_catT"])
                for half in range(2):
                    bk = "ps%d" % half
                    for k in range(8):
                        pe(lambda e, k=k, half=half: e.matmul(PS[half][:, :], lhsT=catT[:, k, :],
                                                              rhs=wo[:, k, half * 512:(half + 1) * 512],
                                                              start=(k == 0), stop=(k == 7)), ["C_catT", "W"], [bk])
                    dve(lambda e, half=half, slot=slot: e.scalar_tensor_tensor(
                        out=y1[:, half * 512:(half + 1) * 512], in0=X[:, slot, half * 512:(half + 1) * 512],
                        scalar=float(ALPHA), in1=PS[half][:, :], op0=ALU.mult, op1=ALU.add), [xk, bk], ["C_y1"])
                layer_norm_rows(y1, "C_y1", D, stats, mv[:, 0, :], sd[:, 0:1], rstd[:, 0:1], "C_")
                dve(lambda e: e.tensor_scalar(out=y1, in0=y1, scalar1=mv[:, 0, 0:1], scalar2=rstd[:, 0:1],
                                              op0=ALU.subtract, op1=ALU.mult), ["C_y1", "C_mv", "C_rstd"], ["C_y1"])
                dve(lambda e: e.tensor_tensor(out=y1, in0=y1, in1=g1, op=ALU.mult), ["C_y1", "C_g1"], ["C_y1"])
                dve(lambda e, slot=slot: e.tensor_tensor(out=X[:, slot, :], in0=y1, in1=b1, op=ALU.add),
                    ["C_y1", "C_b1"], [xk])

        def phase_c2(l, tiles, last_layer):
            emit_cv(len(cv_pieces))
            T.join_bg()
            T.barrier()
            cv = Carver(OV, OVN)
            A8 = cv.f32(2048)
            B8 = cv.f32(2048)
            C8 = cv.f32(2048)
            UTflat = UT[:].rearrange("p c n -> p (c n)")
            UTSflat = UTS[:].rearrange("p c b n -> p (c b n)")
            YTflat = YT[:].rearrange("p t c -> p (t c)")
            UG = [cv.bf16(2048) for _ in range(4)] + [UTflat[:, 0:2048], UTflat[:, 2048:4096], UTSflat[:, 0:2048]]
            UG += [W[:, 8 * 2048 + k * 2048:8 * 2048 + (k + 1) * 2048] for k in range(4)]
            NG = len(UG)
            y2 = YTflat[:, 0:2048].bitcast(F32)
            junk = YTflat[:, 2048:3072]
            xbb = [xb[:, :], YTflat[:, 3072:4096]]
            xbk = ["xb", "P_xb2"]
            gact = cv.f32(128)
            g2 = cv.f32(1024)
            b2 = cv.f32(1024)
            vals = cv.f32(256).rearrange("p (h s k) -> p h s k", h=PH, s=2)
            idxu = cv.u32(256).rearrange("p (h s k) -> p h s k", h=PH, s=2)
            idxf = cv.f32(256).rearrange("p (h s k) -> p h s k", h=PH, s=2)
            sv = cv.f32(128).rearrange("p (h k) -> p h k", h=PH)
            ciu = cv.u32(128).rearrange("p (h k) -> p h k", h=PH)
            a16u = cv.u32(128).rearrange("p (h k) -> p h k", h=PH)
            b16u = cv.u32(128).rearrange("p (h k) -> p h k", h=PH)
            a16f = cv.f32(128).rearrange("p (h k) -> p h k", h=PH)
            b16f = cv.f32(128).rearrange("p (h k) -> p h k", h=PH)
            i1s = cv.f32(128).rearrange("p (h k) -> p h k", h=PH)
            i2s = cv.f32(128).rearrange("p (h k) -> p h k", h=PH)
            eidf = cv.f32(128)
            eidu2 = [cv.u32(128), cv.u32(128)]
            ex = cv.f32(128).rearrange("p (h k) -> p h k", h=PH)
            zs = cv.f32(8)
            rz = cv.f32(8)
            gate2 = [cv.f32(128).rearrange("p (h k) -> p h k", h=PH), cv.f32(128).rearrange("p (h k) -> p h k", h=PH)]
            actv = cv.f32(128)
            coef = cv.f32(128)
            dg = [cv.bf16(128) for _ in range(4)]
            stats = cv.f32(12).rearrange("p (c s) -> p c s", s=6)
            mv = cv.f32(2)
            sd = cv.f32(1)
            rstd = cv.f32(1)
            qT16 = cv.bf16(2048).rearrange("p (c t) -> p c t", c=16)
            keysT = cv.bf16(2048).rearrange("p (c t) -> p c t", c=16)
            kst = cv.bf16(1024)

            wqv = W[:, 0:8 * 2048].rearrange("p (k e) -> p k e", k=8)
            ldcast(wqv, w_q[l].rearrange("(k p) e -> p k e", p=128), w=["W"])
            ld(g2, ln2_g[l].partition_broadcast(128), w=["P_g2"])
            ld(b2, ln2_b[l].partition_broadcast(128), w=["P_b2"])
            for side, skd in enumerate((sk1, sk2)):
                ld(A8[:, 0:PH * 128].rearrange("p (h d) -> p h d", h=PH), skd[l].rearrange("h k d -> k h d"), w=["P_A8"])
                act(lambda e: e.activation(out=kst, in_=A8[:, 0:PH * 128], func=AF.Copy), ["P_A8"], ["P_kst"])
                for h in range(PH):
                    pe(lambda e, h=h: e.transpose(out=PSB[6][:, h * 128:(h + 1) * 128], in_=kst[:, h * 128:(h + 1) * 128],
                                                  identity=identb[:]), ["P_kst", "identb"], ["ps6"])
                for h in range(PH):
                    dve(lambda e, h=h, side=side: e.tensor_copy(out=keysT[:, 2 * h + side, :],
                                                                in_=PSB[6][:, h * 128:(h + 1) * 128]),
                        ["ps6"], ["P_keysT"])

            QBANK = [0, 3, 0, 3]
            SBANK = [4, 5, 7, 4]

            def sel_ops(slot, pb):
                ops = []

                def D_(fn, r=(), w=()):
                    ops.append(lambda: T.op("dve", fn, r, w))

                def A_(fn, r=(), w=()):
                    ops.append(lambda: T.op("act", fn, r, w))

                def P_(fn, r=(), w=()):
                    ops.append(lambda: T.op("pe", fn, r, w))

                xk = "X%d" % slot
                xbt = xbb[pb]
                xbtk = xbk[pb]
                eidu = eidu2[pb]
                eiduk = "P_eidu%d" % pb
                gate = gate2[pb]
                gatek = "P_gate%d" % pb
                A_(lambda e: e.activation(out=xbt, in_=X[:, slot, :], func=AF.Copy), [xk], [xbtk])
                for k in range(8):
                    P_(lambda e, k=k: e.transpose(out=PSB[6][:, k * 128:(k + 1) * 128], in_=xbt[:, k * 128:(k + 1) * 128],
                                                  identity=identb[:]), [xbtk, "identb"], ["ps6"])
                D_(lambda e: e.tensor_copy(out=xT[:].rearrange("p k t -> p (k t)"), in_=PSB[6][:, :]), ["ps6"], ["xT"])
                for q4 in range(4):
                    bank = QBANK[q4]
                    bk = "ps%d" % bank
                    for c in range(q4 * 4, q4 * 4 + 4):
                        for k in range(8):
                            P_(lambda e, c=c, k=k, bank=bank: e.matmul(
                                PS[bank][:, (c % 4) * 128:(c % 4 + 1) * 128], lhsT=wqv[:, k, c * 128:(c + 1) * 128],
                                rhs=xT[:, k, :], start=(k == 0), stop=(k == 7)), ["xT", "W"], [bk])
                    A_(lambda e, q4=q4, bank=bank: e.activation(
                        out=qT16[:, q4 * 4:(q4 + 1) * 4, :].rearrange("p c t -> p (c t)"), in_=PS[bank][:, :],
                        func=AF.Copy), [bk], ["P_qT16"])
                Sv = A8.rearrange("p (c k) -> p c k", c=16)
                S2v = B8.rearrange("p (c k) -> p c k", c=16)
                for q4 in range(4):
                    bank = SBANK[q4]
                    bk = "ps%d" % bank
                    for c in range(q4 * 4, q4 * 4 + 4):
                        P_(lambda e, c=c, bank=bank: e.matmul(PS[bank][:, (c % 4) * 128:(c % 4 + 1) * 128],
                                                              lhsT=qT16[:, c, :], rhs=keysT[:, c, :], start=True,
                                                              stop=True), ["P_qT16", "P_keysT"], [bk])
                    A_(lambda e, q4=q4, bank=bank: e.activation(out=A8[:, q4 * 512:(q4 + 1) * 512], in_=PS[bank][:, :],
                                                                func=AF.Copy), [bk], ["P_A8"])
                for c in range(16):
                    h, s_ = c // 2, c % 2
                    D_(lambda e, c=c, h=h, s_=s_: e.max(out=vals[:, h, s_, 0:8], in_=Sv[:, c, :]), ["P_A8"], ["P_vals"])
                    D_(lambda e, c=c, h=h, s_=s_: e.max_index(out=idxu[:, h, s_, 0:8], in_max=vals[:, h, s_, 0:8],
                                                              in_values=Sv[:, c, :]), ["P_A8", "P_vals"], ["P_idxu"])
                    D_(lambda e, c=c, h=h, s_=s_: e.match_replace(out=S2v[:, c, :], in_to_replace=vals[:, h, s_, 0:8],
                                                                  in_values=Sv[:, c, :], imm_value=-1e30),
                       ["P_A8", "P_vals"], ["P_B8"])
                    D_(lambda e, c=c, h=h, s_=s_: e.max(out=vals[:, h, s_, 8:16], in_=S2v[:, c, :]), ["P_B8"], ["P_vals"])
                    D_(lambda e, c=c, h=h, s_=s_: e.max_index(out=idxu[:, h, s_, 8:16], in_max=vals[:, h, s_, 8:16],
                                                              in_values=S2v[:, c, :]), ["P_B8", "P_vals"], ["P_idxu"])
                D_(lambda e: e.tensor_copy(out=idxf[:], in_=idxu[:]), ["P_idxu"], ["P_idxf"])
                cand = A8.rearrange("p (h a b) -> p h a b", h=PH, a=16)
                D_(lambda e: e.tensor_tensor(out=cand, in0=vals[:, :, 0, :].unsqueeze(3).to_broadcast([128, PH, 16, 16]),
                                             in1=vals[:, :, 1, :].unsqueeze(2).to_broadcast([128, PH, 16, 16]),
                                             op=ALU.add), ["P_vals"], ["P_A8"])
                cf = A8.rearrange("p (h n) -> p h n", h=PH)
                cf2 = B8.rearrange("p (h n) -> p h n", h=PH)
                for h in range(PH):
                    D_(lambda e, h=h: e.max(out=sv[:, h, 0:8], in_=cf[:, h, :]), ["P_A8"], ["P_sv"])
                    D_(lambda e, h=h: e.max_index(out=ciu[:, h, 0:8], in_max=sv[:, h, 0:8], in_values=cf[:, h, :]),
                       ["P_A8", "P_sv"], ["P_ciu"])
                    D_(lambda e, h=h: e.match_replace(out=cf2[:, h, :], in_to_replace=sv[:, h, 0:8],
                                                      in_values=cf[:, h, :], imm_value=-1e30), ["P_A8", "P_sv"], ["P_B8"])
                    D_(lambda e, h=h: e.max(out=sv[:, h, 8:16], in_=cf2[:, h, :]), ["P_B8"], ["P_sv"])
                    D_(lambda e, h=h: e.max_index(out=ciu[:, h, 8:16], in_max=sv[:, h, 8:16], in_values=cf2[:, h, :]),
                       ["P_B8", "P_sv"], ["P_ciu"])
                D_(lambda e: e.tensor_single_scalar(out=a16u[:], in_=ciu[:], scalar=4, op=ALU.logical_shift_right),
                   ["P_ciu"], ["P_a16u"])
                D_(lambda e: e.tensor_single_scalar(out=b16u[:], in_=ciu[:], scalar=15, op=ALU.bitwise_and),
                   ["P_ciu"], ["P_b16u"])
                D_(lambda e: e.tensor_copy(out=a16f[:], in_=a16u[:]), ["P_a16u"], ["P_a16f"])
                D_(lambda e: e.tensor_copy(out=b16f[:], in_=b16u[:]), ["P_b16u"], ["P_b16f"])
                eq = C8.rearrange("p (h k a) -> p h k a", h=PH, k=16)
                io_b = iota16[:, :].unsqueeze(1).unsqueeze(1).to_broadcast([128, PH, 16, 16])
                for (pf, sidx, dst, nm) in ((a16f, 0, i1s, "P_i1s"), (b16f, 1, i2s, "P_i2s")):
                    D_(lambda e, pf=pf: e.tensor_tensor(out=eq, in0=io_b,
                                                        in1=pf[:].unsqueeze(3).to_broadcast([128, PH, 16, 16]),
                                                        op=ALU.is_equal), ["iota16", "P_a16f", "P_b16f"], ["P_C8"])
                    D_(lambda e, sidx=sidx: e.tensor_tensor(
                        out=eq, in0=eq, in1=idxf[:, :, sidx, :].unsqueeze(2).to_broadcast([128, PH, 16, 16]),
                        op=ALU.mult), ["P_C8", "P_idxf"], ["P_C8"])
                    D_(lambda e, dst=dst: e.tensor_reduce(out=dst[:], in_=eq, axis=AX.X, op=ALU.add), ["P_C8"], [nm])
                D_(lambda e: e.scalar_tensor_tensor(out=eidf, in0=i1s[:].rearrange("p h k -> p (h k)"),
                                                    scalar=float(NKEYS), in1=i2s[:].rearrange("p h k -> p (h k)"),
                                                    op0=ALU.mult, op1=ALU.add), ["P_i1s", "P_i2s"], ["P_eidf"])
                if l > 0:
                    D_(lambda e: e.tensor_scalar(out=eidf, in0=eidf, scalar1=float(l * NEXP), scalar2=None,
                                                 op0=ALU.add), ["P_eidf"], ["P_eidf"])
                D_(lambda e: e.tensor_copy(out=eidu, in_=eidf), ["P_eidf"], [eiduk])
                D_(lambda e: e.tensor_tensor(out=ex[:], in0=sv[:], in1=sv[:, :, 0:1].to_broadcast([128, PH, 16]),
                                             op=ALU.subtract), ["P_sv"], ["P_ex"])
                A_(lambda e: e.activation(out=ex[:], in_=ex[:], func=AF.Exp), ["P_ex"], ["P_ex"])
                D_(lambda e: e.tensor_reduce(out=zs, in_=ex[:], axis=AX.X, op=ALU.add), ["P_ex"], ["P_zs"])
                D_(lambda e: e.reciprocal(out=rz, in_=zs), ["P_zs"], ["P_rz"])
                D_(lambda e: e.tensor_tensor(out=gate[:], in0=ex[:], in1=rz.unsqueeze(2).to_broadcast([128, PH, 16]),
                                             op=ALU.mult), ["P_ex", "P_rz"], [gatek])
                return ops

            def run_ops(ops, n):
                for _ in range(min(n, len(ops))):
                    ops.pop(0)()

            NSLOT = PH * TOPK
            cur = sel_ops(0, 0)
            run_ops(cur, len(cur))
            for i, t in enumerate(tiles):
                slot = i
                pb = i % 2
                samp, gi = tile_info(t)
                xk = "X%d" % slot
                xbt = xbb[pb]
                xbtk = xbk[pb]
                eidu = eidu2[pb]
                eiduk = "P_eidu%d" % pb
                gflat = gate2[pb][:].rearrange("p h k -> p (h k)")
                gatek = "P_gate%d" % pb
                nxt = sel_ops(i + 1, (i + 1) % 2) if i + 1 < len(tiles) else []
                per = (len(nxt) + NSLOT - 1) // NSLOT if nxt else 0
                for j in range(NSLOT):
                    ug = UG[j % NG]
                    ugk = "P_ug%d" % (j % NG)
                    d = dg[j % 4]
                    dk = "P_dg%d" % (j % 4)
                    ak = "P_ac%d" % j
                    gk = "P_gc%d" % j
                    ck = "P_cc%d" % j
                    T.dma("pool", lambda e, ug=ug, j=j, eidu=eidu: e.indirect_dma_start(
                        out=ug, out_offset=None, in_=puv,
                        in_offset=bass.IndirectOffsetOnAxis(ap=eidu[:, j:j + 1], axis=0)), "gat",
                        reads=[eiduk], writes=[ugk], npool=NG)
                    dve(lambda e, ug=ug, j=j, xbt=xbt: e.scalar_tensor_tensor(
                        out=ug[:, 0:D], in0=ug[:, 0:D], scalar=1.0, in1=xbt, op0=ALU.mult, op1=ALU.mult,
                        accum_out=actv[:, j:j + 1]), [ugk, xbtk], [ugk, ak])
                    act(lambda e, j=j: e.activation(out=gact[:, j:j + 1], in_=actv[:, j:j + 1], func=AF.Gelu),
                        [ak], [gk])
                    act(lambda e, j=j, gflat=gflat: e.activation(out=coef[:, j:j + 1], in_=gact[:, j:j + 1],
                                                                 func=AF.Copy, scale=gflat[:, j:j + 1]),
                        [gk, gatek], [ck])
                    act(lambda e, d=d, j=j: e.activation(out=d, in_=identb[:], func=AF.Copy, scale=coef[:, j:j + 1]),
                        ["identb", ck], [dk])
                    for half in range(2):
                        pe(lambda e, d=d, ug=ug, j=j, half=half: e.matmul(
                            PS[1 + half][:, :], lhsT=d, rhs=ug[:, D + half * 512:D + (half + 1) * 512],
                            start=(j == 0), stop=(j == NSLOT - 1)), [dk, ugk], ["ps%d" % (1 + half)])
                    if nxt and j >= 2:
                        run_ops(nxt, per)
                for half in range(2):
                    dve(lambda e, half=half, slot=slot: e.scalar_tensor_tensor(
                        out=y2[:, half * 512:(half + 1) * 512], in0=X[:, slot, half * 512:(half + 1) * 512],
                        scalar=float(ALPHA), in1=PS[1 + half][:, :], op0=ALU.mult, op1=ALU.add),
                        [xk, "ps%d" % (1 + half)], ["P_y2"])
                layer_norm_rows(y2, "P_y2", D, stats, mv, sd, rstd, "P_")
                dve(lambda e: e.tensor_scalar(out=y2, in0=y2, scalar1=mv[:, 0:1], scalar2=rstd[:, 0:1],
                                              op0=ALU.subtract, op1=ALU.mult), ["P_y2", "P_mv", "P_rstd"], ["P_y2"])
                dve(lambda e: e.tensor_tensor(out=y2, in0=y2, in1=g2, op=ALU.mult), ["P_y2", "P_g2"], ["P_y2"])
                dve(lambda e, slot=slot: e.tensor_tensor(out=X[:, slot, :], in0=y2, in1=b2, op=ALU.add),
                    ["P_y2", "P_b2"], [xk])
                if last_layer:
                    if samp:
                        stq(ys, X[0:SB * DEC_SEQ, slot, :], r=[xk])
                    else:
                        stq(yp[t * 128:(t + 1) * 128, :], X[:, slot, :], r=[xk])
                if nxt:
                    run_ops(nxt, len(nxt))

        for tiles in groups:
            for l in range(DEPTH):
                phase_a(l, tiles, l == 0)
                phase_b(l, tiles)
                phase_c1(l, tiles)
                phase_c2(l, tiles, l == DEPTH - 1)
        T.finish()
        T.replay()
        stats_ = dict(nins=T.nins, nsem=T.nsem, nprog={e: len(T.prog[e]) for e in T.NAMES})
    return nc, consts, stats_


_CACHE = {}


def kernel(**inputs):
    if "prog" not in _CACHE:
        _CACHE["prog"] = build_program()
    nc, consts, _ = _CACHE["prog"]
    f = lambda a: np.ascontiguousarray(np.asarray(a, dtype=np.float32))
    x_prompt = f(inputs["x_prompt"])
    x_sample = f(inputs["x_sample"])
    st = f(inputs["state_retention"])
    cc = f(inputs["cache_conv"])
    shared = {
        "w_in": f(inputs["w_in"]), "b_in": f(inputs["b_in"]), "dw_k": f(inputs["dw_kernel"]),
        "dw_b": f(inputs["dw_bias"]), "cln_g": f(inputs["conv_ln_g"]), "cln_b": f(inputs["conv_ln_b"]),
        "w_out": f(inputs["w_out"]), "ln1_g": f(inputs["ln1_g"]), "ln1_b": f(inputs["ln1_b"]),
        "w_q": f(inputs["w_query"]), "sk1": f(inputs["sub_keys_1"]), "sk2": f(inputs["sub_keys_2"]),
        "pu": f(inputs["peer_u"]), "pv": f(inputs["peer_v"]), "ln2_g": f(inputs["ln2_g"]),
        "ln2_b": f(inputs["ln2_b"]),
    }
    shared.update(consts)
    in_maps = []
    for c in range(N_CORES):
        m = dict(shared)
        m["xp"] = x_prompt[c]
        m["xs"] = np.ascontiguousarray(x_sample[c * SB:(c + 1) * SB].reshape(SB * DEC_SEQ, D))
        m["st"] = np.ascontiguousarray(st[:, c * SB:(c + 1) * SB])
        m["cc"] = np.ascontiguousarray(cc[:, c * SB:(c + 1) * SB])
        in_maps.append(m)
    res = run_bass_kernel_spmd(nc, in_maps, core_ids=list(range(N_CORES)))
    rs = res.results
    y_p = np.stack([np.asarray(r["yp"]) for r in rs], axis=0).astype(np.float32)
    y_s = np.concatenate([np.asarray(r["ys"]).reshape(SB, DEC_SEQ, D) for r in rs], axis=0).astype(np.float32)
    n_sp = np.stack([np.asarray(r["nsp"]) for r in rs], axis=1).astype(np.float32)
    n_cp = np.stack([np.asarray(r["ncp"]) for r in rs], axis=1).astype(np.float32)
    n_ss = np.concatenate([np.asarray(r["nss"]) for r in rs], axis=1).astype(np.float32)
    n_cs = np.concatenate([np.asarray(r["ncs"]) for r in rs], axis=1).astype(np.float32)
    return (y_p, y_s, n_sp, n_cp, n_ss, n_cs)
```

```python
import math
from contextlib import ExitStack

import numpy as np
import concourse.bass as bass
import concourse.mybir as mybir
from concourse.bass_utils import run_bass_kernel_spmd
from concourse.alu_op_type import AluOpType as ALU

F32 = mybir.dt.float32
BF16 = mybir.dt.bfloat16
I32 = mybir.dt.int32
U32 = mybir.dt.uint32
AF = mybir.ActivationFunctionType
AX = mybir.AxisListType

N_CORES = 8
D = 1024
SEQ = 2048
DEPTH = 2
SB = 16
DEC_SEQ = 4
PAST_LEN = 16384
RW = 512
CW = 512
H = 4
DK = 128
CONV_K = 31
INW = 3072
PH = 8
NKEYS = 128
NEXP = NKEYS * NKEYS
TOPK = 16
ALPHA = (2.0 * DEPTH) ** 0.25
LN_EPS = 1e-5
NPT = SEQ // 128
STILE = NPT
NGROUPS = 2
NUG = 6
import os
DBG_BAR = os.environ.get("DBG_BAR", "")
SEM_CAP = 30000


class Sem:
    def __init__(self, tr):
        self.tr = tr
        self.h = tr.new_hw_sem()
        self.val = 0

    def next(self, inc):
        if self.val + inc > SEM_CAP:
            self.h = self.tr.new_hw_sem()
            self.val = 0
        self.val += inc
        return (self.h, self.val)


class Tracker:
    NAMES = ["pe", "act", "dve", "pool", "sp"]

    def __init__(self, nc, stack):
        self.nc = nc
        self.stack = stack
        self.nsem = 0
        self.esem = {e: Sem(self) for e in self.NAMES}
        self.known = {e: {} for e in self.NAMES}
        self.prog = {e: [] for e in self.NAMES}
        self.last_tok = {e: None for e in self.NAMES}
        self.last_w = {}
        self.reads = {}
        self.all_dma = {}
        self.bg_dma = {}
        self.pools = {}
        self.nins = 0

    def new_hw_sem(self):
        self.nsem += 1
        return self.stack.enter_context(self.nc.semaphore("s%d" % self.nsem))

    def _wait(self, e, deps):
        kn = self.known[e]
        best = {}
        for (h, v) in deps:
            k = id(h)
            if k not in best or v > best[k][1]:
                best[k] = (h, v)
        for k, (h, v) in best.items():
            if kn.get(k, 0) >= v:
                continue
            self.prog[e].append(("wait", h, v))
            kn[k] = v

    def _deps(self, reads, writes):
        deps = []
        for r in reads:
            if r in self.last_w:
                deps.append(self.last_w[r])
        for w in writes:
            if w in self.last_w:
                deps.append(self.last_w[w])
            deps.extend(self.reads.get(w, []))
        return deps

    def _commit(self, tok, reads, writes):
        for w in writes:
            self.last_w[w] = tok
            self.reads[w] = []
        for r in reads:
            if r not in writes:
                self.reads.setdefault(r, []).append(tok)

    def op(self, e, fn, reads=(), writes=()):
        self._wait(e, self._deps(reads, writes))
        tok = self.esem[e].next(1)
        self.prog[e].append(("ins", fn, tok[0], 1))
        self._commit(tok, reads, writes)
        self.last_tok[e] = tok
        self.nins += 1
        return tok

    def dma(self, e, fn, pool, reads=(), writes=(), npool=4, bg=False):
        if pool not in self.pools:
            self.pools[pool] = [[Sem(self) for _ in range(npool)], 0]
        pl = self.pools[pool]
        sem = pl[0][pl[1] % len(pl[0])]
        pl[1] += 1
        deps = self._deps(reads, writes)
        if sem.val > 0:
            deps.append((sem.h, sem.val))
        self._wait(e, deps)
        tok = sem.next(16)
        self.prog[e].append(("ins", fn, tok[0], 16))
        self._commit(tok, reads, writes)
        if bg:
            self.bg_dma[id(tok[0])] = tok
        else:
            self.all_dma[id(tok[0])] = tok
        self.nins += 1
        return tok

    def join_bg(self):
        self.all_dma.update(self.bg_dma)
        self.bg_dma = {}

    def barrier(self):
        toks = [t for t in self.last_tok.values() if t is not None]
        toks += list(self.all_dma.values())
        for e in self.NAMES:
            self._wait(e, toks)

    def finish(self):
        self.barrier()

    def replay(self):
        nc = self.nc
        prog = self.prog

        def run(eng, lst):
            for a in lst:
                if a[0] == "wait":
                    eng.wait_ge(a[1], a[2])
                else:
                    ins = a[1](eng)
                    ins.then_inc(a[2], a[3])

        with nc.Block() as block:
            @block.tensor
            def _(eng):
                run(eng, prog["pe"])

            @block.scalar
            def _(eng):
                run(eng, prog["act"])

            @block.vector
            def _(eng):
                run(eng, prog["dve"])

            @block.gpsimd
            def _(eng):
                run(eng, prog["pool"])

            @block.sync
            def _(eng):
                run(eng, prog["sp"])


class Carver:
    def __init__(self, ov, nwords):
        self.ov = ov
        self.n = nwords
        self.off = 0

    def f32(self, n):
        assert self.off + n <= self.n, ("overlay overflow", self.off + n, self.n)
        v = self.ov[:, self.off:self.off + n]
        self.off += n
        return v

    def bf16(self, n):
        w = (n + 1) // 2
        assert self.off + w <= self.n, ("overlay overflow", self.off + w, self.n)
        v = self.ov[:, self.off:self.off + w].bitcast(BF16)
        self.off += w
        return v

    def u32(self, n):
        assert self.off + n <= self.n
        v = self.ov[:, self.off:self.off + n].bitcast(U32)
        self.off += n
        return v


def _const_tables():
    log_gamma = np.log(1.0 - 2.0 ** (-5.0 - np.arange(H, dtype=np.float32))).astype(np.float32)
    inv_freq = (np.float32(10000.0) ** (-np.arange(0, DK, 2, dtype=np.float32) / np.float32(DK))).astype(np.float32)

    def rope(pos):
        ang = pos.astype(np.float32)[:, None] * inv_freq[None, :]
        return np.cos(ang).astype(np.float32), np.sin(ang).astype(np.float32)

    ks = np.float32(DK ** -0.5)
    tab = np.zeros((NPT + 1, 128, 2, 256), np.float32)
    cp, sp_ = rope(np.arange(SEQ))
    for t in range(NPT):
        c = cp[t * 128:(t + 1) * 128]
        s = sp_[t * 128:(t + 1) * 128]
        tab[t, :, 0] = np.concatenate([c, c, -s, s], axis=1)
        tab[t, :, 1] = np.concatenate([c, c, -s, s], axis=1) * ks
    cs, ss = rope(PAST_LEN + np.arange(DEC_SEQ))
    rows = np.arange(128) % DEC_SEQ
    c = cs[rows]
    s = ss[rows]
    tab[NPT, :, 0] = np.concatenate([c, c, -s, s], axis=1)
    tab[NPT, :, 1] = np.concatenate([c, c, -s, s], axis=1) * ks

    lg = log_gamma.astype(np.float64)
    dm = np.zeros((2, 128, H, 128), np.float64)
    i = np.arange(128)
    rel = i[None, :] - i[:, None]
    for h in range(H):
        dm[0, :, h, :] = np.where(rel >= 0, np.exp(np.maximum(rel, 0) * lg[h]), 0.0)
        same = (i[None, :] // DEC_SEQ) == (i[:, None] // DEC_SEQ)
        rel4 = (i[None, :] % DEC_SEQ) - (i[:, None] % DEC_SEQ)
        dm[1, :, h, :] = np.where(same & (rel4 >= 0), np.exp(np.maximum(rel4, 0) * lg[h]), 0.0)
    din = np.zeros((128, 2, H), np.float64)
    dout = np.zeros((128, 2, H), np.float64)
    for h in range(H):
        din[:, 0, h] = np.exp((i + 1.0) * lg[h])
        dout[:, 0, h] = np.exp((128 - 1.0 - i) * lg[h])
        din[:, 1, h] = np.exp((i % DEC_SEQ + 1.0) * lg[h])
        dout[:, 1, h] = np.exp((DEC_SEQ - 1.0 - i % DEC_SEQ) * lg[h])
    cd = np.stack([np.exp(128 * lg), np.exp(DEC_SEQ * lg)])
    bmask = np.zeros((128, SB, 128), np.float32)
    for b in range(SB):
        bmask[:, b, b * DEC_SEQ:(b + 1) * DEC_SEQ] = 1.0
    rbm = np.zeros((128, SB), np.float32)
    for b in range(SB):
        rbm[b * DEC_SEQ:(b + 1) * DEC_SEQ, b] = 1.0
    iota16 = np.tile(np.arange(16, dtype=np.float32)[None, :], (128, 1))
    return dict(
        c_rope=tab,
        c_dm=np.ascontiguousarray(dm.transpose(1, 0, 2, 3)).astype(np.float32),
        c_din=din.astype(np.float32), c_dout=dout.astype(np.float32),
        c_bmask=bmask, c_rbm=rbm, c_iota=iota16,
        c_ident=np.eye(128, dtype=np.float32),
    ), cd.astype(np.float64)


def build_program():
    consts, cd = _const_tables()
    nc = bass.Bass("TRN2", target_bir_lowering=False)

    def din_(name, shape, dt=F32):
        return nc.dram_tensor(name, list(shape), dt, kind="ExternalInput").ap()

    def dout_(name, shape, dt=F32):
        return nc.dram_tensor(name, list(shape), dt, kind="ExternalOutput").ap()

    xp = din_("xp", [SEQ, D])
    xs = din_("xs", [SB * DEC_SEQ, D])
    st_in = din_("st", [DEPTH, SB, H, DK, DK])
    cc_in = din_("cc", [DEPTH, SB, CONV_K - 1, CW])
    w_in = din_("w_in", [DEPTH, D, INW])
    b_in = din_("b_in", [DEPTH, INW])
    dw_k = din_("dw_k", [DEPTH, CONV_K, CW])
    dw_b = din_("dw_b", [DEPTH, CW])
    cln_g = din_("cln_g", [DEPTH, CW])
    cln_b = din_("cln_b", [DEPTH, CW])
    w_out = din_("w_out", [DEPTH, D, D])
    ln1_g = din_("ln1_g", [DEPTH, D])
    ln1_b = din_("ln1_b", [DEPTH, D])
    w_q = din_("w_q", [DEPTH, D, 2 * D])
    sk1 = din_("sk1", [DEPTH, PH, NKEYS, 128])
    sk2 = din_("sk2", [DEPTH, PH, NKEYS, 128])
    pu = din_("pu", [DEPTH, NEXP, D])
    pv = din_("pv", [DEPTH, NEXP, D])
    ln2_g = din_("ln2_g", [DEPTH, D])
    pu_flat = pu.rearrange("l n d -> (l n) d")
    pv_flat = pv.rearrange("l n d -> (l n) d")
    ln2_b = din_("ln2_b", [DEPTH, D])
    c_rope = din_("c_rope", [NPT + 1, 128, 2, 256])
    c_dm = din_("c_dm", [128, 2, H, 128])
    c_din = din_("c_din", [128, 2, H])
    c_dout = din_("c_dout", [128, 2, H])
    c_bmask = din_("c_bmask", [128, SB, 128])
    c_rbm = din_("c_rbm", [128, SB])
    c_iota = din_("c_iota", [128, 16])
    c_ident = din_("c_ident", [128, 128])

    puv = nc.dram_tensor("puv", [DEPTH * NEXP, 2 * D], BF16, kind="Internal").ap()

    yp = dout_("yp", [SEQ, D])
    ys = dout_("ys", [SB * DEC_SEQ, D])
    nsp = dout_("nsp", [DEPTH, H, DK, DK])
    ncp = dout_("ncp", [DEPTH, CONV_K - 1, CW])
    nss = dout_("nss", [DEPTH, SB, H, DK, DK])
    ncs = dout_("ncs", [DEPTH, SB, CONV_K - 1, CW])

    per = NPT // NGROUPS
    groups = [list(range(g * per, (g + 1) * per)) for g in range(NGROUPS)]
    groups[-1] = groups[-1] + [STILE]
    MAXT = max(len(g) for g in groups)
    MAXP = per
    LMAX = MAXP * 128

    with ExitStack() as stk:
        T = Tracker(nc, stk)

        def sb(name, shape, dt=F32):
            return stk.enter_context(nc.sbuf_tensor(name, list(shape), dt))

        X = sb("X", [128, MAXT, D])
        W = sb("W", [128, 8 * INW], BF16)
        identf = sb("identf", [128, 128])
        identb = sb("identb", [128, 128], BF16)
        dmT = sb("dmT", [128, 2, H, 128])
        dinT = sb("dinT", [128, 2, H])
        doutT = sb("doutT", [128, 2, H])
        bmask = sb("bmask", [128, SB, 128], BF16)
        rbm = sb("rbm", [128, SB])
        iota16 = sb("iota16", [128, 16])
        R = sb("R", [128, DEPTH, H, 128])
        Rb = sb("Rb", [128, DEPTH, H, 128], BF16)
        UTAIL = sb("UTAIL", [128, DEPTH, 4, CONV_K - 1], BF16)
        xb = sb("xb", [128, D], BF16)
        xT = sb("xT", [128, 8, 128], BF16)
        UT = sb("UT", [128, 4, CONV_K - 1 + LMAX], BF16)
        UTS = sb("UTS", [128, 4, SB, CONV_K - 1 + DEC_SEQ], BF16)
        YT = sb("YT", [128, MAXT, CW], BF16)
        epsT = sb("epsT", [128, 1])
        OVN = 18432
        OV = sb("OV", [128, OVN])

        PS = [stk.enter_context(nc.psum_tensor("ps%d" % i, [128, 512], F32)) for i in range(8)]
        PSB = [p[:].bitcast(BF16) for p in PS]

        def dve(fn, r=(), w=()):
            return T.op("dve", fn, r, w)

        def act(fn, r=(), w=()):
            return T.op("act", fn, r, w)

        def pe(fn, r=(), w=()):
            return T.op("pe", fn, r, w)

        def pool(fn, r=(), w=()):
            return T.op("pool", fn, r, w)

        def ld(out, in_, r=(), w=(), pool_="ld"):
            return T.dma("sp", lambda e, out=out, in_=in_: e.dma_start(out=out, in_=in_), pool_, r, w)

        def stq(out, in_, r=(), w=(), pool_="st"):
            return T.dma("sp", lambda e, out=out, in_=in_: e.dma_start(out=out, in_=in_), pool_, r, w)

        def ldcast(out, in_, r=(), w=()):
            return T.dma("pool", lambda e, out=out, in_=in_: e.dma_start(out=out, in_=in_), "ldc", r, w, npool=2)

        ld(identf[:], c_ident, w=["identf"])
        ld(dmT[:], c_dm, w=["dmT"])
        ld(dinT[:], c_din, w=["dinT"])
        ld(doutT[:], c_dout, w=["doutT"])
        ldcast(bmask[:], c_bmask, w=["bmask"])
        ld(rbm[:], c_rbm, w=["rbm"])
        ld(iota16[:], c_iota, w=["iota16"])
        act(lambda e: e.activation(out=identb[:], in_=identf[:], func=AF.Copy), ["identf"], ["identb"])
        dve(lambda e: e.memset(R[:], 0.0), w=["R"])
        dve(lambda e: e.memset(Rb[:], 0.0), w=["Rb"])
        dve(lambda e: e.memset(UTAIL[:], 0.0), w=["UTAIL"])
        dve(lambda e: e.memset(epsT[:], LN_EPS), w=["epsT"])
        dve(lambda e: e.memset(YT[:], 0.0), w=["YT"])
        dve(lambda e: e.memset(X[:, MAXT - 1, :], 0.0), w=["X%d" % (MAXT - 1)])

        CV_ROWS = 1024
        puv_v = puv.rearrange("n (two d) -> n two d", two=2)
        cv_pieces = []
        for r0 in range(0, DEPTH * NEXP, CV_ROWS):
            for two, tsrc in enumerate((pu_flat, pv_flat)):
                cv_pieces.append((puv_v[r0:r0 + CV_ROWS, two, :], tsrc[r0:r0 + CV_ROWS, :]))

        def emit_cv(n):
            for _ in range(min(n, len(cv_pieces))):
                dst, src = cv_pieces.pop(0)
                T.dma("pool", lambda e, dst=dst, src=src: e.dma_start(out=dst, in_=src), "cvt", npool=4, bg=True)

        def tile_info(t):
            return (t == STILE), (1 if t == STILE else 0)

        def make_xT(slot, pref):
            xk = "X%d" % slot
            act(lambda e: e.activation(out=xb[:], in_=X[:, slot, :], func=AF.Copy), [xk], ["xb"])
            for k in range(8):
                pe(lambda e, k=k: e.transpose(out=PSB[6][:, k * 128:(k + 1) * 128], in_=xb[:, k * 128:(k + 1) * 128],
                                              identity=identb[:]), ["xb", "identb"], ["ps6"])
            dve(lambda e: e.tensor_copy(out=xT[:].rearrange("p k t -> p (k t)"), in_=PSB[6][:, :]), ["ps6"], ["xT"])

        def proj(bank, wview, col0, ncol=512):
            bk = "ps%d" % bank
            for k in range(8):
                pe(lambda e, k=k: e.matmul(PS[bank][:, 0:ncol], lhsT=xT[:, k, :], rhs=wview[:, k, col0:col0 + ncol],
                                           start=(k == 0), stop=(k == 7)), ["xT", "W"], [bk])

        def layer_norm_rows(src, srck, n, stats, mv, sd, rstd, pfx):
            nch = (n + 511) // 512
            for c in range(nch):
                dve(lambda e, c=c: e.bn_stats(out=stats[:, c, :], in_=src[:, c * 512:min(n, (c + 1) * 512)]),
                    [srck], [pfx + "stats"])
            dve(lambda e: e.bn_aggr(out=mv[:, :], in_=stats[:, 0:nch, :].rearrange("p c s -> p (c s)")),
                [pfx + "stats"], [pfx + "mv"])
            act(lambda e: e.activation(out=sd[:, :], in_=mv[:, 1:2], func=AF.Sqrt, bias=epsT[:, 0:1], scale=1.0),
                [pfx + "mv", "epsT"], [pfx + "sd"])
            dve(lambda e: e.reciprocal(out=rstd[:, :], in_=sd[:, :]), [pfx + "sd"], [pfx + "rstd"])

        def phase_a(l, tiles, first_layer):
            T.barrier()
            cv = Carver(OV, OVN)
            bcA = cv.f32(1024)
            t_cg = cv.f32(512)
            t_sig = cv.f32(512)
            t_ca = cv.f32(512)
            utile = [cv.f32(512), cv.f32(512)]
            ub = cv.bf16(512)
            cst = cv.f32(512)
            wA = W[:, 0:8 * 1024].rearrange("p (k e) -> p k e", k=8)
            ldcast(wA, w_in[l].rearrange("(k p) e -> p k e", p=128)[:, :, 2048:3072], w=["W"])
            ld(bcA, b_in[l, 2048:3072].partition_broadcast(128), w=["A_bc"])
            dve(lambda e: e.tensor_copy(out=UT[:, :, 0:CONV_K - 1], in_=UTAIL[:, l, :, :]), ["UTAIL"], ["UT"])
            ptiles = [t for t in tiles if t != STILE]
            for slot, t in enumerate(tiles):
                emit_cv(4)
                if "A" in DBG_BAR:
                    T.barrier()
                samp, _ = tile_info(t)
                xk = "X%d" % slot
                if first_layer:
                    if samp:
                        ld(X[0:SB * DEC_SEQ, slot, :], xs, w=[xk])
                    else:
                        ld(X[:, slot, :], xp[t * 128:(t + 1) * 128, :], w=[xk])
                make_xT(slot, "A")
                proj(0, wA, 0)
                proj(1, wA, 512)
                ut = utile[slot % 2]
                uk = "A_ut%d" % (slot % 2)
                dve(lambda e: e.tensor_tensor(out=t_cg, in0=PS[1][:, :], in1=bcA[:, 512:1024], op=ALU.add),
                    ["ps1", "A_bc"], ["A_cg"])
                act(lambda e: e.activation(out=t_sig, in_=t_cg, func=AF.Sigmoid), ["A_cg"], ["A_sig"])
                dve(lambda e: e.tensor_tensor(out=t_ca, in0=PS[0][:, :], in1=bcA[:, 0:512], op=ALU.add),
                    ["ps0", "A_bc"], ["A_ca"])
                dve(lambda e, ut=ut: e.tensor_tensor(out=ut, in0=t_ca, in1=t_sig, op=ALU.mult),
                    ["A_ca", "A_sig"], [uk])
                act(lambda e, ut=ut: e.activation(out=ub, in_=ut, func=AF.Copy), [uk], ["A_ub"])
                for c4 in range(4):
                    pe(lambda e, c4=c4: e.transpose(out=PSB[6][:, c4 * 128:(c4 + 1) * 128],
                                                    in_=ub[:, c4 * 128:(c4 + 1) * 128], identity=identb[:]),
                       ["A_ub", "identb"], ["ps6"])
                if not samp:
                    off = CONV_K - 1 + slot * 128
                    dve(lambda e, off=off: e.tensor_copy(
                        out=UT[:, :, off:off + 128],
                        in_=PSB[6][:, 0:512].rearrange("p (c t) -> p c t", c=4)), ["ps6"], ["UT"])
                    if t == NPT - 1:
                        stq(ncp[l], ut[128 - (CONV_K - 1):128, :], r=[uk])
                else:
                    for c4 in range(4):
                        dve(lambda e, c4=c4: e.tensor_copy(
                            out=UTS[:, c4, :, CONV_K - 1:CONV_K - 1 + DEC_SEQ],
                            in_=PSB[6][:, c4 * 128:c4 * 128 + SB * DEC_SEQ].rearrange("p (b r) -> p b r", r=DEC_SEQ)),
                            ["ps6"], ["UTS"])
                    for b in range(SB):
                        stq(ncs[l, b, CONV_K - 1 - DEC_SEQ:CONV_K - 1, :], ut[b * DEC_SEQ:(b + 1) * DEC_SEQ, :], r=[uk])
                    stq(ncs[l, :, 0:CONV_K - 1 - DEC_SEQ, :], cc_in[l, :, DEC_SEQ:CONV_K - 1, :])
                    nb = 4
                    rows = nb * (CONV_K - 1)
                    for g4 in range(SB // nb):
                        cstv = cst[0:rows, 0:512]
                        ld(cstv, cc_in[l, g4 * nb:(g4 + 1) * nb].rearrange("b r c -> (b r) c"), w=["A_cst"])
                        for c4 in range(4):
                            pe(lambda e, c4=c4: e.transpose(out=PS[7][:, c4 * 128:c4 * 128 + rows],
                                                            in_=cst[0:rows, c4 * 128:(c4 + 1) * 128],
                                                            identity=identf[0:rows, 0:rows]),
                               ["A_cst", "identf"], ["ps7"])
                        for c4 in range(4):
                            dve(lambda e, c4=c4, g4=g4: e.tensor_copy(
                                out=UTS[:, c4, g4 * nb:(g4 + 1) * nb, 0:CONV_K - 1],
                                in_=PS[7][:, c4 * 128:c4 * 128 + rows].rearrange("p (b r) -> p b r", r=CONV_K - 1)),
                                ["ps7"], ["UTS"])

        def phase_b(l, tiles):
            T.barrier()
            ldcast(W[:, 0:8 * 2048].rearrange("p (k e) -> p k e", k=8),
                   w_in[l].rearrange("(k p) e -> p k e", p=128)[:, :, 0:2048], w=["W"])
            ldcast(W[:, 8 * 2048:8 * 3072].rearrange("p (k e) -> p k e", k=8),
                   w_out[l].rearrange("(k p) e -> p k e", p=128), w=["W"])
            cv = Carver(OV, OVN)
            YCH = cv.f32(LMAX)
            YS = cv.f32(4 * SB * DEC_SEQ).rearrange("p (c b r) -> p c b r", c=4, b=SB)
            dwst = cv.f32(512)
            dwT = cv.f32(4 * 32).rearrange("p (c k) -> p c k", c=4)
            dwb = cv.f32(512)
            ptiles = [t for t in tiles if t != STILE]
            L = len(ptiles) * 128
            ld(dwst[0:CONV_K, :], dw_k[l], w=["B_dwst"])
            ld(dwb, dw_b[l].partition_broadcast(128), w=["B_dwb"])
            for c4 in range(4):
                pe(lambda e, c4=c4: e.transpose(out=PS[7][:, c4 * 32:c4 * 32 + CONV_K],
                                                in_=dwst[0:CONV_K, c4 * 128:(c4 + 1) * 128],
                                                identity=identf[0:CONV_K, 0:CONV_K]),
                   ["B_dwst", "identf"], ["ps7"])
            dve(lambda e: e.tensor_copy(out=dwT[:, :, 0:CONV_K],
                                        in_=PS[7][:, 0:128].rearrange("p (c k) -> p c k", c=4)[:, :, 0:CONV_K]),
                ["ps7"], ["B_dwT"])
            for c4 in range(4):
                for k in range(CONV_K):
                    if k == 0:
                        dve(lambda e, c4=c4: e.tensor_scalar(out=YCH[:, 0:L], in0=UT[:, c4, 0:L],
                                                             scalar1=dwT[:, c4, 0:1], scalar2=None, op0=ALU.mult),
                            ["UT", "B_dwT"], ["B_ych"])
                    else:
                        dve(lambda e, c4=c4, k=k: e.scalar_tensor_tensor(
                            out=YCH[:, 0:L], in0=UT[:, c4, k:k + L], scalar=dwT[:, c4, k:k + 1],
                            in1=YCH[:, 0:L], op0=ALU.mult, op1=ALU.add), ["UT", "B_dwT", "B_ych"], ["B_ych"])
                ntl = len(ptiles)
                for t0 in range(0, ntl, 4):
                    n4 = min(4, ntl - t0)
                    for i in range(n4):
                        pe(lambda e, i=i, t0=t0: e.transpose(out=PS[7][:, i * 128:(i + 1) * 128],
                                                             in_=YCH[:, (t0 + i) * 128:(t0 + i + 1) * 128],
                                                             identity=identf[:]), ["B_ych", "identf"], ["ps7"])
                    dve(lambda e, c4=c4, t0=t0, n4=n4: e.tensor_tensor(
                        out=YT[:, t0:t0 + n4, c4 * 128:(c4 + 1) * 128],
                        in0=PS[7][:, 0:n4 * 128].rearrange("p (t c) -> p t c", c=128),
                        in1=dwb[:, c4 * 128:(c4 + 1) * 128].unsqueeze(1).to_broadcast([128, n4, 128]),
                        op=ALU.add), ["ps7", "B_dwb"], ["YT"])
            dve(lambda e: e.tensor_copy(out=UTAIL[:, l, :, :], in_=UT[:, :, L:L + CONV_K - 1]), ["UT"], ["UTAIL"])
            if STILE in tiles:
                slot = tiles.index(STILE)
                for c4 in range(4):
                    for k in range(CONV_K):
                        if k == 0:
                            dve(lambda e, c4=c4: e.tensor_scalar(out=YS[:, c4, :, :], in0=UTS[:, c4, :, 0:DEC_SEQ],
                                                                 scalar1=dwT[:, c4, 0:1], scalar2=None, op0=ALU.mult),
                                ["UTS", "B_dwT"], ["B_ys"])
                        else:
                            dve(lambda e, c4=c4, k=k: e.scalar_tensor_tensor(
                                out=YS[:, c4, :, :], in0=UTS[:, c4, :, k:k + DEC_SEQ], scalar=dwT[:, c4, k:k + 1],
                                in1=YS[:, c4, :, :], op0=ALU.mult, op1=ALU.add), ["UTS", "B_dwT", "B_ys"], ["B_ys"])
                nst = SB * DEC_SEQ
                for c4 in range(4):
                    pe(lambda e, c4=c4: e.transpose(out=PS[7][0:nst, c4 * 128:(c4 + 1) * 128],
                                                    in_=YS[:, c4, :, :].rearrange("p b r -> p (b r)"),
                                                    identity=identf[:]), ["B_ys", "identf"], ["ps7"])
                dve(lambda e: e.tensor_tensor(out=YT[0:nst, slot, :], in0=PS[7][0:nst, :], in1=dwb[0:nst, :],
                                              op=ALU.add), ["ps7", "B_dwb"], ["YT"])

        def phase_c1(l, tiles):
            T.barrier()
            cv = Carver(OV, OVN)
            bc = cv.f32(2048)
            g1 = cv.f32(1024)
            b1 = cv.f32(1024)
            gc = cv.f32(512)
            bcn = cv.f32(512)
            rp = cv.f32(512).rearrange("p (a f) -> p a f", a=2)
            qf = cv.f32(512)
            kf = cv.f32(512)
            m1 = cv.f32(512)
            m2 = cv.f32(512)
            of = cv.f32(512)
            gs = cv.f32(512)
            y1 = cv.f32(1024)
            stats = cv.f32(24).rearrange("p (c s) -> p c s", s=6)
            mv = cv.f32(8).rearrange("p (h s) -> p h s", s=2)
            sd = cv.f32(4)
            rstd = cv.f32(4)
            qr = cv.bf16(512)
            kr = cv.bf16(512)
            qTt = cv.bf16(512).rearrange("p (h t) -> p h t", h=4)
            kTt = cv.bf16(512).rearrange("p (h t) -> p h t", h=4)
            vb = cv.bf16(512).rearrange("p (h e) -> p h e", h=4)
            vdec = cv.bf16(512).rearrange("p (h e) -> p h e", h=4)
            STt = cv.bf16(512).rearrange("p (h t) -> p h t", h=4)
            cat = cv.bf16(1024)
            catT = cv.bf16(1024).rearrange("p (k t) -> p k t", k=8)
            R0f = cv.f32(SB * 128).rearrange("p (b e) -> p b e", b=SB)
            R0b = cv.bf16(SB * 128).rearrange("p (b e) -> p b e", b=SB)
            qTx = cv.bf16(SB * 128).rearrange("p (b t) -> p b t", b=SB)
            rbig = cv.bf16(SB * 128).rearrange("p (b e) -> p b e", b=SB)

            wq4 = W[:, 0:8 * 2048].rearrange("p (k e) -> p k e", k=8)
            wo = W[:, 8 * 2048:8 * 3072].rearrange("p (k e) -> p k e", k=8)
            ld(bc, b_in[l, 0:2048].partition_broadcast(128), w=["C_bc"])
            ld(g1, ln1_g[l].partition_broadcast(128), w=["C_g1"])
            ld(b1, ln1_b[l].partition_broadcast(128), w=["C_b1"])
            ld(gc, cln_g[l].partition_broadcast(128), w=["C_gc"])
            ld(bcn, cln_b[l].partition_broadcast(128), w=["C_bcn"])

            for slot, t in enumerate(tiles):
                emit_cv(4)
                if "C" in DBG_BAR:
                    T.barrier()
                samp, gi = tile_info(t)
                xk = "X%d" % slot
                ld(rp[:].rearrange("p a f -> p (a f)"), c_rope[t].rearrange("p a f -> p (a f)"), w=["C_rp"])
                make_xT(slot, "C")

                def rope_pair(bank, boff, dstf, a, dst_bf, dk):
                    bk = "ps%d" % bank
                    dve(lambda e: e.tensor_tensor(out=dstf, in0=PS[bank][:, :], in1=bc[:, boff:boff + 512],
                                                  op=ALU.add), [bk, "C_bc"], [dk])
                    x4 = dstf.rearrange("p (h f) -> p h f", h=4)
                    m14 = m1.rearrange("p (h f) -> p h f", h=4)
                    m24 = m2.rearrange("p (h f) -> p h f", h=4)
                    dve(lambda e: e.tensor_tensor(out=m14, in0=x4,
                                                  in1=rp[:, a, 0:128].unsqueeze(1).to_broadcast([128, 4, 128]),
                                                  op=ALU.mult), [dk, "C_rp"], ["C_m1"])
                    dve(lambda e: e.tensor_tensor(out=m24[:, :, 0:64], in0=x4[:, :, 64:128],
                                                  in1=rp[:, a, 128:192].unsqueeze(1).to_broadcast([128, 4, 64]),
                                                  op=ALU.mult), [dk, "C_rp"], ["C_m2"])
                    dve(lambda e: e.tensor_tensor(out=m24[:, :, 64:128], in0=x4[:, :, 0:64],
                                                  in1=rp[:, a, 192:256].unsqueeze(1).to_broadcast([128, 4, 64]),
                                                  op=ALU.mult), [dk, "C_rp", "C_m2"], ["C_m2"])
                    dve(lambda e: e.tensor_tensor(out=dst_bf, in0=m1, in1=m2, op=ALU.add), ["C_m1", "C_m2"],
                        [dk + "r"])

                proj(0, wq4, 0)
                proj(1, wq4, 512)
                rope_pair(0, 0, qf, 0, qr, "C_qf")
                rope_pair(1, 512, kf, 1, kr, "C_kf")
                proj(0, wq4, 1024)
                proj(1, wq4, 1536)
                dve(lambda e: e.tensor_tensor(out=qf, in0=PS[0][:, :], in1=bc[:, 1024:1536], op=ALU.add),
                    ["ps0", "C_bc"], ["C_qf"])
                act(lambda e: e.activation(out=vb[:].rearrange("p h e -> p (h e)"), in_=qf, func=AF.Copy),
                    ["C_qf"], ["C_vb"])
                dve(lambda e, gi=gi: e.tensor_tensor(out=vdec[:], in0=qf.rearrange("p (h e) -> p h e", h=4),
                                                     in1=doutT[:, gi, :].unsqueeze(2).to_broadcast([128, 4, 128]),
                                                     op=ALU.mult), ["C_qf", "doutT"], ["C_vdec"])
                dve(lambda e: e.tensor_tensor(out=kf, in0=PS[1][:, :], in1=bc[:, 1536:2048], op=ALU.add),
                    ["ps1", "C_bc"], ["C_kf"])
                act(lambda e: e.activation(out=gs, in_=kf, func=AF.Silu), ["C_kf"], ["C_gs"])
                for h in range(4):
                    pe(lambda e, h=h: e.transpose(out=PSB[6][:, h * 128:(h + 1) * 128], in_=qr[:, h * 128:(h + 1) * 128],
                                                  identity=identb[:]), ["C_qfr", "identb"], ["ps6"])
                for h in range(4):
                    pe(lambda e, h=h: e.transpose(out=PSB[6][:, 512 + h * 128:512 + (h + 1) * 128],
                                                  in_=kr[:, h * 128:(h + 1) * 128], identity=identb[:]),
                       ["C_kfr", "identb"], ["ps6"])
                act(lambda e: e.activation(out=qTt[:].rearrange("p h t -> p (h t)"), in_=PSB[6][:, 0:512], func=AF.Copy),
                    ["ps6"], ["C_qT"])
                act(lambda e: e.activation(out=kTt[:].rearrange("p h t -> p (h t)"), in_=PSB[6][:, 512:1024],
                                           func=AF.Copy), ["ps6"], ["C_kT"])
                for h in range(4):
                    pe(lambda e, h=h: e.matmul(PS[2][:, h * 128:(h + 1) * 128], lhsT=kTt[:, h, :], rhs=qTt[:, h, :],
                                               start=True, stop=True), ["C_kT", "C_qT"], ["ps2"])
                dve(lambda e, gi=gi: e.tensor_tensor(out=STt[:].rearrange("p h t -> p (h t)"), in0=PS[2][:, :],
                                                     in1=dmT[:, gi, :, :].rearrange("p h t -> p (h t)"), op=ALU.mult),
                    ["ps2", "dmT"], ["C_ST"])
                for h in range(4):
                    pe(lambda e, h=h: e.matmul(PS[3][:, h * 128:(h + 1) * 128], lhsT=STt[:, h, :], rhs=vb[:, h, :],
                                               start=True, stop=True), ["C_ST", "C_vb"], ["ps3"])
                if not samp:
                    for h in range(4):
                        pe(lambda e, h=h: e.matmul(PS[4][:, h * 128:(h + 1) * 128], lhsT=qTt[:, h, :],
                                                   rhs=Rb[:, l, h, :], start=True, stop=True), ["C_qT", "Rb"], ["ps4"])
                    for h in range(4):
                        pe(lambda e, h=h: e.matmul(PS[5][:, h * 128:(h + 1) * 128], lhsT=kr[:, h * 128:(h + 1) * 128],
                                                   rhs=vdec[:, h, :], start=True, stop=True), ["C_kfr", "C_vdec"],
                           ["ps5"])
                    for h in range(4):
                        dve(lambda e, h=h: e.scalar_tensor_tensor(out=R[:, l, h, :], in0=R[:, l, h, :],
                                                                  scalar=float(cd[0, h]),
                                                                  in1=PS[5][:, h * 128:(h + 1) * 128],
                                                                  op0=ALU.mult, op1=ALU.add), ["R", "ps5"], ["R"])
                    act(lambda e: e.activation(out=Rb[:, l, :, :], in_=R[:, l, :, :], func=AF.Copy), ["R"], ["Rb"])
                    if t == NPT - 1:
                        stq(nsp[l].rearrange("h d e -> d h e"), R[:, l, :, :], r=["R"])
                else:
                    for h in range(4):
                        ld(R0f[:], st_in[l, :, h].rearrange("b d e -> d b e"), w=["C_R0f"])
                        act(lambda e: e.activation(out=R0b[:], in_=R0f[:], func=AF.Copy), ["C_R0f"], ["C_R0b"])
                        dve(lambda e, h=h: e.tensor_tensor(out=qTx[:],
                                                           in0=qTt[:, h, :].unsqueeze(1).to_broadcast([128, SB, 128]),
                                                           in1=bmask[:], op=ALU.mult), ["C_qT", "bmask"], ["C_qTx"])
                        for b in range(SB):
                            pe(lambda e, h=h, b=b: e.matmul(PS[4][:, h * 128:(h + 1) * 128], lhsT=qTx[:, b, :],
                                                            rhs=R0b[:, b, :], start=(b == 0), stop=(b == SB - 1)),
                               ["C_qTx", "C_R0b"], ["ps4"])
                        dve(lambda e, h=h: e.tensor_tensor(out=rbig[:],
                                                           in0=vdec[:, h, :].unsqueeze(1).to_broadcast([128, SB, 128]),
                                                           in1=rbm[:, :].unsqueeze(2).to_broadcast([128, SB, 128]),
                                                           op=ALU.mult), ["C_vdec", "rbm"], ["C_rbig"])
                        for g4 in range(SB // 4):
                            pe(lambda e, h=h, g4=g4: e.matmul(
                                PS[5][:, :], lhsT=kr[:, h * 128:(h + 1) * 128],
                                rhs=rbig[:, g4 * 4:(g4 + 1) * 4, :].rearrange("p b e -> p (b e)"),
                                start=True, stop=True), ["C_kfr", "C_rbig"], ["ps5"])
                            dve(lambda e, h=h, g4=g4: e.scalar_tensor_tensor(
                                out=R0f[:, g4 * 4:(g4 + 1) * 4, :].rearrange("p b e -> p (b e)"),
                                in0=R0f[:, g4 * 4:(g4 + 1) * 4, :].rearrange("p b e -> p (b e)"),
                                scalar=float(cd[1, h]), in1=PS[5][:, :], op0=ALU.mult, op1=ALU.add),
                                ["C_R0f", "ps5"], ["C_R0f"])
                        stq(nss[l, :, h].rearrange("b d e -> d b e"), R0f[:], r=["C_R0f"])
                dve(lambda e, gi=gi: e.tensor_tensor(out=m1.rearrange("p (h e) -> p h e", h=4),
                                                     in0=PS[4][:, :].rearrange("p (h e) -> p h e", h=4),
                                                     in1=dinT[:, gi, :].unsqueeze(2).to_broadcast([128, 4, 128]),
                                                     op=ALU.mult), ["ps4", "dinT"], ["C_m1"])
                dve(lambda e: e.tensor_tensor(out=of, in0=PS[3][:, :], in1=m1, op=ALU.add), ["ps3", "C_m1"], ["C_of"])
                for h in range(4):
                    dve(lambda e, h=h: e.bn_stats(out=stats[:, h, :], in_=of[:, h * 128:(h + 1) * 128]),
                        ["C_of"], ["C_stats"])
                for h in range(4):
                    dve(lambda e, h=h: e.bn_aggr(out=mv[:, h, :], in_=stats[:, h, :]), ["C_stats"], ["C_mv"])
                act(lambda e: e.activation(out=sd[:, :], in_=mv[:, :, 1], func=AF.Sqrt, bias=epsT[:, 0:1], scale=1.0),
                    ["C_mv", "epsT"], ["C_sd"])
                dve(lambda e: e.reciprocal(out=rstd[:, :], in_=sd[:, :]), ["C_sd"], ["C_rstd"])
                of4 = of.rearrange("p (h e) -> p h e", h=4)
                dve(lambda e: e.tensor_tensor(out=of4, in0=of4, in1=mv[:, :, 0:1].to_broadcast([128, 4, 128]),
                                              op=ALU.subtract), ["C_of", "C_mv"], ["C_of"])
                dve(lambda e: e.tensor_tensor(out=of4, in0=of4, in1=rstd[:, :].unsqueeze(2).to_broadcast([128, 4, 128]),
                                              op=ALU.mult), ["C_of", "C_rstd"], ["C_of"])
                dve(lambda e: e.tensor_tensor(out=cat[:, 0:512], in0=of, in1=gs, op=ALU.mult), ["C_of", "C_gs"],
                    ["C_cat"])
                dve(lambda e, slot=slot: e.bn_stats(out=stats[:, 0, :], in_=YT[:, slot, :]), ["YT"], ["C_stats"])
                dve(lambda e: e.bn_aggr(out=mv[:, 0, :], in_=stats[:, 0, :]), ["C_stats"], ["C_mv"])
                act(lambda e: e.activation(out=sd[:, 0:1], in_=mv[:, 0, 1:2], func=AF.Sqrt, bias=epsT[:, 0:1], scale=1.0),
                    ["C_mv", "epsT"], ["C_sd"])
                dve(lambda e: e.reciprocal(out=rstd[:, 0:1], in_=sd[:, 0:1]), ["C_sd"], ["C_rstd"])
                dve(lambda e, slot=slot: e.tensor_scalar(out=m2, in0=YT[:, slot, :], scalar1=mv[:, 0, 0:1],
                                                         scalar2=rstd[:, 0:1], op0=ALU.subtract, op1=ALU.mult),
                    ["YT", "C_mv", "C_rstd"], ["C_m2"])
                dve(lambda e: e.tensor_tensor(out=m2, in0=m2, in1=gc, op=ALU.mult), ["C_m2", "C_gc"], ["C_m2"])
                dve(lambda e: e.tensor_tensor(out=m2, in0=m2, in1=bcn, op=ALU.add), ["C_m2", "C_bcn"], ["C_m2"])
                act(lambda e: e.activation(out=cat[:, 512:1024], in_=m2, func=AF.Silu), ["C_m2"], ["C_cat"])
                for k in range(8):
                    pe(lambda e, k=k: e.transpose(out=PSB[6][:, k * 128:(k + 1) * 128], in_=cat[:, k * 128:(k + 1) * 128],
                                                  identity=identb[:]), ["C_cat", "identb"], ["ps6"])
                act(lambda e: e.activation(out=catT[:].rearrange("p k t -> p (k t)"), in_=PSB[6][:, :], func=AF.Copy),
                    ["ps6"], ["C_catT"])
                for half in range(2):
                    bk = "ps%d" % half
                    for k in range(8):
                        pe(lambda e, k=k, half=half: e.matmul(PS[half][:, :], lhsT=catT[:, k, :],
                                                              rhs=wo[:, k, half * 512:(half + 1) * 512],
                                                              start=(k == 0), stop=(k == 7)), ["C_catT", "W"], [bk])
                    dve(lambda e, half=half, slot=slot: e.scalar_tensor_tensor(
                        out=y1[:, half * 512:(half + 1) * 512], in0=X[:, slot, half * 512:(half + 1) * 512],
                        scalar=float(ALPHA), in1=PS[half][:, :], op0=ALU.mult, op1=ALU.add), [xk, bk], ["C_y1"])
                layer_norm_rows(y1, "C_y1", D, stats, mv[:, 0, :], sd[:, 0:1], rstd[:, 0:1], "C_")
                dve(lambda e: e.tensor_scalar(out=y1, in0=y1, scalar1=mv[:, 0, 0:1], scalar2=rstd[:, 0:1],
                                              op0=ALU.subtract, op1=ALU.mult), ["C_y1", "C_mv", "C_rstd"], ["C_y1"])
                dve(lambda e: e.tensor_tensor(out=y1, in0=y1, in1=g1, op=ALU.mult), ["C_y1", "C_g1"], ["C_y1"])
                dve(lambda e, slot=slot: e.tensor_tensor(out=X[:, slot, :], in0=y1, in1=b1, op=ALU.add),
                    ["C_y1", "C_b1"], [xk])

        def phase_c2(l, tiles, last_layer):
            emit_cv(len(cv_pieces))
            T.join_bg()
            T.barrier()
            cv = Carver(OV, OVN)
            A8 = cv.f32(2048)
            B8 = cv.f32(2048)
            C8 = cv.f32(2048)
            UTflat = UT[:].rearrange("p c n -> p (c n)")
            UTSflat = UTS[:].rearrange("p c b n -> p (c b n)")
            YTflat = YT[:].rearrange("p t c -> p (t c)")
            UG = [cv.bf16(2048) for _ in range(4)] + [UTflat[:, 0:2048], UTflat[:, 2048:4096], UTSflat[:, 0:2048]]
            UG += [W[:, 8 * 2048 + k * 2048:8 * 2048 + (k + 1) * 2048] for k in range(4)]
            NG = len(UG)
            y2 = YTflat[:, 0:2048].bitcast(F32)
            junk = YTflat[:, 2048:3072]
            xbb = [xb[:, :], YTflat[:, 3072:4096]]
            xbk = ["xb", "P_xb2"]
            gact = cv.f32(128)
            g2 = cv.f32(1024)
            b2 = cv.f32(1024)
            vals = cv.f32(256).rearrange("p (h s k) -> p h s k", h=PH, s=2)
            idxu = cv.u32(256).rearrange("p (h s k) -> p h s k", h=PH, s=2)
            idxf = cv.f32(256).rearrange("p (h s k) -> p h s k", h=PH, s=2)
            sv = cv.f32(128).rearrange("p (h k) -> p h k", h=PH)
            ciu = cv.u32(128).rearrange("p (h k) -> p h k", h=PH)
            a16u = cv.u32(128).rearrange("p (h k) -> p h k", h=PH)
            b16u = cv.u32(128).rearrange("p (h k) -> p h k", h=PH)
            a16f = cv.f32(128).rearrange("p (h k) -> p h k", h=PH)
            b16f = cv.f32(128).rearrange("p (h k) -> p h k", h=PH)
            i1s = cv.f32(128).rearrange("p (h k) -> p h k", h=PH)
            i2s = cv.f32(128).rearrange("p (h k) -> p h k", h=PH)
            eidf = cv.f32(128)
            eidu2 = [cv.u32(128), cv.u32(128)]
            ex = cv.f32(128).rearrange("p (h k) -> p h k", h=PH)
            zs = cv.f32(8)
            rz = cv.f32(8)
            gate2 = [cv.f32(128).rearrange("p (h k) -> p h k", h=PH), cv.f32(128).rearrange("p (h k) -> p h k", h=PH)]
            actv = cv.f32(128)
            coef = cv.f32(128)
            dg = [cv.bf16(128) for _ in range(4)]
            stats = cv.f32(12).rearrange("p (c s) -> p c s", s=6)
            mv = cv.f32(2)
            sd = cv.f32(1)
            rstd = cv.f32(1)
            qT16 = cv.bf16(2048).rearrange("p (c t) -> p c t", c=16)
            keysT = cv.bf16(2048).rearrange("p (c t) -> p c t", c=16)
            kst = cv.bf16(1024)

            wqv = W[:, 0:8 * 2048].rearrange("p (k e) -> p k e", k=8)
            ldcast(wqv, w_q[l].rearrange("(k p) e -> p k e", p=128), w=["W"])
            ld(g2, ln2_g[l].partition_broadcast(128), w=["P_g2"])
            ld(b2, ln2_b[l].partition_broadcast(128), w=["P_b2"])
            for side, skd in enumerate((sk1, sk2)):
                ld(A8[:, 0:PH * 128].rearrange("p (h d) -> p h d", h=PH), skd[l].rearrange("h k d -> k h d"), w=["P_A8"])
                act(lambda e: e.activation(out=kst, in_=A8[:, 0:PH * 128], func=AF.Copy), ["P_A8"], ["P_kst"])
                for h in range(PH):
                    pe(lambda e, h=h: e.transpose(out=PSB[6][:, h * 128:(h + 1) * 128], in_=kst[:, h * 128:(h + 1) * 128],
                                                  identity=identb[:]), ["P_kst", "identb"], ["ps6"])
                for h in range(PH):
                    dve(lambda e, h=h, side=side: e.tensor_copy(out=keysT[:, 2 * h + side, :],
                                                                in_=PSB[6][:, h * 128:(h + 1) * 128]),
                        ["ps6"], ["P_keysT"])

            QBANK = [0, 3, 0, 3]
            SBANK = [4, 5, 7, 4]

            def sel_ops(slot, pb):
                ops = []

                def D_(fn, r=(), w=()):
                    ops.append(lambda: T.op("dve", fn, r, w))

                def A_(fn, r=(), w=()):
                    ops.append(lambda: T.op("act", fn, r, w))

                def P_(fn, r=(), w=()):
                    ops.append(lambda: T.op("pe", fn, r, w))

                xk = "X%d" % slot
                xbt = xbb[pb]
                xbtk = xbk[pb]
                eidu = eidu2[pb]
                eiduk = "P_eidu%d" % pb
                gate = gate2[pb]
                gatek = "P_gate%d" % pb
                A_(lambda e: e.activation(out=xbt, in_=X[:, slot, :], func=AF.Copy), [xk], [xbtk])
                for k in range(8):
                    P_(lambda e, k=k: e.transpose(out=PSB[6][:, k * 128:(k + 1) * 128], in_=xbt[:, k * 128:(k + 1) * 128],
                                                  identity=identb[:]), [xbtk, "identb"], ["ps6"])
                D_(lambda e: e.tensor_copy(out=xT[:].rearrange("p k t -> p (k t)"), in_=PSB[6][:, :]), ["ps6"], ["xT"])
                for q4 in range(4):
                    bank = QBANK[q4]
                    bk = "ps%d" % bank
                    for c in range(q4 * 4, q4 * 4 + 4):
                        for k in range(8):
                            P_(lambda e, c=c, k=k, bank=bank: e.matmul(
                                PS[bank][:, (c % 4) * 128:(c % 4 + 1) * 128], lhsT=wqv[:, k, c * 128:(c + 1) * 128],
                                rhs=xT[:, k, :], start=(k == 0), stop=(k == 7)), ["xT", "W"], [bk])
                    A_(lambda e, q4=q4, bank=bank: e.activation(
                        out=qT16[:, q4 * 4:(q4 + 1) * 4, :].rearrange("p c t -> p (c t)"), in_=PS[bank][:, :],
                        func=AF.Copy), [bk], ["P_qT16"])
                Sv = A8.rearrange("p (c k) -> p c k", c=16)
                S2v = B8.rearrange("p (c k) -> p c k", c=16)
                for q4 in range(4):
                    bank = SBANK[q4]
                    bk = "ps%d" % bank
                    for c in range(q4 * 4, q4 * 4 + 4):
                        P_(lambda e, c=c, bank=bank: e.matmul(PS[bank][:, (c % 4) * 128:(c % 4 + 1) * 128],
                                                              lhsT=qT16[:, c, :], rhs=keysT[:, c, :], start=True,
                                                              stop=True), ["P_qT16", "P_keysT"], [bk])
                    A_(lambda e, q4=q4, bank=bank: e.activation(out=A8[:, q4 * 512:(q4 + 1) * 512], in_=PS[bank][:, :],
                                                                func=AF.Copy), [bk], ["P_A8"])
                for c in range(16):
                    h, s_ = c // 2, c % 2
                    D_(lambda e, c=c, h=h, s_=s_: e.max(out=vals[:, h, s_, 0:8], in_=Sv[:, c, :]), ["P_A8"], ["P_vals"])
                    D_(lambda e, c=c, h=h, s_=s_: e.max_index(out=idxu[:, h, s_, 0:8], in_max=vals[:, h, s_, 0:8],
                                                              in_values=Sv[:, c, :]), ["P_A8", "P_vals"], ["P_idxu"])
                    D_(lambda e, c=c, h=h, s_=s_: e.match_replace(out=S2v[:, c, :], in_to_replace=vals[:, h, s_, 0:8],
                                                                  in_values=Sv[:, c, :], imm_value=-1e30),
                       ["P_A8", "P_vals"], ["P_B8"])
                    D_(lambda e, c=c, h=h, s_=s_: e.max(out=vals[:, h, s_, 8:16], in_=S2v[:, c, :]), ["P_B8"], ["P_vals"])
                    D_(lambda e, c=c, h=h, s_=s_: e.max_index(out=idxu[:, h, s_, 8:16], in_max=vals[:, h, s_, 8:16],
                                                              in_values=S2v[:, c, :]), ["P_B8", "P_vals"], ["P_idxu"])
                D_(lambda e: e.tensor_copy(out=idxf[:], in_=idxu[:]), ["P_idxu"], ["P_idxf"])
                cand = A8.rearrange("p (h a b) -> p h a b", h=PH, a=16)
                D_(lambda e: e.tensor_tensor(out=cand, in0=vals[:, :, 0, :].unsqueeze(3).to_broadcast([128, PH, 16, 16]),
                                             in1=vals[:, :, 1, :].unsqueeze(2).to_broadcast([128, PH, 16, 16]),
                                             op=ALU.add), ["P_vals"], ["P_A8"])
                cf = A8.rearrange("p (h n) -> p h n", h=PH)
                cf2 = B8.rearrange("p (h n) -> p h n", h=PH)
                for h in range(PH):
                    D_(lambda e, h=h: e.max(out=sv[:, h, 0:8], in_=cf[:, h, :]), ["P_A8"], ["P_sv"])
                    D_(lambda e, h=h: e.max_index(out=ciu[:, h, 0:8], in_max=sv[:, h, 0:8], in_values=cf[:, h, :]),
                       ["P_A8", "P_sv"], ["P_ciu"])
                    D_(lambda e, h=h: e.match_replace(out=cf2[:, h, :], in_to_replace=sv[:, h, 0:8],
                                                      in_values=cf[:, h, :], imm_value=-1e30), ["P_A8", "P_sv"], ["P_B8"])
                    D_(lambda e, h=h: e.max(out=sv[:, h, 8:16], in_=cf2[:, h, :]), ["P_B8"], ["P_sv"])
                    D_(lambda e, h=h: e.max_index(out=ciu[:, h, 8:16], in_max=sv[:, h, 8:16], in_values=cf2[:, h, :]),
                       ["P_B8", "P_sv"], ["P_ciu"])
                D_(lambda e: e.tensor_single_scalar(out=a16u[:], in_=ciu[:], scalar=4, op=ALU.logical_shift_right),
                   ["P_ciu"], ["P_a16u"])
                D_(lambda e: e.tensor_single_scalar(out=b16u[:], in_=ciu[:], scalar=15, op=ALU.bitwise_and),
                   ["P_ciu"], ["P_b16u"])
                D_(lambda e: e.tensor_copy(out=a16f[:], in_=a16u[:]), ["P_a16u"], ["P_a16f"])
                D_(lambda e: e.tensor_copy(out=b16f[:], in_=b16u[:]), ["P_b16u"], ["P_b16f"])
                eq = C8.rearrange("p (h k a) -> p h k a", h=PH, k=16)
                io_b = iota16[:, :].unsqueeze(1).unsqueeze(1).to_broadcast([128, PH, 16, 16])
                for (pf, sidx, dst, nm) in ((a16f, 0, i1s, "P_i1s"), (b16f, 1, i2s, "P_i2s")):
                    D_(lambda e, pf=pf: e.tensor_tensor(out=eq, in0=io_b,
                                                        in1=pf[:].unsqueeze(3).to_broadcast([128, PH, 16, 16]),
                                                        op=ALU.is_equal), ["iota16", "P_a16f", "P_b16f"], ["P_C8"])
                    D_(lambda e, sidx=sidx: e.tensor_tensor(
                        out=eq, in0=eq, in1=idxf[:, :, sidx, :].unsqueeze(2).to_broadcast([128, PH, 16, 16]),
                        op=ALU.mult), ["P_C8", "P_idxf"], ["P_C8"])
                    D_(lambda e, dst=dst: e.tensor_reduce(out=dst[:], in_=eq, axis=AX.X, op=ALU.add), ["P_C8"], [nm])
                D_(lambda e: e.scalar_tensor_tensor(out=eidf, in0=i1s[:].rearrange("p h k -> p (h k)"),
                                                    scalar=float(NKEYS), in1=i2s[:].rearrange("p h k -> p (h k)"),
                                                    op0=ALU.mult, op1=ALU.add), ["P_i1s", "P_i2s"], ["P_eidf"])
                if l > 0:
                    D_(lambda e: e.tensor_scalar(out=eidf, in0=eidf, scalar1=float(l * NEXP), scalar2=None,
                                                 op0=ALU.add), ["P_eidf"], ["P_eidf"])
                D_(lambda e: e.tensor_copy(out=eidu, in_=eidf), ["P_eidf"], [eiduk])
                D_(lambda e: e.tensor_tensor(out=ex[:], in0=sv[:], in1=sv[:, :, 0:1].to_broadcast([128, PH, 16]),
                                             op=ALU.subtract), ["P_sv"], ["P_ex"])
                A_(lambda e: e.activation(out=ex[:], in_=ex[:], func=AF.Exp), ["P_ex"], ["P_ex"])
                D_(lambda e: e.tensor_reduce(out=zs, in_=ex[:], axis=AX.X, op=ALU.add), ["P_ex"], ["P_zs"])
                D_(lambda e: e.reciprocal(out=rz, in_=zs), ["P_zs"], ["P_rz"])
                D_(lambda e: e.tensor_tensor(out=gate[:], in0=ex[:], in1=rz.unsqueeze(2).to_broadcast([128, PH, 16]),
                                             op=ALU.mult), ["P_ex", "P_rz"], [gatek])
                return ops

            def run_ops(ops, n):
                for _ in range(min(n, len(ops))):
                    ops.pop(0)()

            NSLOT = PH * TOPK
            cur = sel_ops(0, 0)
            run_ops(cur, len(cur))
            for i, t in enumerate(tiles):
                slot = i
                pb = i % 2
                samp, gi = tile_info(t)
                xk = "X%d" % slot
                xbt = xbb[pb]
                xbtk = xbk[pb]
                eidu = eidu2[pb]
                eiduk = "P_eidu%d" % pb
                gflat = gate2[pb][:].rearrange("p h k -> p (h k)")
                gatek = "P_gate%d" % pb
                nxt = sel_ops(i + 1, (i + 1) % 2) if i + 1 < len(tiles) else []
                per = (len(nxt) + NSLOT - 1) // NSLOT if nxt else 0
                for j in range(NSLOT):
                    ug = UG[j % NG]
                    ugk = "P_ug%d" % (j % NG)
                    d = dg[j % 4]
                    dk = "P_dg%d" % (j % 4)
                    ak = "P_ac%d" % j
                    gk = "P_gc%d" % j
                    ck = "P_cc%d" % j
                    T.dma("pool", lambda e, ug=ug, j=j, eidu=eidu: e.indirect_dma_start(
                        out=ug, out_offset=None, in_=puv,
                        in_offset=bass.IndirectOffsetOnAxis(ap=eidu[:, j:j + 1], axis=0)), "gat",
                        reads=[eiduk], writes=[ugk], npool=NG)
                    dve(lambda e, ug=ug, j=j, xbt=xbt: e.scalar_tensor_tensor(
                        out=ug[:, 0:D], in0=ug[:, 0:D], scalar=1.0, in1=xbt, op0=ALU.mult, op1=ALU.mult,
                        accum_out=actv[:, j:j + 1]), [ugk, xbtk], [ugk, ak])
                    act(lambda e, j=j: e.activation(out=gact[:, j:j + 1], in_=actv[:, j:j + 1], func=AF.Gelu),
                        [ak], [gk])
                    act(lambda e, j=j, gflat=gflat: e.activation(out=coef[:, j:j + 1], in_=gact[:, j:j + 1],
                                                                 func=AF.Copy, scale=gflat[:, j:j + 1]),
                        [gk, gatek], [ck])
                    act(lambda e, d=d, j=j: e.activation(out=d, in_=identb[:], func=AF.Copy, scale=coef[:, j:j + 1]),
                        ["identb", ck], [dk])
                    for half in range(2):
                        pe(lambda e, d=d, ug=ug, j=j, half=half: e.matmul(
                            PS[1 + half][:, :], lhsT=d, rhs=ug[:, D + half * 512:D + (half + 1) * 512],
                            start=(j == 0), stop=(j == NSLOT - 1)), [dk, ugk], ["ps%d" % (1 + half)])
                    if nxt and j >= 2:
                        run_ops(nxt, per)
                for half in range(2):
                    dve(lambda e, half=half, slot=slot: e.scalar_tensor_tensor(
                        out=y2[:, half * 512:(half + 1) * 512], in0=X[:, slot, half * 512:(half + 1) * 512],
                        scalar=float(ALPHA), in1=PS[1 + half][:, :], op0=ALU.mult, op1=ALU.add),
                        [xk, "ps%d" % (1 + half)], ["P_y2"])
                layer_norm_rows(y2, "P_y2", D, stats, mv, sd, rstd, "P_")
                dve(lambda e: e.tensor_scalar(out=y2, in0=y2, scalar1=mv[:, 0:1], scalar2=rstd[:, 0:1],
                                              op0=ALU.subtract, op1=ALU.mult), ["P_y2", "P_mv", "P_rstd"], ["P_y2"])
                dve(lambda e: e.tensor_tensor(out=y2, in0=y2, in1=g2, op=ALU.mult), ["P_y2", "P_g2"], ["P_y2"])
                dve(lambda e, slot=slot: e.tensor_tensor(out=X[:, slot, :], in0=y2, in1=b2, op=ALU.add),
                    ["P_y2", "P_b2"], [xk])
                if last_layer:
                    if samp:
                        stq(ys, X[0:SB * DEC_SEQ, slot, :], r=[xk])
                    else:
                        stq(yp[t * 128:(t + 1) * 128, :], X[:, slot, :], r=[xk])
                if nxt:
                    run_ops(nxt, len(nxt))

        for tiles in groups:
            for l in range(DEPTH):
                phase_a(l, tiles, l == 0)
                phase_b(l, tiles)
                phase_c1(l, tiles)
                phase_c2(l, tiles, l == DEPTH - 1)
        T.finish()
        T.replay()
        stats_ = dict(nins=T.nins, nsem=T.nsem, nprog={e: len(T.prog[e]) for e in T.NAMES})
    return nc, consts, stats_


_CACHE = {}


def kernel(**inputs):
    if "prog" not in _CACHE:
        _CACHE["prog"] = build_program()
    nc, consts, _ = _CACHE["prog"]
    f = lambda a: np.ascontiguousarray(np.asarray(a, dtype=np.float32))
    x_prompt = f(inputs["x_prompt"])
    x_sample = f(inputs["x_sample"])
    st = f(inputs["state_retention"])
    cc = f(inputs["cache_conv"])
    shared = {
        "w_in": f(inputs["w_in"]), "b_in": f(inputs["b_in"]), "dw_k": f(inputs["dw_kernel"]),
        "dw_b": f(inputs["dw_bias"]), "cln_g": f(inputs["conv_ln_g"]), "cln_b": f(inputs["conv_ln_b"]),
        "w_out": f(inputs["w_out"]), "ln1_g": f(inputs["ln1_g"]), "ln1_b": f(inputs["ln1_b"]),
        "w_q": f(inputs["w_query"]), "sk1": f(inputs["sub_keys_1"]), "sk2": f(inputs["sub_keys_2"]),
        "pu": f(inputs["peer_u"]), "pv": f(inputs["peer_v"]), "ln2_g": f(inputs["ln2_g"]),
        "ln2_b": f(inputs["ln2_b"]),
    }
    shared.update(consts)
    in_maps = []
    for c in range(N_CORES):
        m = dict(shared)
        m["xp"] = x_prompt[c]
        m["xs"] = np.ascontiguousarray(x_sample[c * SB:(c + 1) * SB].reshape(SB * DEC_SEQ, D))
        m["st"] = np.ascontiguousarray(st[:, c * SB:(c + 1) * SB])
        m["cc"] = np.ascontiguousarray(cc[:, c * SB:(c + 1) * SB])
        in_maps.append(m)
    res = run_bass_kernel_spmd(nc, in_maps, core_ids=list(range(N_CORES)))
    rs = res.results
    y_p = np.stack([np.asarray(r["yp"]) for r in rs], axis=0).astype(np.float32)
    y_s = np.concatenate([np.asarray(r["ys"]).reshape(SB, DEC_SEQ, D) for r in rs], axis=0).astype(np.float32)
    n_sp = np.stack([np.asarray(r["nsp"]) for r in rs], axis=1).astype(np.float32)
    n_cp = np.stack([np.asarray(r["ncp"]) for r in rs], axis=1).astype(np.float32)
    n_ss = np.concatenate([np.asarray(r["nss"]) for r in rs], axis=1).astype(np.float32)
    n_cs = np.concatenate([np.asarray(r["ncs"]) for r in rs], axis=1).astype(np.float32)
    return (y_p, y_s, n_sp, n_cp, n_ss, n_cs)
```
